# Optimizing a Trainium2 kernel written in Bass

```python
import math
import jax
import jax.numpy as jnp
from jax import lax
import numpy as np

D_MODEL = 1024
BATCH = 16
SEQ = 2048
DEPTH = 2

N_MIXERS = 2
HEAD_DIM = 64
N_HEADS = D_MODEL // HEAD_DIM
ATTN_WIDTH = N_HEADS * HEAD_DIM
ROPE_THETA = 10000.0
NORM_EPS = 1e-6

NSA_KV_HEADS = 2
NSA_GROUP = N_HEADS // NSA_KV_HEADS
NSA_KV_WIDTH = NSA_KV_HEADS * HEAD_DIM
NSA_BRANCHES = 3
CMP_LEN = 32
CMP_STRIDE = 16
CMP_HIDDEN = 256
SLC_LEN = 64
SLC_TOP = 16
WIN_LEN = 512
NSA_IN = ATTN_WIDTH + 6 * NSA_KV_WIDTH + ATTN_WIDTH + NSA_BRANCHES * N_HEADS
SLC_Q_BLOCK = 32
WIN_Q_BLOCK = 128

DIL_PATTERNS = ((128, 1), (512, 4), (2048, 16))
DIL_GROUP_HEADS = (N_HEADS - 2 * (N_HEADS // 3), N_HEADS // 3, N_HEADS // 3)
DIL_IN = 4 * ATTN_WIDTH
DIL_Q_BLOCK = 32

kernel_name = "hybrid_nsa_dilated_adaln_trunk"


def rms_norm(x, g):
    xf = x.astype(jnp.float32)
    y = xf * lax.rsqrt(jnp.mean(xf * xf, axis=-1, keepdims=True) + NORM_EPS)
    return (y * g.astype(jnp.float32)).astype(x.dtype)


def rope(x):
    T = x.shape[1]
    half = x.shape[-1] // 2
    inv = ROPE_THETA ** (-jnp.arange(half, dtype=jnp.float32) / half)
    ang = jnp.arange(T, dtype=jnp.float32)[:, None] * inv[None, :]
    cos = jnp.cos(ang)[None, :, None, :]
    sin = jnp.sin(ang)[None, :, None, :]
    xf = x.astype(jnp.float32)
    x1, x2 = xf[..., :half], xf[..., half:]
    return jnp.concatenate([x1 * cos - x2 * sin, x2 * cos + x1 * sin], axis=-1).astype(x.dtype)


def masked_softmax(s, mask):
    s = jnp.where(mask, s.astype(jnp.float32), -jnp.inf)
    m = jnp.max(s, axis=-1, keepdims=True)
    m = jnp.where(jnp.isfinite(m), m, 0.0)
    e = jnp.exp(s - m)
    den = jnp.sum(e, axis=-1, keepdims=True)
    return e / jnp.where(den > 0, den, 1.0)


def adaln_norm(x, c, g, w_ada, b_ada):
    mod = jax.nn.silu(c) @ w_ada + b_ada
    shift, scale, gate = jnp.split(mod, 3, axis=-1)
    h = rms_norm(x, g) * (1.0 + scale[:, None, :]) + shift[:, None, :]
    return h, gate[:, None, :]


def nsa_mixer(h, w_in, pe_k, pe_v, ck_w1, ck_w2, cv_w1, cv_w2, w_out):
    B, T, _ = h.shape
    G, HG, DH = NSA_KV_HEADS, NSA_GROUP, HEAD_DIM
    scale = DH ** -0.5
    u = h @ w_in
    cuts = np.cumsum([ATTN_WIDTH] + [NSA_KV_WIDTH] * 6 + [ATTN_WIDTH]).tolist()
    q, k_c, v_c, k_s, v_s, k_w, v_w, z, g_logit = jnp.split(u, cuts, axis=-1)
    k_c, v_c, k_s, v_s, k_w, v_w = [a.reshape(B, T, G, DH) for a in (k_c, v_c, k_s, v_s, k_w, v_w)]
    gates = jax.nn.sigmoid(g_logit.astype(jnp.float32)).reshape(B, T, G, HG, NSA_BRANCHES).astype(h.dtype)
    t = jnp.arange(T)

    q_plain = q.reshape(B, T, G, HG, DH)
    n_cmp = (T - CMP_LEN) // CMP_STRIDE + 1
    cmp_idx = CMP_STRIDE * np.arange(n_cmp)[:, None] + np.arange(CMP_LEN)[None, :]

    def compress(a, pe, w1, w2):
        blk = a[:, cmp_idx] + pe[:, None, :]
        blk = jnp.moveaxis(blk, 3, 2).reshape(B, n_cmp, G, CMP_LEN * DH)
        return jax.nn.silu(blk @ w1) @ w2

    k_cmp = compress(k_c, pe_k, ck_w1, ck_w2)
    v_cmp = compress(v_c, pe_v, cv_w1, cv_w2)
    cmp_end = CMP_STRIDE * jnp.arange(n_cmp) + CMP_LEN - 1
    cmask = cmp_end[None, :] <= t[:, None]
    s_cmp = jnp.einsum('btghd,bngd->btghn', q_plain, k_cmp) * scale
    p_cmp = masked_softmax(s_cmp, cmask[None, :, None, None, :])
    o_cmp = jnp.einsum('btghn,bngd->btghd', p_cmp.astype(v_cmp.dtype), v_cmp)

    n_slc = T // SLC_LEN
    c_start = CMP_STRIDE * np.arange(n_cmp)
    s_start = SLC_LEN * np.arange(n_slc)
    overlap = ((c_start[:, None] < s_start[None, :] + SLC_LEN)
               & (c_start[:, None] + CMP_LEN > s_start[None, :])).astype(np.float32)
    p_slc = jnp.einsum('btghn,nj->btgj', p_cmp, overlap)
    jblk = jnp.arange(n_slc)[None, :]
    cur = (t // SLC_LEN)[:, None]
    visible = jblk <= cur
    forced = (jblk == 0) | (jblk == cur) | (jblk == cur - 1)
    rank = jnp.where(forced[None, :, None, :], jnp.inf,
                     jnp.where(visible[None, :, None, :], p_slc, -jnp.inf))
    n_top = min(SLC_TOP, n_slc)
    top_val, top_idx = lax.top_k(rank, n_top)
    top_ok = top_val > -jnp.inf

    q_r = rope(q.reshape(B, T, N_HEADS, DH)).reshape(B, T, G, HG, DH)
    k_s = rope(k_s)
    k_w = rope(k_w)

    k_blk = k_s.reshape(B, n_slc, SLC_LEN, G, DH).transpose(0, 3, 1, 2, 4)
    v_blk = v_s.reshape(B, n_slc, SLC_LEN, G, DH).transpose(0, 3, 1, 2, 4)
    C = SLC_Q_BLOCK
    n_qb = T // C
    q_sb = jnp.moveaxis(q_r.reshape(B, n_qb, C, G, HG, DH), 1, 0)
    idx_sb = jnp.moveaxis(top_idx.reshape(B, n_qb, C, G, n_top), 1, 0)
    ok_sb = jnp.moveaxis(top_ok.reshape(B, n_qb, C, G, n_top), 1, 0)
    gather_blocks = jax.vmap(jax.vmap(lambda blk, ix: blk[ix]))

    def slc_block(args):
        s0, qq, ii, ok = args
        ii_g = jnp.swapaxes(ii, 1, 2)
        kg = gather_blocks(k_blk, ii_g)
        vg = gather_blocks(v_blk, ii_g)
        s = jnp.einsum('bcghd,bgcnld->bcghnl', qq, kg) * scale
        tq = s0 + jnp.arange(C)
        kpos = ii[..., None] * SLC_LEN + jnp.arange(SLC_LEN)
        mask = (kpos <= tq[None, :, None, None, None]) & ok[..., None]
        p = masked_softmax(s.reshape(B, C, G, HG, n_top * SLC_LEN),
                           mask.reshape(B, C, G, 1, n_top * SLC_LEN))
        return jnp.einsum('bcghnl,bgcnld->bcghd', p.reshape(s.shape).astype(vg.dtype), vg)

    o_slc = lax.map(slc_block, (jnp.arange(n_qb) * C, q_sb, idx_sb, ok_sb))
    o_slc = jnp.moveaxis(o_slc, 0, 1).reshape(B, T, G, HG, DH)

    span = WIN_LEN + WIN_Q_BLOCK
    n_wb = T // WIN_Q_BLOCK
    k_pad = jnp.pad(k_w, ((0, 0), (WIN_LEN, 0), (0, 0), (0, 0)))
    v_pad = jnp.pad(v_w, ((0, 0), (WIN_LEN, 0), (0, 0), (0, 0)))
    q_wb = jnp.moveaxis(q_r.reshape(B, n_wb, WIN_Q_BLOCK, G, HG, DH), 1, 0)

    def win_block(args):
        s0, qq = args
        kk = lax.dynamic_slice_in_dim(k_pad, s0, span, axis=1)
        vv = lax.dynamic_slice_in_dim(v_pad, s0, span, axis=1)
        s = jnp.einsum('bcghd,bkgd->bcghk', qq, kk) * scale
        tq = s0 + jnp.arange(WIN_Q_BLOCK)
        kpos = s0 - WIN_LEN + jnp.arange(span)
        mask = ((kpos[None, :] <= tq[:, None]) & (kpos[None, :] > tq[:, None] - WIN_LEN)
                & (kpos[None, :] >= 0))
        p = masked_softmax(s, mask[None, :, None, None, :])
        return jnp.einsum('bcghk,bkgd->bcghd', p.astype(vv.dtype), vv)

    o_win = lax.map(win_block, (jnp.arange(n_wb) * WIN_Q_BLOCK, q_wb))
    o_win = jnp.moveaxis(o_win, 0, 1).reshape(B, T, G, HG, DH)

    o = gates[..., 0:1] * o_cmp + gates[..., 1:2] * o_slc + gates[..., 2:3] * o_win
    o = o.reshape(B, T, ATTN_WIDTH)
    return (o * jax.nn.silu(z)) @ w_out


def dilated_mixer(h, w_in, w_out):
    B, T, _ = h.shape
    DH = HEAD_DIM
    scale = DH ** -0.5
    n_grp = len(DIL_PATTERNS)
    q, k, v, z = jnp.split(h @ w_in, 4, axis=-1)
    q = rope(q.reshape(B, T, N_HEADS, DH))
    k = rope(k.reshape(B, T, N_HEADS, DH))
    v = v.reshape(B, T, N_HEADS, DH)
    offs = np.cumsum((0,) + DIL_GROUP_HEADS).tolist()
    k_grp = [k[:, :, offs[i]:offs[i + 1]] for i in range(n_grp)]
    v_grp = [v[:, :, offs[i]:offs[i + 1]] for i in range(n_grp)]
    C = DIL_Q_BLOCK
    n_qb = T // C
    q_blocks = jnp.moveaxis(q.reshape(B, n_qb, C, N_HEADS, DH), 1, 0)

    def block(args):
        s0, qb = args
        tq = s0 + jnp.arange(C)
        outs, lses = [], []
        for gi, (win, dil) in enumerate(DIL_PATTERNS):
            n_k = win // dil + 1
            kpos = tq[:, None] - dil * jnp.arange(n_k)[None, :]
            ok = kpos >= 0
            kidx = jnp.maximum(kpos, 0)
            kg = k_grp[gi][:, kidx]
            vg = v_grp[gi][:, kidx]
            qg = qb[:, :, offs[gi]:offs[gi + 1]]
            s = jnp.einsum('bchd,bckhd->bchk', qg, kg).astype(jnp.float32) * scale
            s = jnp.where(ok[None, :, None, :], s, -jnp.inf)
            m = jnp.max(s, axis=-1, keepdims=True)
            e = jnp.exp(s - m)
            den = jnp.sum(e, axis=-1, keepdims=True)
            outs.append(jnp.einsum('bchk,bckhd->bchd', (e / den).astype(vg.dtype), vg))
            lse = (m + jnp.log(den))[..., 0]
            lses.append(jax.nn.logsumexp(lse, axis=-1) - math.log(DIL_GROUP_HEADS[gi]))
        alpha = jax.nn.softmax(jnp.stack(lses, axis=-1), axis=-1) * n_grp
        return jnp.concatenate(
            [outs[i] * alpha[:, :, i, None, None].astype(outs[i].dtype) for i in range(n_grp)], axis=2)

    o = lax.map(block, (jnp.arange(n_qb) * C, q_blocks))
    o = jnp.moveaxis(o, 0, 1).reshape(B, T, ATTN_WIDTH)
    return (o * jax.nn.silu(z)) @ w_out


def setup_inputs(seed: int = 0) -> dict:
    key = jax.random.key(seed)
    ks = jax.random.split(key, 16)
    n_a = len(range(0, DEPTH, N_MIXERS))
    n_b = len(range(1, DEPTH, N_MIXERS))
    D = D_MODEL

    def nrm(k, shape, s):
        return jax.random.normal(k, shape, jnp.float32) * s

    return {
        "x": nrm(ks[0], (BATCH, SEQ, D), 1.0),
        "c": nrm(ks[1], (BATCH, D), 1.0),
        "norm_g": 1.0 + nrm(ks[2], (DEPTH, D), 0.05),
        "ada_w": nrm(ks[3], (DEPTH, D, 3 * D), 0.5 * D ** -0.5),
        "ada_b": nrm(ks[4], (DEPTH, 3 * D), 0.02),
        "nsa_w_in": nrm(ks[5], (n_a, D, NSA_IN), D ** -0.5),
        "nsa_pe_k": nrm(ks[6], (n_a, CMP_LEN, HEAD_DIM), 0.1),
        "nsa_pe_v": nrm(ks[7], (n_a, CMP_LEN, HEAD_DIM), 0.1),
        "nsa_ck_w1": nrm(ks[8], (n_a, CMP_LEN * HEAD_DIM, CMP_HIDDEN), (CMP_LEN * HEAD_DIM) ** -0.5),
        "nsa_ck_w2": nrm(ks[9], (n_a, CMP_HIDDEN, HEAD_DIM), CMP_HIDDEN ** -0.5),
        "nsa_cv_w1": nrm(ks[10], (n_a, CMP_LEN * HEAD_DIM, CMP_HIDDEN), (CMP_LEN * HEAD_DIM) ** -0.5),
        "nsa_cv_w2": nrm(ks[11], (n_a, CMP_HIDDEN, HEAD_DIM), CMP_HIDDEN ** -0.5),
        "nsa_w_out": nrm(ks[12], (n_a, ATTN_WIDTH, D), ATTN_WIDTH ** -0.5),
        "dil_w_in": nrm(ks[13], (n_b, D, DIL_IN), D ** -0.5),
        "dil_w_out": nrm(ks[14], (n_b, ATTN_WIDTH, D), ATTN_WIDTH ** -0.5),
        "final_g": 1.0 + nrm(ks[15], (D,), 0.05),
    }


def reference(x, c, norm_g, ada_w, ada_b, nsa_w_in, nsa_pe_k, nsa_pe_v, nsa_ck_w1, nsa_ck_w2,
              nsa_cv_w1, nsa_cv_w2, nsa_w_out, dil_w_in, dil_w_out, final_g):
    for i in range(DEPTH):
        h, gate = adaln_norm(x, c, norm_g[i], ada_w[i], ada_b[i])
        j = i // N_MIXERS
        if i % N_MIXERS == 0:
            y = nsa_mixer(h, nsa_w_in[j], nsa_pe_k[j], nsa_pe_v[j], nsa_ck_w1[j], nsa_ck_w2[j],
                          nsa_cv_w1[j], nsa_cv_w2[j], nsa_w_out[j])
        else:
            y = dilated_mixer(h, dil_w_in[j], dil_w_out[j])
        x = x + gate * y
    return rms_norm(x, final_g)
```

```python
import numpy as np
import ml_dtypes
from contextlib import ExitStack
import concourse.bass as bass
import concourse.mybir as mybir
from concourse.bass_utils import run_bass_kernel_spmd

F32 = mybir.dt.float32
BF16 = mybir.dt.bfloat16
AF = mybir.ActivationFunctionType
ALU = mybir.AluOpType
AX = mybir.AxisListType
NPBF = ml_dtypes.bfloat16

T = 2048
D = 1024
NT = 16
NEG = -30000.0
EPS = 1e-6
NCORES = 8
DIL_G = [(0, 6), (6, 11), (11, 16)]
DIL_W = [128, 512, 2048]


class Res:
    __slots__ = ("name", "w", "rs")

    def __init__(self, name=""):
        self.name = name
        self.w = None
        self.rs = {}


class Sched:
    GEN = 30000

    def __init__(self, nc, es, n_dma=12):
        self.nc = nc
        self.es = es
        self.eng = {"pe": nc.tensor, "act": nc.scalar, "dve": nc.vector, "pool": nc.gpsimd, "sp": nc.sync}
        self.sem = {}
        self.cnt = {}
        self.cur = {}
        self.ngen = {}
        for e in ("pe", "act", "dve", "pool"):
            self.ngen[e] = 0
            self._new_sem(e)
        self.dkeys = []
        self.qkeys = {}
        self.rr = {}
        for q in ("sp", "pool"):
            self.qkeys[q] = []
            self.rr[q] = 0
            for i in range(n_dma):
                k = "dma_%s%d" % (q, i)
                self.sem[k] = es.enter_context(nc.semaphore(k))
                self.cnt[k] = 0
                self.dkeys.append(k)
                self.qkeys[q].append(k)
        self.seen = {e: {} for e in self.eng}
        self.nins = 0
        self.stopped = False

    def _new_sem(self, e):
        k = "%s#%d" % (e, self.ngen[e])
        self.ngen[e] += 1
        self.sem[k] = self.es.enter_context(self.nc.semaphore(k.replace("#", "_")))
        self.cnt[k] = 0
        self.cur[e] = k
        return k

    def _deps(self, reads, writes):
        deps = []
        for r in reads:
            if r.w is not None:
                deps.append(r.w)
        for r in writes:
            if r.w is not None:
                deps.append(r.w)
            deps.extend(r.rs.items())
        return deps

    def _waits(self, e, deps):
        need = {}
        for (k, v) in deps:
            if need.get(k, 0) < v:
                need[k] = v
        eo = self.eng[e]
        for k, v in need.items():
            if e == "pe" and k.startswith("pe#"):
                continue
            if self.seen[e].get(k, 0) >= v:
                continue
            eo.wait_ge(self.sem[k], v)
            self.seen[e][k] = v

    def _mark(self, tok, reads, writes):
        k, v = tok
        for r in reads:
            if r.rs.get(k, 0) < v:
                r.rs[k] = v
        for r in writes:
            r.w = tok
            r.rs = {}

    def op(self, e, fn, reads=(), writes=()):
        if self.stopped:
            return
        self._waits(e, self._deps(reads, writes))
        k = self.cur[e]
        if self.cnt[k] >= self.GEN:
            k = self._new_sem(e)
        ins = fn(self.eng[e])
        self.cnt[k] += 1
        ins.then_inc(self.sem[k], 1)
        self.nins += 1
        self._mark((k, self.cnt[k]), reads, writes)

    def dma(self, q, out, in_, reads=(), writes=()):
        if self.stopped:
            return
        deps = self._deps(reads, writes)
        k = self.qkeys[q][self.rr[q]]
        self.rr[q] = (self.rr[q] + 1) % len(self.qkeys[q])
        if self.cnt[k] > 0:
            deps.append((k, self.cnt[k]))
        self._waits(q, deps)
        ins = self.eng[q].dma_start(out=out, in_=in_)
        self.cnt[k] += 16
        ins.then_inc(self.sem[k], 16)
        self.nins += 1
        self._mark((k, self.cnt[k]), reads, writes)

    def barrier(self):
        if self.stopped:
            return
        toks = [(self.cur[e], self.cnt[self.cur[e]]) for e in ("pe", "act", "dve", "pool")]
        toks += [(k, self.cnt[k]) for k in self.dkeys if self.cnt[k] > 0]
        toks = [t for t in toks if t[1] > 0]
        for e in ("pe", "act", "dve", "pool", "sp"):
            need = {}
            for (k, v) in toks:
                need[k] = v
            for k, v in need.items():
                if self.seen[e].get(k, 0) >= v:
                    continue
                self.eng[e].wait_ge(self.sem[k], v)
                self.seen[e][k] = v

    def finish(self):
        sp = self.eng["sp"]
        for k in self.dkeys:
            if self.cnt[k] > 0 and self.seen["sp"].get(k, 0) < self.cnt[k]:
                sp.wait_ge(self.sem[k], self.cnt[k])
        for e in ("pe", "act", "dve", "pool"):
            k = self.cur[e]
            if self.cnt[k] > 0:
                sp.wait_ge(self.sem[k], self.cnt[k])


def make_consts():
    c = {}
    p = np.arange(128)
    c["c_ident"] = np.eye(128, dtype=np.float32).astype(NPBF)
    perm = np.zeros((128, 128), np.float32)
    for m in range(128):
        perm[(m // 64) * 64 + ((m % 64) + 32) % 64, m] = 1.0
    c["c_perm"] = perm.astype(NPBF)
    inv = (np.float32(10000.0) ** (-np.arange(32, dtype=np.float32) / np.float32(32))).astype(np.float32)
    t = np.arange(T, dtype=np.float32)
    ang = (t[None, :] * inv[p % 32][:, None]).astype(np.float32)
    c["c_cos"] = np.cos(ang).astype(np.float32)
    sgn = np.where((p % 64) < 32, -1.0, 1.0).astype(np.float32)
    c["c_sin"] = (np.sin(ang).astype(np.float32) * sgn[:, None]).astype(np.float32)
    k = p[:, None]
    q = p[None, :]
    c["c_causal"] = np.where(k <= q, 0.0, NEG).astype(NPBF)
    c["c_anti"] = np.where(k > q, 0.0, NEG).astype(NPBF)
    tt = np.arange(T)[None, :]
    cb = np.where((16 * k + 31 <= tt) & (k < 127), 0.0, NEG)
    c["c_cmpbias"] = cb.astype(NPBF)
    E = np.zeros((128, 16, 128), np.float32)
    for kt in range(16):
        for kk in range(128):
            E[2 * kt + kk // 64, kt, kk] = 1.0
    c["c_esel"] = E.astype(NPBF)
    vc = np.zeros((128, 33), np.float32)
    vc[:127, 0] = 1.0
    for n in range(127):
        for j in range(32):
            if 4 * j - 1 <= n <= 4 * j + 3:
                vc[n, 1 + j] = 1.0
    c["c_vcmp"] = vc.astype(NPBF)
    cand = np.zeros((128, 8, 32), np.float32)
    forced = np.zeros((128, 8, 32), np.float32)
    for ti in range(8, 16):
        for pp in range(128):
            cur = (128 * ti + pp) // 64
            for j in range(32):
                if 1 <= j <= cur - 2:
                    cand[pp, ti - 8, j] = 1.0
                if j == 0 or j == cur or j == cur - 1:
                    forced[pp, ti - 8, j] = 1.0
    c["c_cand"] = cand
    c["c_forced"] = forced
    d = q - k
    m0 = np.zeros((128, 3, 128), np.float32)
    m0[:, 0, :] = np.where((d >= 1) & (d <= 8), 0.0, NEG)
    m0[:, 1, :] = np.where((d >= 0) & (d <= 8), 0.0, NEG)
    m0[:, 2, :] = np.where((d >= 0) & (d <= 7), 0.0, NEG)
    c["c_m0"] = m0.astype(NPBF)
    m1 = np.zeros((128, 3, 128), np.float32)
    m1[:, 0, :] = np.where((d >= 1) & (d <= 32), 0.0, NEG)
    m1[:, 1, :] = np.where((d >= 0) & (d <= 32), 0.0, NEG)
    m1[:, 2, :] = np.where((d >= 0) & (d <= 31), 0.0, NEG)
    c["c_m1"] = m1.astype(NPBF)
    return c


CONST_SPECS = [
    ("c_ident", [128, 128], BF16), ("c_perm", [128, 128], BF16), ("c_cos", [128, T], F32),
    ("c_sin", [128, T], F32), ("c_causal", [128, 128], BF16), ("c_anti", [128, 128], BF16),
    ("c_cmpbias", [128, T], BF16), ("c_esel", [128, 16, 128], BF16), ("c_vcmp", [128, 33], BF16),
    ("c_cand", [128, 8, 32], F32), ("c_forced", [128, 8, 32], F32), ("c_m0", [128, 3, 128], BF16),
    ("c_m1", [128, 3, 128], BF16),
]


class _Stop(Exception):
    pass


def build_program(nseq=2, layers=(0, 1), stop=None):
    nc = bass.Bass("TRN2", target_bir_lowering=False)
    es = ExitStack()

    def din(name, shape, dt=F32):
        return nc.dram_tensor(name, list(shape), dt, kind="ExternalInput").ap()

    x_d = din("x", [nseq, T, D])
    ct_d = din("ct", [128, 8, nseq])
    normg_d = din("norm_g", [2, D])
    adaw_d = din("ada_w", [2, D, 3 * D])
    adab_d = din("ada_b", [2, 3 * D])
    win0_d = din("nsa_w_in", [D, 2864])
    pe2_d = [din("pe2k", [128, 16]), din("pe2v", [128, 16])]
    ckw1_d = din("nsa_ck_w1", [2048, 256])
    ckw2_d = din("nsa_ck_w2", [256, 64])
    cvw1_d = din("nsa_cv_w1", [2048, 256])
    cvw2_d = din("nsa_cv_w2", [256, 64])
    wout0_d = din("nsa_w_out", [D, D])
    win1_d = din("dil_w_in", [D, 4 * D])
    wout1_d = din("dil_w_out", [D, D])
    fing_d = din("final_g", [1, D])
    cdram = {nm: din(nm, shp, dt) for (nm, shp, dt) in CONST_SPECS}
    out_d = nc.dram_tensor("out", [nseq, T, D], F32, kind="ExternalOutput").ap()
    x1_d = nc.dram_tensor("x1s", [nseq, T, D], F32, kind="Internal").ap()
    mod_d = nc.dram_tensor("mods", [2, nseq, 3 * D], F32, kind="Internal").ap()

    S = Sched(nc, es)
    R_x1 = [Res("x1d%d" % s) for s in range(nseq)]
    R_mod = Res("modd")
    R_out = Res("outd")

    uid = [0]

    def sb(name, shape, dt, stack=None):
        uid[0] += 1
        return (stack or es).enter_context(nc.sbuf_tensor("s%d_%s" % (uid[0], name), list(shape), dt))

    def V(e, fn, reads=(), writes=()):
        S.op(e, fn, reads, writes)

    def mark(name):
        if stop is not None and name == stop:
            S.stopped = True

    R_const = Res("const")
    SPEC = {nm: (shp, dt) for (nm, shp, dt) in CONST_SPECS}

    def load_const(nm, stack):
        shp, dt = SPEC[nm]
        t_ = sb(nm, shp, dt, stack)
        S.dma("sp", t_[:], cdram[nm], writes=[R_const])
        return t_

    ident = load_const("c_ident", es)
    perm = load_const("c_perm", es)
    cos_t = load_const("c_cos", es)
    sin_t = load_const("c_sin", es)
    causal = load_const("c_causal", es)

    banks = []
    R_bank = []
    for i in range(8):
        banks.append(es.enter_context(nc.psum_tensor("bank%d" % i, [128, 512], F32)))
        R_bank.append(Res("bank%d" % i))
    P2SB = [0, 1, 2, 3]
    MB = [6, 7]
    SB = [0, 1, 2]
    OB = [3, 4]
    AB = [5, 6]
    CB = 7
    state = {"srot": 0, "prot": 0}

    NX = 3
    xt = [sb("xt%d" % i, [128, D], F32) for i in range(NX)]
    R_xt = [Res("xt%d" % i) for i in range(NX)]
    tmpf = sb("tmpf", [128, D], F32)
    R_tmpf = Res("tmpf")
    tmpf2 = sb("tmpf2", [128, D], F32)
    R_tmpf2 = Res("tmpf2")
    hb = sb("hb", [128, D], BF16)
    R_hb = Res("hb")
    hT = sb("hT", [128, 8, 128], BF16)
    R_hT = Res("hT")
    small = sb("small", [128, 64], F32)
    R_small = Res("small")
    modv = sb("modv", [128, 3 * D], F32)
    R_modv = Res("modv")
    qp = sb("qp", [128, 8, 128], BF16)
    R_qp = Res("qp")
    wreg = sb("wreg", [128, 8, 3 * D], BF16)
    R_wreg = [Res("wreg%d" % i) for i in range(3)]
    ws = {}
    NP = 4
    R_qr, R_zs, R_stage, R_sov, R_coef, R_gates, R_ozb, R_ozT, R_xo = (
        Res("qr"), Res("zs"), Res("stage"), Res("sov"), Res("coef"), Res("gates"), Res("ozb"), Res("ozT"), Res("xo"))
    R_pbuf = [Res("pb%d" % i) for i in range(NP)]

    R_qrp = [Res("qr0"), Res("qr1")]
    R_zsp = [Res("zs0"), Res("zs1")]
    R_tmpC, R_tmpC2, R_smallC, R_smallK = Res("tmpC"), Res("tmpC2"), Res("smallC"), Res("smallK")
    R_zscr = Res("zscr")

    def alloc_ws(stack, nbr, l0):
        ws["qr"] = [sb("qr%d" % i, [128, 16, 128], BF16, stack) for i in range(2)]
        for i in range(2):
            V("pool", lambda e, i=i: e.memset(ws["qr"][i][:], 0.0), [], [R_qrp[i]])
        ws["zs"] = [sb("zs%d" % i, [128, D], BF16, stack) for i in range(2)]
        ws["pbuf"] = [sb("pb%d" % i, [128, 512], BF16, stack) for i in range(NP)]
        ws["stage"] = sb("stage", [128, nbr, 16, 65], F32, stack)
        ws["coef"] = sb("coef", [128, nbr, 16], F32, stack)
        ws["ozb"] = sb("ozb", [128, D], BF16, stack)
        ws["ozT"] = sb("ozT", [128, 8, 128], BF16, stack)
        ws["xo"] = sb("xo", [128, D], F32, stack)
        ws["tmpC"] = sb("tmpC", [128, D], F32, stack)
        ws["smallC"] = sb("smallC", [128, 8], F32, stack)
        ws["zscr"] = sb("zscr", [128, 512], F32, stack)
        if l0:
            ws["tmpC2"] = sb("tmpC2", [128, D], F32, stack)

    bgq = []

    def bg_add(g_, front=False):
        if front:
            bgq.insert(0, [g_, 0])
        else:
            bgq.append([g_, 0])

    def bg_step(force=False):
        for ent in list(bgq):
            ent[1] -= 1
            if force or ent[1] <= 0:
                try:
                    d = next(ent[0])
                    ent[1] = d if isinstance(d, int) else 1
                except StopIteration:
                    bgq.remove(ent)

    def bg_drain():
        while bgq:
            bg_step(force=True)

    def bg_tick():
        bg_step()

    def A_gen(src_rows, R_src, j, wq, R_wq, wz, R_wz, cview, sview, l0=None, part=1):
        par = j % 2
        slot = j % NX
        X, RX = xt[slot], R_xt[slot]
        qrp, zsp = ws["qr"][par], ws["zs"][par]
        if part == 2:
            state["a2_active"] = True
            zscr = ws["zscr"]
            if l0 is not None:
                wg, R_wg = l0["wg"], l0["R_wg"]
                gt, R_gt = l0["gates"][par], l0["R_gates"][par]
                bk = AB[0]
                for kc in range(8):
                    V("pe", lambda e, kc=kc, bk=bk: e.matmul(banks[bk][:, 0:48], lhsT=hT[:, kc, :], rhs=wg[:, kc, :],
                                                             start=(kc == 0), stop=(kc == 7)), [R_wg, R_hT], [R_bank[bk]])
                yield 3
                V("act", lambda e, bk=bk: e.activation(out=gt[:], in_=banks[bk][:, 0:48], func=AF.Exp, scale=-1.0),
                  [R_bank[bk]], [R_gt])
                yield 2
            for h2 in range(2):
                bk2 = AB[h2]
                for kc in range(8):
                    V("pe", lambda e, kc=kc, h2=h2, bk2=bk2: e.matmul(
                        banks[bk2][:, :], lhsT=hT[:, kc, :], rhs=wz[:, kc, h2 * 512:(h2 + 1) * 512],
                        start=(kc == 0), stop=(kc == 7)), [R_wz, R_hT], [R_bank[bk2]])
            yield 8
            for h2 in range(2):
                bk2 = AB[h2]
                sl = slice(h2 * 512, (h2 + 1) * 512)
                V("act", lambda e, bk2=bk2: e.activation(out=zscr[:], in_=banks[bk2][:, :], func=AF.Exp, scale=-1.0),
                  [R_bank[bk2]], [R_zscr])
                yield 1
                V("act", lambda e: e.activation(out=zscr[:], in_=zscr[:], func=AF.Ln, bias=1.0), [R_zscr], [R_zscr])
                yield 1
                V("act", lambda e: e.activation(out=zscr[:], in_=zscr[:], func=AF.Exp, scale=-1.0), [R_zscr], [R_zscr])
                V("dve", lambda e, bk2=bk2, sl=sl: e.tensor_tensor(out=zsp[:, sl], in0=banks[bk2][:, :], in1=zscr[:],
                                                                   op=ALU.mult), [R_bank[bk2], R_zscr], [R_zsp[par]])
                if l0 is not None and h2 == 0:
                    V("dve", lambda e: e.tensor_scalar_add(out=gt[:], in0=gt[:], scalar1=1.0), [R_gt], [R_gt])
                    V("dve", lambda e: e.reciprocal(out=gt[:], in_=gt[:]), [R_gt], [R_gt])
                if h2 == 0:
                    yield 3
            state["a2_active"] = False
            return
        S.dma("sp", X[:], src_rows, reads=[R_src] if R_src is not None else [], writes=[RX])
        yield 5
        rstd_of(X[:], RX, 0, part=1)
        yield 4
        rstd_of(X[:], RX, 0, part=2)
        yield 2
        V("dve", lambda e: e.scalar_tensor_tensor(out=tmpf[:], in0=X[:], scalar=small[:, 1:2], in1=modv[:, D:2 * D],
                                                  op0=ALU.mult, op1=ALU.mult), [RX, R_small, R_modv], [R_tmpf])
        V("pool", lambda e: e.tensor_tensor(out=hb[:], in0=tmpf[:], in1=modv[:, 0:D], op=ALU.add),
          [R_tmpf, R_modv], [R_hb])
        yield 6
        while state.get("a2_active", False):
            yield 1
        bk = AB[0]
        pv = banks[bk][:].bitcast(BF16)
        for kc in range(8):
            V("pe", lambda e, kc=kc: e.transpose(pv[:, kc * 128:(kc + 1) * 128], hb[:, kc * 128:(kc + 1) * 128], ident[:]),
              [R_hb, R_const], [R_bank[bk]])
        yield 4
        V("dve", lambda e: e.tensor_copy(out=hT[:].rearrange("p a b -> p (a b)"), in_=pv), [R_bank[bk]], [R_hT])
        yield 2
        proj_fm(wq, R_wq, 4, AB, 0)
        yield 2
        proj_fm(wq, R_wq, 4, AB, 4)
        yield 4
        V("act", lambda e: e.copy(out=qp[:, 0:4, :].rearrange("p a b -> p (a b)"), in_=banks[AB[0]][:, :]),
          [R_bank[AB[0]]], [R_qp])
        yield 1
        V("act", lambda e: e.copy(out=qp[:, 4:8, :].rearrange("p a b -> p (a b)"), in_=banks[AB[1]][:, :]),
          [R_bank[AB[1]]], [R_qp])
        if l0 is not None:
            qpz_p, R_qpz_p = l0["qpz"][par], l0["R_qpz"][par]
            for b in range(2):
                for (p0, p1, dv) in qz_dst(qpz_p, 4 * b, 4):
                    V("pool", lambda e, b=b, p0=p0, p1=p1, dv=dv: e.tensor_copy(out=dv, in_=qp[p0:p1, 4 * b:4 * b + 4, :]),
                      [R_qp], [R_qpz_p])
        yield 4
        rope_n(qp[:, 0:4, :], R_qp, 4, qz_dst(qrp, 0, 4), R_qrp[par], cview, sview, AB[0], 0)
        yield 1
        rope_n(qp[:, 4:8, :], R_qp, 4, qz_dst(qrp, 4, 4), R_qrp[par], cview, sview, AB[1], 512)

    def C_gen(slot, wo, R_w, dst_rows, R_dst, fing, combine, cdelay=8):
        ozb, ozT, xo, tmpC, smallC = ws["ozb"], ws["ozT"], ws["xo"], ws["tmpC"], ws["smallC"]
        combine()
        yield cdelay
        bk = CB
        pv = banks[bk][:].bitcast(BF16)
        for kc in range(8):
            V("pe", lambda e, kc=kc: e.transpose(pv[:, kc * 128:(kc + 1) * 128], ozb[:, kc * 128:(kc + 1) * 128], ident[:]),
              [R_ozb, R_const], [R_bank[bk]])
        yield 4
        V("dve", lambda e: e.tensor_copy(out=ozT[:].rearrange("p a b -> p (a b)"), in_=pv), [R_bank[bk]], [R_ozT])
        yield 2
        for h2 in range(2):
            sl = slice(h2 * 512, (h2 + 1) * 512)
            for kc in range(8):
                V("pe", lambda e, kc=kc, h2=h2: e.matmul(
                    banks[bk][:, :], lhsT=ozT[:, kc, :], rhs=wo[:, kc, h2 * 512:(h2 + 1) * 512],
                    start=(kc == 0), stop=(kc == 7)), [R_w, R_ozT], [R_bank[bk]])
            yield 3
            V("dve", lambda e, sl=sl, h2=h2: e.tensor_tensor(
                out=tmpC[:, sl], in0=banks[bk][:, :], in1=modv[:, 2 * D + h2 * 512:2 * D + (h2 + 1) * 512], op=ALU.mult),
              [R_bank[bk], R_modv], [R_tmpC])
            V("pool", lambda e, sl=sl: e.tensor_tensor(out=xo[:, sl], in0=tmpC[:, sl], in1=xt[slot][:, sl], op=ALU.add),
              [R_tmpC, R_xt[slot]], [R_xo])
            yield 3
        if fing is not None:
            fg, R_fg = fing
            rstd_of(xo[:], R_xo, 0, scr=(tmpC, R_tmpC), small=smallC, R_small=R_smallC, part=1)
            yield 4
            rstd_of(xo[:], R_xo, 0, scr=(tmpC, R_tmpC), small=smallC, R_small=R_smallC, part=2)
            yield 2
            V("dve", lambda e: e.scalar_tensor_tensor(out=xo[:], in0=xo[:], scalar=smallC[:, 1:2], in1=fg[:],
                                                      op0=ALU.mult, op1=ALU.mult), [R_xo, R_smallC, R_fg], [R_xo])
            yield 2
        S.dma("sp", dst_rows, xo[:], reads=[R_xo], writes=[R_dst])

    def run_p4(n_tiles, mkA, mkB, mkC):
        for _ in mkA(0, 1):
            pass
        for j in range(n_tiles):
            bg_add(mkA(j, 2), front=True)
            if j + 1 < n_tiles:
                bg_add(mkA(j + 1, 1))
            mkB(j)
            flush()
            bg_drain()
            gC = mkC(j)
            d = next(gC)
            bgq.append([gC, d if isinstance(d, int) else 1])
        bg_drain()

    def ada_rows(layer):
        with ExitStack() as les:
            cts = sb("cts", [128, 8, nseq], F32, les)
            awb = [sb("awb%d" % i, [128, 3 * D], F32, les) for i in range(2)]
            R_awb = [Res("awb0"), Res("awb1")]
            rows = sb("rows", [nseq, 3 * D], F32, les)
            brow = sb("brow", [nseq, 3 * D], F32, les)
            R_cts, R_rows, R_brow = Res("cts"), Res("rows"), Res("brow")
            ctf = cts[:].rearrange("p a b -> p (a b)")
            S.dma("sp", cts[:], ct_d, writes=[R_cts])
            for s in range(nseq):
                S.dma("sp", brow[s:s + 1, :], adab_d[layer:layer + 1, :], writes=[R_brow])
            sm = small[:, 0:8 * nseq]
            V("act", lambda e: e.activation(out=sm, in_=ctf, func=AF.Exp, scale=-1.0), [R_cts], [R_small])
            V("dve", lambda e: e.tensor_scalar_add(out=sm, in0=sm, scalar1=1.0), [R_small], [R_small])
            V("dve", lambda e: e.reciprocal(out=sm, in_=sm), [R_small], [R_small])
            V("dve", lambda e: e.tensor_mul(out=ctf, in0=ctf, in1=sm), [R_small, R_cts], [R_cts])
            aw = adaw_d[layer].rearrange("(c p) n -> c p n", p=128)
            for kc in range(8):
                b = kc % 2
                S.dma("sp", awb[b][:], aw[kc], writes=[R_awb[b]])
                for nb in range(6):
                    V("pe", lambda e, kc=kc, nb=nb, b=b: e.matmul(
                        banks[nb][0:nseq, :], lhsT=cts[:, kc, :], rhs=awb[b][:, nb * 512:(nb + 1) * 512],
                        start=(kc == 0), stop=(kc == 7)), [R_cts, R_awb[b]], [R_bank[nb]])
            for nb in range(6):
                V("dve", lambda e, nb=nb: e.tensor_add(out=rows[:, nb * 512:(nb + 1) * 512], in0=banks[nb][0:nseq, :],
                                                       in1=brow[:, nb * 512:(nb + 1) * 512]),
                  [R_bank[nb], R_brow, R_rows], [R_rows])
            S.dma("sp", mod_d[layer], rows[:], reads=[R_rows], writes=[R_mod])
            S.barrier()

    def load_mod(layer, s):
        S.dma("sp", modv[:], mod_d[layer, s:s + 1, :].broadcast_to([128, 3 * D]), reads=[R_mod], writes=[R_modv])
        S.dma("sp", tmpf2[:], normg_d[layer:layer + 1, :].broadcast_to([128, D]), writes=[R_tmpf2])
        V("dve", lambda e: e.scalar_tensor_tensor(out=modv[:, D:2 * D], in0=modv[:, D:2 * D], scalar=1.0,
                                                  in1=tmpf2[:], op0=ALU.add, op1=ALU.mult),
          [R_modv, R_tmpf2], [R_modv])

    def load_x_tile(src_rows, R_src, slot):
        S.dma("sp", xt[slot][:], src_rows, reads=[R_src] if R_src is not None else [], writes=[R_xt[slot]])

    def rstd_of(X, RX, col, scr=None, small=small, R_small=R_small, part=0):
        scr_t, R_scr = scr if scr is not None else (tmpf2, R_tmpf2)
        if part in (0, 1):
            V("act", lambda e: e.activation(out=scr_t[:], in_=X, func=AF.Square), [RX], [R_scr])
            V("dve", lambda e: e.reduce_sum(out=small[:, col:col + 1], in_=scr_t[:], axis=AX.X), [R_scr], [R_small])
            V("dve", lambda e: e.tensor_scalar(out=small[:, col + 1:col + 2], in0=small[:, col:col + 1], scalar1=1.0 / D,
                                               scalar2=EPS, op0=ALU.mult, op1=ALU.add), [R_small], [R_small])
        if part in (0, 2):
            V("act", lambda e: e.activation(out=small[:, col + 1:col + 2], in_=small[:, col + 1:col + 2], func=AF.Ln),
              [R_small], [R_small])
            V("act", lambda e: e.activation(out=small[:, col + 1:col + 2], in_=small[:, col + 1:col + 2], func=AF.Exp,
                                            scale=-0.5), [R_small], [R_small])

    def norm_tile(slot):
        X = xt[slot]
        RX = R_xt[slot]
        rstd_of(X[:], RX, 0)
        V("dve", lambda e: e.scalar_tensor_tensor(out=tmpf[:], in0=X[:], scalar=small[:, 1:2], in1=modv[:, D:2 * D],
                                                  op0=ALU.mult, op1=ALU.mult), [RX, R_small, R_modv], [R_tmpf])
        V("pool", lambda e: e.tensor_tensor(out=hb[:], in0=tmpf[:], in1=modv[:, 0:D], op=ALU.add),
          [R_tmpf, R_modv], [R_hb])
        bk = MB[0]
        pv = banks[bk][:].bitcast(BF16)
        for kc in range(8):
            V("pe", lambda e, kc=kc: e.transpose(pv[:, kc * 128:(kc + 1) * 128], hb[:, kc * 128:(kc + 1) * 128], ident[:]),
              [R_hb, R_const], [R_bank[bk]])
        V("dve", lambda e: e.tensor_copy(out=hT[:].rearrange("p a b -> p (a b)"), in_=pv), [R_bank[bk]], [R_hT])

    def norm_stage(slot, par, P):
        X, RX = xt[slot], R_xt[slot]
        sq, R_sq = P["sq"]
        st, R_st = P["stt"]
        sm, R_sm = P["small"]
        hbp, R_hbp = P["hb"][par]
        rstd_of(X[:], RX, 2 * par, scr=(sq, R_sq), small=sm, R_small=R_sm)
        V("dve", lambda e: e.scalar_tensor_tensor(out=st[:], in0=X[:], scalar=sm[:, 2 * par + 1:2 * par + 2],
                                                  in1=modv[:, D:2 * D], op0=ALU.mult, op1=ALU.mult),
          [RX, R_sm, R_modv], [R_st])
        V("pool", lambda e: e.tensor_tensor(out=hbp[:], in0=st[:], in1=modv[:, 0:D], op=ALU.add),
          [R_st, R_modv], [R_hbp])

    def transpose_stage(par, P):
        hbp, R_hbp = P["hb"][par]
        hT, R_hT = P["hT"][par]
        bk = MB[0]
        pv = banks[bk][:].bitcast(BF16)
        for kc in range(8):
            V("pe", lambda e, kc=kc: e.transpose(pv[:, kc * 128:(kc + 1) * 128], hbp[:, kc * 128:(kc + 1) * 128], ident[:]),
              [R_hbp, R_const], [R_bank[bk]])
        V("dve", lambda e: e.tensor_copy(out=hT[:].rearrange("p a b -> p (a b)"), in_=pv), [R_bank[bk]], [R_hT])

    def proj_fm(wtile, R_w, ntiles, bk_list, j0=0, hTb=None):
        hT_, R_hT_ = hTb if hTb is not None else (hT, R_hT)
        for j in range(j0, j0 + ntiles):
            bk = bk_list[j // 4]
            for kc in range(8):
                V("pe", lambda e, j=j, kc=kc, bk=bk: e.matmul(
                    banks[bk][:, (j % 4) * 128:(j % 4 + 1) * 128], lhsT=wtile[:, kc, j * 128:(j + 1) * 128],
                    rhs=hT_[:, kc, :], start=(kc == 0), stop=(kc == 7)), [R_w, R_hT_], [R_bank[bk]])

    def rope_n(srcv, R_src, n, dstv, R_dst, cview, sview, bk, tcol):
        V("pe", lambda e: e.matmul(banks[bk][:, 0:n * 128], lhsT=perm[:], rhs=srcv, start=True, stop=True),
          [R_src, R_const], [R_bank[bk]])
        cb = cview.unsqueeze(1).broadcast_to([128, n, 128])
        sbv = sview.unsqueeze(1).broadcast_to([128, n, 128])
        t1 = tmpf[:, tcol:tcol + n * 128].rearrange("p (a b) -> p a b", b=128)
        t2 = tmpf2[:, tcol:tcol + n * 128].rearrange("p (a b) -> p a b", b=128)
        V("dve", lambda e: e.tensor_tensor(out=t1, in0=banks[bk][:, 0:n * 128].rearrange("p (a b) -> p a b", b=128),
                                           in1=sbv, op=ALU.mult), [R_bank[bk], R_const], [R_tmpf])
        V("pool", lambda e: e.tensor_tensor(out=t2, in0=srcv, in1=cb, op=ALU.mult), [R_src, R_const], [R_tmpf2])
        if isinstance(dstv, list):
            for (p0, p1, dv) in dstv:
                V("dve", lambda e, p0=p0, p1=p1, dv=dv: e.tensor_tensor(out=dv, in0=t1[p0:p1], in1=t2[p0:p1], op=ALU.add),
                  [R_tmpf, R_tmpf2], [R_dst])
        else:
            V("dve", lambda e: e.tensor_tensor(out=dstv, in0=t1, in1=t2, op=ALU.add), [R_tmpf, R_tmpf2], [R_dst])

    def qz_dst(qz, pair0, n):
        return [(0, 64, qz[0:64, 2 * pair0:2 * (pair0 + n):2, :]),
                (64, 128, qz[64:128, 2 * pair0 + 1:2 * (pair0 + n):2, :])]

    def evac_qp(bl=None):
        bl = bl or P2SB
        for b in range(2):
            V("act", lambda e, b=b: e.copy(out=qp[:, 4 * b:4 * b + 4, :].rearrange("p a b -> p (a b)"),
                                           in_=banks[bl[b]][:, :]), [R_bank[bl[b]]], [R_qp])

    def z_proj(wz, R_w):
        zs = ws["zs"]
        for h2 in range(2):
            bk = MB[h2]
            for kc in range(8):
                V("pe", lambda e, kc=kc, h2=h2, bk=bk: e.matmul(
                    banks[bk][:, :], lhsT=hT[:, kc, :], rhs=wz[:, kc, h2 * 512:(h2 + 1) * 512],
                    start=(kc == 0), stop=(kc == 7)), [R_w, R_hT], [R_bank[bk]])
        for h2 in range(2):
            bk = MB[h2]
            sl = slice(h2 * 512, (h2 + 1) * 512)
            V("act", lambda e, bk=bk, sl=sl: e.activation(out=tmpf[:, sl], in_=banks[bk][:, :], func=AF.Exp, scale=-1.0),
              [R_bank[bk]], [R_tmpf])
            V("dve", lambda e, sl=sl: e.tensor_scalar_add(out=tmpf[:, sl], in0=tmpf[:, sl], scalar1=1.0),
              [R_tmpf], [R_tmpf])
            V("dve", lambda e, sl=sl: e.reciprocal(out=tmpf[:, sl], in_=tmpf[:, sl]), [R_tmpf], [R_tmpf])
            V("dve", lambda e, bk=bk, sl=sl: e.tensor_tensor(out=zs[:, sl], in0=banks[bk][:, :], in1=tmpf[:, sl],
                                                             op=ALU.mult), [R_bank[bk], R_tmpf], [R_zs])

    pending = []
    LOOK = 2

    def _drain(limit):
        while pending and sum(1 for it in pending if it[0] == "pv") > limit:
            kind, fn = pending.pop(0)
            fn()
        while pending and pending[0][0] != "pv":
            kind, fn = pending.pop(0)
            fn()

    def flush():
        while pending:
            kind, fn = pending.pop(0)
            fn()

    def defer(fn):
        pending.append(("ev", fn))
        if not any(it[0] == "pv" for it in pending[:-1]):
            _drain(LOOK)

    def attn_unit(heads, kT_fn, q_src, v_ap, o_bank, o_cols, o_w, first, last, bias=None, scale=0.125,
                  reads_extra=(), kparts=128, shared_k=False):
        n = len(heads)
        sbk = SB[state["srot"] % len(SB)]
        state["srot"] += 1
        pb_i = state["prot"] % NP
        state["prot"] += 1
        pb = ws["pbuf"][pb_i]
        rd = [R_const] + list(reads_extra)
        if bias is not None:
            blhs, brhs = bias
            V("pe", lambda e: e.matmul(banks[sbk][0:kparts, 0:n * 128], lhsT=blhs, rhs=brhs, start=True, stop=False, skip_group_check=True),
              rd, [R_bank[sbk]])
        if shared_k:
            h0_, hstep = heads[0], (heads[1] - heads[0] if n > 1 else 1)
            V("pe", lambda e: e.matmul(banks[sbk][0:kparts, 0:n * 128], lhsT=kT_fn(0),
                                       rhs=q_src[:, h0_:h0_ + hstep * (n - 1) + 1:hstep, :],
                                       start=(bias is None), stop=True, skip_group_check=True), rd, [R_bank[sbk]])
        else:
            for i, h in enumerate(heads):
                V("pe", lambda e, i=i, h=h: e.matmul(
                    banks[sbk][0:kparts, i * 128:(i + 1) * 128], lhsT=kT_fn(i),
                    rhs=q_src[:, h, :], start=(bias is None), stop=True, skip_group_check=True),
                  rd, [R_bank[sbk]])
        V("act", lambda e: e.activation(out=pb[0:kparts, 0:n * 128], in_=banks[sbk][0:kparts, 0:n * 128],
                                        func=AF.Exp, scale=scale), [R_bank[sbk]], [R_pbuf[pb_i]])

        def part2():
            for i in range(n):
                V("pe", lambda e, i=i: e.matmul(
                    banks[o_bank][:, o_cols[i]:o_cols[i] + o_w], lhsT=pb[0:kparts, i * 128:(i + 1) * 128],
                    rhs=v_ap(i), start=(first and i == 0), stop=last, skip_group_check=True), [R_pbuf[pb_i]] + rd, [R_bank[o_bank]])
        pending.append(("pv", part2))
        _drain(LOOK)
        bg_tick()

    def out_proj_residual(wo, R_w, slot, dst_rows, R_dst, fing):
        ozb, ozT, xo = ws["ozb"], ws["ozT"], ws["xo"]
        bk = MB[0]
        pv = banks[bk][:].bitcast(BF16)
        for kc in range(8):
            V("pe", lambda e, kc=kc: e.transpose(pv[:, kc * 128:(kc + 1) * 128], ozb[:, kc * 128:(kc + 1) * 128], ident[:]),
              [R_ozb, R_const], [R_bank[bk]])
        V("dve", lambda e: e.tensor_copy(out=ozT[:].rearrange("p a b -> p (a b)"), in_=pv), [R_bank[bk]], [R_ozT])
        for h2 in range(2):
            bk2 = SB[h2]
            for kc in range(8):
                V("pe", lambda e, kc=kc, h2=h2, bk2=bk2: e.matmul(
                    banks[bk2][:, :], lhsT=ozT[:, kc, :], rhs=wo[:, kc, h2 * 512:(h2 + 1) * 512],
                    start=(kc == 0), stop=(kc == 7)), [R_w, R_ozT], [R_bank[bk2]])
        for h2 in range(2):
            bk2 = SB[h2]
            sl = slice(h2 * 512, (h2 + 1) * 512)
            V("dve", lambda e, bk2=bk2, sl=sl, h2=h2: e.tensor_tensor(
                out=tmpf[:, sl], in0=banks[bk2][:, :], in1=modv[:, 2 * D + h2 * 512:2 * D + (h2 + 1) * 512], op=ALU.mult),
              [R_bank[bk2], R_modv], [R_tmpf])
            V("pool", lambda e, sl=sl: e.tensor_tensor(out=xo[:, sl], in0=tmpf[:, sl], in1=xt[slot][:, sl], op=ALU.add),
              [R_tmpf, R_xt[slot]], [R_xo])
        if fing is not None:
            fg, R_fg = fing
            rstd_of(xo[:], R_xo, 2)
            V("dve", lambda e: e.scalar_tensor_tensor(out=xo[:], in0=xo[:], scalar=small[:, 3:4], in1=fg[:],
                                                      op0=ALU.mult, op1=ALU.mult), [R_xo, R_small, R_fg], [R_xo])
        S.dma("sp", dst_rows, xo[:], reads=[R_xo], writes=[R_dst])

    def load_w(dst, src2d, R_w, chunk=1024):
        ncols = src2d.shape[1]
        v = src2d.rearrange("(c p) n -> p c n", p=128)
        c0 = 0
        while c0 < ncols:
            c1 = min(ncols, c0 + chunk)
            S.dma("pool", dst[:, :, c0:c1], v[:, :, c0:c1], writes=[R_w])
            c0 = c1

    def layer0():
        with ExitStack() as les:
            def lsb(name, shape, dt):
                return sb(name, shape, dt, les)
            vcmp_c = load_const("c_vcmp", les)
            wq = wreg[:, :, 0:D]
            wz = wreg[:, :, D:2 * D]
            wo = wreg[:, :, 2 * D:3 * D]
            wg = lsb("wg", [128, 8, 48], BF16)
            w2k = lsb("w2k", [128, 2, 128], BF16)
            w2v = lsb("w2v", [128, 2, 64], BF16)
            pe2 = [lsb("pe2k_s", [128, 16], BF16), lsb("pe2v_s", [128, 16], BF16)]
            R_wq, R_wz, R_wo = R_wreg
            R_wkv, R_wv, R_wg, R_w1, R_w2, R_pe = Res("wkv"), Res("wv"), Res("wg"), Res("w1"), Res("w2"), Res("pe")
            ksT = lsb("ksT", [128, 2, T], BF16)
            kwT = lsb("kwT", [128, 2, T], BF16)
            vs = lsb("vs", [128, NT, 2, 65], BF16)
            vw = lsb("vw", [128, NT, 2, 65], BF16)
            kcmpT = lsb("kcmpT", [128, 2, 128], BF16)
            vcmp = lsb("vcmp", [128, 2, 97], BF16)
            R_kc2, R_ksT, R_kwT, R_vs, R_vw = Res("kc2"), Res("ksT"), Res("kwT"), Res("vs"), Res("vw")
            R_kcmp, R_vcmp, R_hid, R_hbias, R_kvp, R_negT, R_wkt = (Res("kcmp"), Res("vcmp"), Res("hid"),
                                                                    Res("hbias"), Res("kvp"), Res("negT"), Res("wkt"))
            W = win0_d
            offs = {"q": 0, "kc": 1024, "vc": 1152, "ks": 1280, "vs": 1408, "kw": 1536, "vw": 1664, "z": 1792, "g": 2816}
            Wv = W.rearrange("(c p) n -> p c n", p=128)
            for kv in range(2):
                S.dma("pool", pe2[kv][:], pe2_d[kv], writes=[R_pe])
            w2kd = ckw2_d.rearrange("(c p) n -> p c n", p=128)
            S.dma("pool", w2k[:, :, 0:64], w2kd, writes=[R_w2])
            S.dma("pool", w2k[:, :, 64:128], w2kd, writes=[R_w2])
            S.dma("pool", w2v[:, :, :], cvw2_d.rearrange("(c p) n -> p c n", p=128), writes=[R_w2])
            S.dma("pool", wg[:, :, :], Wv[:, :, offs["g"]:offs["g"] + 48], writes=[R_wg])
            V("pool", lambda e: e.memset(vs[:, :, :, 64:65], 1.0), [], [R_vs])
            V("pool", lambda e: e.memset(vw[:, :, :, 64:65], 1.0), [], [R_vw])
            for g in range(2):
                V("pool", lambda e, g=g: e.tensor_copy(out=vcmp[:, g, 64:97], in_=vcmp_c[:]), [R_const], [R_vcmp])

            mark("w0")
            for s in range(nseq):
                load_mod(0, s)
                mark("mod0")
                with ExitStack() as sa:
                    w1 = [sb("w1k", [128, 16, 256], BF16, sa), sb("w1v", [128, 16, 256], BF16, sa)]
                    wsrc = sb("wsrc", [128, 8, 768], BF16, sa)
                    S.dma("pool", wsrc[:, :, :], Wv[:, :, 1024:1792], writes=[R_wkv])
                    loc = {"kc": 0, "vc": 128, "ks": 256, "kw": 512}
                    dup_c0 = [loc[nm] + 64 * g for nm in ("kc", "vc", "ks", "kw") for g in range(2)]
                    wkv = sb("wkv", [128, 8, 8 * 128], BF16, sa)
                    R_wkv2 = Res("wkv2")
                    for j in range(8):
                        V("pool", lambda e, j=j: e.tensor_copy(
                            out=wkv[:, :, j * 128:(j + 1) * 128].rearrange("p c (a b) -> p c a b", b=64),
                            in_=wsrc[:, :, dup_c0[j]:dup_c0[j] + 64].unsqueeze(2).broadcast_to([128, 8, 2, 64])),
                          [R_wkv], [R_wkv2])
                    kc2 = [sb("kc2", [128, 2, T], BF16, sa), sb("vc2", [128, 2, T], BF16, sa)]
                    hid = sb("hid", [128, 2, 128], BF16, sa)
                    hbias = sb("hbias", [128, 4], F32, sa)
                    kvp = sb("kvp", [128, 4, 128], BF16, sa)
                    for kv, w1d in enumerate((ckw1_d, cvw1_d)):
                        w1v_ = w1d.rearrange("(j p) n -> p j n", p=128)
                        for j0 in range(0, 16, 8):
                            S.dma("pool", w1[kv][:, j0:j0 + 8, :], w1v_[:, j0:j0 + 8, :], writes=[R_w1])
                    hbB = sb("hbB", [128, D], BF16, sa)
                    sqS = sb("sqS", [128, D], F32, sa)
                    sttS = sb("sttS", [128, D], F32, sa)
                    smallP = sb("smallP", [128, 8], F32, sa)
                    hTB = sb("hTB", [128, 8, 128], BF16, sa)
                    P2 = {"sq": (sqS, Res("sqS")), "stt": (sttS, Res("sttS")), "small": (smallP, Res("smallP")),
                          "hb": [(hb, R_hb), (hbB, Res("hbB"))], "hT": [(hT, R_hT), (hTB, Res("hTB"))]}
                    for t_ in range(2):
                        load_x_tile(x_d[s, t_ * 128:(t_ + 1) * 128, :], None, t_ % NX)
                        norm_stage(t_ % NX, t_ % 2, P2)
                    transpose_stage(0, P2)
                    for tt in range(NT):
                        slot = tt % NX
                        cs = slice(tt * 128, (tt + 1) * 128)
                        if tt + 2 < NT:
                            load_x_tile(x_d[s, (tt + 2) * 128:(tt + 3) * 128, :], None, (tt + 2) % NX)
                            norm_stage((tt + 2) % NX, (tt + 2) % 2, P2)
                        if tt + 1 < NT:
                            transpose_stage((tt + 1) % 2, P2)
                        hTc, R_hTc = P2["hT"][tt % 2]
                        proj_fm(wkv, R_wkv2, 8, [P2SB[0], P2SB[1]], hTb=(hTc, R_hTc))
                        bk = MB[1]
                        for kc in range(8):
                            V("pe", lambda e, kc=kc, bk=bk: e.matmul(
                                banks[bk][:, 0:256], lhsT=hTc[:, kc, :],
                                rhs=wsrc[:, kc, 384:768].rearrange("p (a b) -> p a b", b=128)[:, 0:3:2, :],
                                start=(kc == 0), stop=(kc == 7)), [R_wkv, R_hTc], [R_bank[bk]])
                        for j in range(4):
                            kv, g = j // 2, j % 2
                            src = banks[P2SB[0]][:, j * 128:(j + 1) * 128]
                            V("dve", lambda e, kv=kv, g=g, src=src, cs=cs: e.tensor_copy(out=kc2[kv][0:64, g, cs],
                                                                                         in_=src[0:64, :]),
                              [R_bank[P2SB[0]]], [R_kc2])
                            if tt == 0:
                                V("dve", lambda e, kv=kv, g=g, src=src: e.tensor_copy(out=kc2[kv][64:128, g, 0:127],
                                                                                      in_=src[64:128, 1:128]),
                                  [R_bank[P2SB[0]]], [R_kc2])
                            else:
                                V("dve", lambda e, kv=kv, g=g, src=src, tt=tt: e.tensor_copy(
                                    out=kc2[kv][64:128, g, tt * 128 - 1:(tt + 1) * 128 - 1], in_=src[64:128, :]),
                                  [R_bank[P2SB[0]]], [R_kc2])
                        V("act", lambda e: e.copy(out=kvp[:].rearrange("p a b -> p (a b)"), in_=banks[P2SB[1]][:, :]),
                          [R_bank[P2SB[1]]], [R_kvp])
                        rope_n(kvp[:, 0:2, :], R_kvp, 2, ksT[:, :, cs], R_ksT, cos_t[:, cs], sin_t[:, cs], P2SB[2], 0)
                        rope_n(kvp[:, 2:4, :], R_kvp, 2, kwT[:, :, cs], R_kwT, cos_t[:, cs], sin_t[:, cs], P2SB[3], 256)
                        bk = MB[1]
                        V("act", lambda e, bk=bk, tt=tt: e.copy(out=vs[:, tt, :, 0:64],
                                                                in_=banks[bk][:, 0:128].rearrange("p (g d) -> p g d", d=64)),
                          [R_bank[bk]], [R_vs])
                        V("act", lambda e, bk=bk, tt=tt: e.copy(out=vw[:, tt, :, 0:64],
                                                                in_=banks[bk][:, 128:256].rearrange("p (g d) -> p g d", d=64)),
                          [R_bank[bk]], [R_vw])
                        mark("p2t%d" % tt)
                    mark("p2")
                    for kv in range(2):
                        for hc in range(2):
                            bk = MB[1]
                            for j in range(16):
                                V("pe", lambda e, kv=kv, hc=hc, j=j, bk=bk: e.matmul(
                                    banks[bk][:, 0:1], lhsT=w1[kv][:, j, hc * 128:(hc + 1) * 128], rhs=pe2[kv][:, j:j + 1],
                                    start=(j == 0), stop=(j == 15)), [R_w1, R_pe], [R_bank[bk]])
                            V("dve", lambda e, kv=kv, hc=hc, bk=bk: e.tensor_copy(
                                out=hbias[:, kv * 2 + hc:kv * 2 + hc + 1], in_=banks[bk][:, 0:1]), [R_bank[bk]], [R_hbias])
                    for kv in range(2):
                        for g in range(2):
                            for hc in range(2):
                                bk = P2SB[(kv * 4 + g * 2 + hc) % 4]
                                for j in range(16):
                                    V("pe", lambda e, kv=kv, g=g, hc=hc, j=j, bk=bk: e.matmul(
                                        banks[bk][:, 0:127], lhsT=w1[kv][:, j, hc * 128:(hc + 1) * 128],
                                        rhs=kc2[kv][:, g, 2 * j:2 * j + 16 * 126 + 1:16], start=(j == 0), stop=(j == 15)),
                                      [R_w1, R_kc2], [R_bank[bk]])
                                bcol = hbias[:, kv * 2 + hc:kv * 2 + hc + 1]
                                V("dve", lambda e, bk=bk, bcol=bcol: e.tensor_scalar(
                                    out=tmpf[:, 0:127], in0=banks[bk][:, 0:127], scalar1=bcol, scalar2=None, op0=ALU.add),
                                  [R_bank[bk], R_hbias], [R_tmpf])
                                V("act", lambda e: e.activation(out=tmpf2[:, 0:127], in_=tmpf[:, 0:127], func=AF.Exp,
                                                                scale=-1.0), [R_tmpf], [R_tmpf2])
                                V("dve", lambda e: e.tensor_scalar_add(out=tmpf2[:, 0:127], in0=tmpf2[:, 0:127], scalar1=1.0),
                                  [R_tmpf2], [R_tmpf2])
                                V("dve", lambda e: e.reciprocal(out=tmpf2[:, 0:127], in_=tmpf2[:, 0:127]),
                                  [R_tmpf2], [R_tmpf2])
                                V("dve", lambda e, hc=hc: e.tensor_tensor(out=hid[:, hc, 0:127], in0=tmpf[:, 0:127],
                                                                          in1=tmpf2[:, 0:127], op=ALU.mult),
                                  [R_tmpf, R_tmpf2], [R_hid])
                            bk = MB[1]
                            if kv == 0:
                                for hc in range(2):
                                    V("pe", lambda e, hc=hc, bk=bk: e.matmul(
                                        banks[bk][:, 0:127], lhsT=w2k[:, hc, :], rhs=hid[:, hc, 0:127],
                                        start=(hc == 0), stop=(hc == 1)), [R_w2, R_hid], [R_bank[bk]])
                                V("dve", lambda e, g=g, bk=bk: e.tensor_copy(out=kcmpT[:, g, 0:127], in_=banks[bk][:, 0:127]),
                                  [R_bank[bk]], [R_kcmp])
                            else:
                                for hc in range(2):
                                    V("pe", lambda e, hc=hc, bk=bk: e.matmul(
                                        banks[bk][0:127, 0:64], lhsT=hid[:, hc, 0:127], rhs=w2v[:, hc, :],
                                        start=(hc == 0), stop=(hc == 1)), [R_w2, R_hid], [R_bank[bk]])
                                V("dve", lambda e, g=g, bk=bk: e.tensor_copy(out=vcmp[0:127, g, 0:64],
                                                                             in_=banks[bk][0:127, 0:64]),
                                  [R_bank[bk]], [R_vcmp])
                    mark("p3")
                    S.barrier()
                with ExitStack() as sB:
                    anti = load_const("c_anti", sB)
                    cmpbias = load_const("c_cmpbias", sB)
                    esel = load_const("c_esel", sB)
                    cand_t = load_const("c_cand", sB)
                    forced_t = load_const("c_forced", sB)
                    alloc_ws(sB, 3, False)
                    stage, coef, ozb, tmpC = ws["stage"], ws["coef"], ws["ozb"], ws["tmpC"]
                    sov = sb("sov", [128, 2, 8, 32], F32, sB)
                    gates2 = [sb("gates%d" % i, [128, 48], F32, sB) for i in range(2)]
                    R_gates2 = [Res("gates0"), Res("gates1")]
                    qpz2 = [sb("qpz%d" % i, [128, 16, 128], BF16, sB) for i in range(2)]
                    R_qpz2 = [Res("qpz0"), Res("qpz1")]
                    for i in range(2):
                        V("pool", lambda e, i=i: e.memset(qpz2[i][:], 0.0), [], [R_qpz2[i]])
                    negT = sb("negT", [128, 2, 4, 128], BF16, sB)
                    V("pool", lambda e: e.memset(negT[:], 0.0), [], [R_negT])
                    wk_t = sb("wk_t", [128, 16, 32], F32, sB)
                    tk = sb("tk", [128, 224], F32, sB)
                    hbK = sb("hbK", [128, 2, 32], BF16, sB)
                    smallK = sb("smallK", [128, 16], F32, sB)
                    R_tk, R_hbK = Res("tk"), Res("hbK")
                    l0info = {"qpz": qpz2, "R_qpz": R_qpz2, "wg": wg, "R_wg": R_wg, "gates": gates2, "R_gates": R_gates2}
                    dst_d, R_dst = (x1_d, R_x1[s]) if 1 in layers else (out_d, R_out)

                    def mkA(j, part, s=s):
                        cs = slice(j * 128, (j + 1) * 128)
                        return A_gen(x_d[s, cs, :], None, j, wq, R_wq, wz, R_wz, cos_t[:, cs], sin_t[:, cs], l0info, part)

                    def mkB(tt):
                        par = tt % 2
                        cs = slice(tt * 128, (tt + 1) * 128)
                        qr, R_qr_ = ws["qr"][par], R_qrp[par]
                        qpz, R_qpz = qpz2[par], R_qpz2[par]
                        n_units = 2 * (2 + 2 * (tt + 1) + 2 * (min(tt, 4) + 1))
                        state["bg_stride"] = max(1, n_units // 26)
                        def cmp_(g):
                            hl = [8 * g + i for i in range(8)]
                            for b in range(2):
                                hs = hl[b:8:2]
                                attn_unit(hs, lambda i, g=g: kcmpT[:, g, 0:127], qpz,
                                          lambda i, g=g: vcmp[0:127, g, :], OB[b], [0, 97, 194, 291], 97, True, True,
                                          bias=(ident[0:127, 0:127],
                                                cmpbias[0:127, cs].unsqueeze(1).broadcast_to([127, 4, 128])),
                                          reads_extra=[R_kcmp, R_vcmp, R_qpz], kparts=127, shared_k=True)
                                ov = banks[OB[b]][:, 0:388].rearrange("p (a b) -> p a b", b=97)
                                defer(lambda b=b, g=g, ov=ov: V("dve", lambda e: e.tensor_copy(
                                    out=stage[:, 0, 8 * g + b:8 * g + 8:2, :], in_=ov[:, :, 0:65]),
                                    [R_bank[OB[b]]], [R_stage]))
                                if tt >= 8:
                                    defer(lambda b=b, g=g, ov=ov: V("dve", lambda e: e.tensor_copy(
                                        out=sov[:, g, b:8:2, :], in_=ov[:, :, 65:97]), [R_bank[OB[b]]], [R_sov]))


                        def topk_all():
                            V("dve", lambda e: e.tensor_scalar_max(out=smallK[:, 0:16], in0=stage[:, 0, :, 64], scalar1=1e-30),
                              [R_stage], [R_smallK])
                            V("dve", lambda e: e.reciprocal(out=smallK[:, 0:16], in_=smallK[:, 0:16]), [R_smallK], [R_smallK])
                            V("dve", lambda e: e.tensor_tensor(
                                out=wk_t[:, :, :], in0=sov[:].rearrange("p g h j -> p (g h) j"),
                                in1=smallK[:, 0:16].unsqueeze(2).broadcast_to([128, 16, 32]), op=ALU.mult),
                              [R_sov, R_smallK], [R_wkt])
                            tkw = tk[:, 0:64].rearrange("p (g j) -> p g j", g=2)
                            V("dve", lambda e: e.tensor_reduce(out=tkw, in_=wk_t[:].rearrange("p (g h) j -> p g j h", g=2),
                                                               axis=AX.X, op=ALU.add), [R_wkt], [R_tk])
                            candb = cand_t[:, tt - 8, :].unsqueeze(1).broadcast_to([128, 2, 32])
                            forcb = forced_t[:, tt - 8, :].unsqueeze(1).broadcast_to([128, 2, 32])
                            V("dve", lambda e: e.scalar_tensor_tensor(out=tkw, in0=tkw, scalar=1.0, in1=candb,
                                                                      op0=ALU.add, op1=ALU.mult), [R_tk, R_const], [R_tk])
                            V("dve", lambda e: e.tensor_scalar_add(out=tk[:, 0:64], in0=tk[:, 0:64], scalar1=-1.0),
                              [R_tk], [R_tk])
                            for g in range(2):
                                wsl = tk[:, 32 * g:32 * g + 32]
                                m1 = tk[:, 64 + 8 * g:72 + 8 * g]
                                m2 = tk[:, 112 + 8 * g:120 + 8 * g]
                                rp = tk[:, 80:112] if g == 0 else tk[:, 192:224]
                                V("dve", lambda e, wsl=wsl, m1=m1: e.max(out=m1, in_=wsl), [R_tk], [R_tk])
                                V("dve", lambda e, wsl=wsl, m1=m1, rp=rp: e.match_replace(
                                    out=rp, in_to_replace=m1, in_values=wsl, imm_value=-2.0), [R_tk], [R_tk])
                                V("dve", lambda e, m2=m2, rp=rp: e.max(out=m2, in_=rp), [R_tk], [R_tk])
                                V("dve", lambda e, g=g, wsl=wsl, m2=m2: e.tensor_scalar(
                                    out=tk[:, 128 + 32 * g:160 + 32 * g], in0=wsl, scalar1=m2[:, 4:5], scalar2=None,
                                    op0=ALU.is_ge), [R_tk], [R_tk])
                            selw = tk[:, 128:192].rearrange("p (g j) -> p g j", g=2)
                            V("dve", lambda e: e.tensor_tensor(out=selw, in0=selw, in1=forcb, op=ALU.add),
                              [R_tk, R_const], [R_tk])
                            V("dve", lambda e: e.tensor_scalar(out=hbK[:], in0=selw, scalar1=-1.0, scalar2=-NEG,
                                                               op0=ALU.add, op1=ALU.mult), [R_tk], [R_hbK])
                        def topk_pe_(g):
                            bk = SB[state["srot"] % len(SB)]
                            state["srot"] += 1
                            pvn = banks[bk][:].bitcast(BF16)
                            V("pe", lambda e, pvn=pvn, g=g: e.transpose(pvn[0:32, 0:128], hbK[:, g, :], ident[:]),
                              [R_hbK, R_const], [R_bank[bk]])
                            V("dve", lambda e, pvn=pvn, g=g: e.tensor_copy(
                                out=negT[0:32, g, :, :], in_=pvn[0:32, 0:128].unsqueeze(1).broadcast_to([32, 4, 128])),
                              [R_bank[bk]], [R_negT])
                        def win_(g):
                            hl = [8 * g + i for i in range(8)]
                            for b in range(2):
                                hs = hl[b:8:2]
                                k0 = max(0, tt - 4)
                                for kt in range(k0, tt + 1):
                                    if kt == tt:
                                        bias = (ident[:], causal[:].unsqueeze(1).broadcast_to([128, 4, 128]))
                                    elif kt == tt - 4:
                                        bias = (ident[:], anti[:].unsqueeze(1).broadcast_to([128, 4, 128]))
                                    else:
                                        bias = None
                                    attn_unit(hs, lambda i, kt=kt, g=g: kwT[:, g, kt * 128:(kt + 1) * 128], qr,
                                              lambda i, kt=kt, g=g: vw[:, kt, g, :], OB[b], [0, 65, 130, 195], 65,
                                              kt == k0, kt == tt, bias=bias, reads_extra=[R_kwT, R_vw, R_qr_],
                                              shared_k=True)
                                defer(lambda b=b, g=g: V("dve", lambda e: e.tensor_copy(
                                    out=stage[:, 2, 8 * g + b:8 * g + 8:2, :],
                                    in_=banks[OB[b]][:, 0:260].rearrange("p (a b) -> p a b", b=65)),
                                    [R_bank[OB[b]]], [R_stage]))
                        def slc_(g):
                            hl = [8 * g + i for i in range(8)]
                            for b in range(2):
                                hs = hl[b:8:2]
                                for kt in range(tt + 1):
                                    if kt == tt:
                                        bias = (ident[:], causal[:].unsqueeze(1).broadcast_to([128, 4, 128]))
                                    elif tt >= 8:
                                        bias = (esel[:, kt, :], negT[:, g, :, :])
                                    else:
                                        bias = None
                                    attn_unit(hs, lambda i, kt=kt, g=g: ksT[:, g, kt * 128:(kt + 1) * 128], qr,
                                              lambda i, kt=kt, g=g: vs[:, kt, g, :], OB[b], [0, 65, 130, 195], 65,
                                              kt == 0, kt == tt, bias=bias, reads_extra=[R_ksT, R_vs, R_qr_, R_negT],
                                              shared_k=True)
                                defer(lambda b=b, g=g: V("dve", lambda e: e.tensor_copy(
                                    out=stage[:, 1, 8 * g + b:8 * g + 8:2, :],
                                    in_=banks[OB[b]][:, 0:260].rearrange("p (a b) -> p a b", b=65)),
                                    [R_bank[OB[b]]], [R_stage]))
                        win_(0)
                        cmp_(0)
                        cmp_(1)
                        if tt >= 8:
                            flush()
                            topk_all()
                        win_(1)
                        if tt >= 8:
                            topk_pe_(0)
                            topk_pe_(1)
                        slc_(0)
                        slc_(1)

                    def mkC(tt, s=s):
                        par = tt % 2
                        cs = slice(tt * 128, (tt + 1) * 128)
                        gt, R_gt = gates2[par], R_gates2[par]
                        zsp, R_z = ws["zs"][par], R_zsp[par]

                        def combine():
                            V("dve", lambda e: e.tensor_scalar_max(out=coef[:], in0=stage[:, :, :, 64], scalar1=1e-30),
                              [R_stage], [R_coef])
                            V("dve", lambda e: e.reciprocal(out=coef[:], in_=coef[:]), [R_coef], [R_coef])
                            V("dve", lambda e: e.tensor_tensor(out=coef[:], in0=coef[:],
                                                               in1=gt[:].rearrange("p (h b) -> p b h", b=3), op=ALU.mult),
                              [R_coef, R_gt], [R_coef])
                            o3 = tmpC[:].rearrange("p (h d) -> p h d", d=64)
                            o3b = tmpf2[:].rearrange("p (h d) -> p h d", d=64)
                            V("dve", lambda e: e.tensor_tensor(
                                out=o3, in0=stage[:, 0, :, 0:64],
                                in1=coef[:, 0, :].unsqueeze(2).broadcast_to([128, 16, 64]), op=ALU.mult),
                              [R_stage, R_coef], [R_tmpC])
                            for br in (1, 2):
                                V("pool", lambda e, br=br: e.tensor_tensor(
                                    out=o3b, in0=stage[:, br, :, 0:64],
                                    in1=coef[:, br, :].unsqueeze(2).broadcast_to([128, 16, 64]), op=ALU.mult),
                                  [R_stage, R_coef], [R_tmpf2])
                                V("dve", lambda e: e.tensor_tensor(out=o3, in0=o3, in1=o3b, op=ALU.add),
                                  [R_tmpC, R_tmpf2], [R_tmpC])
                            V("dve", lambda e: e.tensor_tensor(out=ozb[:], in0=tmpC[:], in1=zsp[:], op=ALU.mult),
                              [R_tmpC, R_z], [R_ozb])
                        return C_gen(tt % NX, wo, R_wo, dst_d[s, cs, :], R_dst, None, combine, cdelay=20)

                    run_p4(NT, mkA, mkB, mkC)
                    S.barrier()

    def layer1():
        with ExitStack() as les:
            m0_t = load_const("c_m0", les)
            m1_t = load_const("c_m1", les)
            fing = sb("fing", [128, D], F32, les)
            R_fing = Res("fing")
            S.dma("sp", fing[:], fing_d.broadcast_to([128, D]), writes=[R_fing])
            kT = sb("kT1", [128, 8, T], BF16, les)
            va = sb("va1", [128, NT, 16, 65], BF16, les)
            R_kT, R_va = Res("kT1"), Res("va1")
            V("pool", lambda e: e.memset(va[:, :, :, 64:65], 1.0), [], [R_va])
            alloc_ws(les, 1, False)
            stage, coef, ozb, tmpC = ws["stage"], ws["coef"], ws["ozb"], ws["tmpC"]
            R_wa, R_wb, R_wc = R_wreg
            W = win1_d
            for s in range(nseq):
                src = x1_d if 0 in layers else x_d
                R_src = R_x1[s] if 0 in layers else None
                srcv = src[s].rearrange("(p r) d -> r p d", r=16)
                dstv = out_d[s].rearrange("(p r) d -> r p d", r=16)
                load_mod(1, s)
                wk = wreg[:, :, 0:D]
                wv = wreg[:, :, D:2 * D]
                load_w(wk, W[:, D:2 * D], R_wa)
                load_w(wv, W[:, 2 * D:3 * D], R_wb)
                if s == 0:
                    load_w(wreg[:, :, 2 * D:3 * D], wout1_d, R_wc)
                P2 = {"sq": (ws["xo"], R_xo), "stt": (ws["tmpC"], R_tmpC), "small": (ws["smallC"], R_smallC),
                      "hb": [(hb, R_hb), (ws["ozb"], R_ozb)], "hT": [(hT, R_hT), (ws["ozT"], R_ozT)]}
                for t_ in range(2):
                    load_x_tile(srcv[t_], R_src, t_ % NX)
                    norm_stage(t_ % NX, t_ % 2, P2)
                transpose_stage(0, P2)
                for r in range(NT):
                    slot = r % NX
                    if r + 2 < NT:
                        load_x_tile(srcv[r + 2], R_src, (r + 2) % NX)
                        norm_stage((r + 2) % NX, (r + 2) % 2, P2)
                    if r + 1 < NT:
                        transpose_stage((r + 1) % 2, P2)
                    hTc, R_hTc = P2["hT"][r % 2]
                    cs = slice(r * 128, (r + 1) * 128)
                    cv = cos_t[:, r:r + 16 * 127 + 1:16]
                    sv = sin_t[:, r:r + 16 * 127 + 1:16]
                    proj_fm(wk, R_wa, 8, [P2SB[0], P2SB[1]], hTb=(hTc, R_hTc))
                    for h2 in range(2):
                        bk = 4 + h2
                        for kc in range(8):
                            V("pe", lambda e, kc=kc, h2=h2, bk=bk: e.matmul(
                                banks[bk][:, :], lhsT=hTc[:, kc, :], rhs=wv[:, kc, h2 * 512:(h2 + 1) * 512],
                                start=(kc == 0), stop=(kc == 7)), [R_wb, R_hTc], [R_bank[bk]])
                    evac_qp()
                    rope_n(qp[:, 0:4, :], R_qp, 4, kT[:, 0:4, cs], R_kT, cv, sv, P2SB[2], 0)
                    rope_n(qp[:, 4:8, :], R_qp, 4, kT[:, 4:8, cs], R_kT, cv, sv, P2SB[3], 512)
                    for h2 in range(2):
                        bk = 4 + h2
                        V("act", lambda e, h2=h2, bk=bk, r=r: e.copy(
                            out=va[:, r, 8 * h2:8 * h2 + 8, 0:64], in_=banks[bk][:, :].rearrange("p (h d) -> p h d", d=64)),
                          [R_bank[bk]], [R_va])
                wq = wreg[:, :, 0:D]
                wz = wreg[:, :, D:2 * D]
                wo = wreg[:, :, 2 * D:3 * D]
                load_w(wq, W[:, 0:D], R_wa)
                load_w(wz, W[:, 3 * D:4 * D], R_wb)

                def mkA(r, part, srcv=srcv, R_src=R_src):
                    return A_gen(srcv[r], R_src, r, wq, R_wa, wz, R_wb,
                                 cos_t[:, r:r + 16 * 127 + 1:16], sin_t[:, r:r + 16 * 127 + 1:16], None, part)

                def mkB(r):
                    par = r % 2
                    qr, R_qr_ = ws["qr"][par], R_qrp[par]
                    state["bg_stride"] = 2
                    for gi, (h0, h1) in enumerate(DIL_G):
                        ob = OB[gi % 2]
                        if gi == 0:
                            klist = [(rp, m0_t[:, (1 if r == rp else (2 if r > rp else 0)), :]) for rp in range(16)]
                        elif gi == 1:
                            klist = [(rp, m1_t[:, (1 if r == rp else (2 if r > rp else 0)), :])
                                     for rp in range(16) if (r - rp) % 4 == 0]
                        else:
                            klist = [(r, causal[:])]
                        chunks = [[h for h in range(h0, h1) if h % 2 == par_] for par_ in range(2)]
                        for ch in chunks:
                            n = len(ch)
                            for ki, (rp, mk) in enumerate(klist):
                                attn_unit(ch, lambda i, ch=ch, rp=rp: kT[:, ch[i] // 2, rp * 128:(rp + 1) * 128], qr,
                                          lambda i, ch=ch, rp=rp: va[:, rp, ch[i], :], ob, [(h - h0) * 65 for h in ch], 65,
                                          ki == 0, ki == len(klist) - 1,
                                          bias=(ident[:], mk.unsqueeze(1).broadcast_to([128, n, 128])),
                                          reads_extra=[R_kT, R_va, R_qr_])
                        nh = h1 - h0
                        defer(lambda ob=ob, h0=h0, nh=nh: V("dve", lambda e: e.tensor_copy(
                            out=stage[:, 0, h0:h0 + nh, :],
                            in_=banks[ob][:, 0:nh * 65].rearrange("p (a b) -> p a b", b=65)), [R_bank[ob]], [R_stage]))

                def mkC(r, dstv=dstv):
                    par = r % 2
                    zsp, R_z = ws["zs"][par], R_zsp[par]
                    smallC = ws["smallC"]

                    def combine():
                        for gi, (h0, h1) in enumerate(DIL_G):
                            V("dve", lambda e, gi=gi, h0=h0, h1=h1: e.tensor_reduce(
                                out=smallC[:, gi:gi + 1], in_=stage[:, 0, h0:h1, 64], axis=AX.X, op=ALU.add),
                              [R_stage], [R_smallC])
                            V("dve", lambda e, gi=gi, h0=h0, h1=h1: e.tensor_scalar_mul(
                                out=smallC[:, gi:gi + 1], in0=smallC[:, gi:gi + 1], scalar1=1.0 / (h1 - h0)),
                              [R_smallC], [R_smallC])
                        V("dve", lambda e: e.tensor_reduce(out=smallC[:, 3:4], in_=smallC[:, 0:3], axis=AX.X, op=ALU.add),
                          [R_smallC], [R_smallC])
                        V("dve", lambda e: e.reciprocal(out=smallC[:, 3:4], in_=smallC[:, 3:4]), [R_smallC], [R_smallC])
                        V("dve", lambda e: e.tensor_scalar(out=smallC[:, 4:7], in0=smallC[:, 0:3], scalar1=smallC[:, 3:4],
                                                           scalar2=3.0, op0=ALU.mult, op1=ALU.mult), [R_smallC], [R_smallC])
                        V("dve", lambda e: e.reciprocal(out=coef[:, 0, :], in_=stage[:, 0, :, 64]), [R_stage], [R_coef])
                        for gi, (h0, h1) in enumerate(DIL_G):
                            V("dve", lambda e, gi=gi, h0=h0, h1=h1: e.tensor_scalar(
                                out=coef[:, 0, h0:h1], in0=coef[:, 0, h0:h1], scalar1=smallC[:, 4 + gi:5 + gi], scalar2=None,
                                op0=ALU.mult), [R_smallC, R_coef], [R_coef])
                        o3 = tmpC[:].rearrange("p (h d) -> p h d", d=64)
                        V("dve", lambda e: e.tensor_tensor(out=o3, in0=stage[:, 0, :, 0:64],
                                                           in1=coef[:, 0, :].unsqueeze(2).broadcast_to([128, 16, 64]),
                                                           op=ALU.mult), [R_stage, R_coef], [R_tmpC])
                        V("dve", lambda e: e.tensor_tensor(out=ozb[:], in0=tmpC[:], in1=zsp[:], op=ALU.mult),
                          [R_tmpC, R_z], [R_ozb])
                    return C_gen(r % NX, wo, R_wc, dstv[r], R_out, (fing, R_fing), combine)

                run_p4(NT, mkA, mkB, mkC)
            S.barrier()

    try:
        mark("consts")
        if 0 in layers:
            load_w(wreg[:, :, 0:D], win0_d[:, 0:D], R_wreg[0])
            load_w(wreg[:, :, D:2 * D], win0_d[:, 1792:1792 + D], R_wreg[1])
            load_w(wreg[:, :, 2 * D:3 * D], wout0_d, R_wreg[2])
        for l_ in layers:
            ada_rows(l_)
        mark("ada0")
        if 0 in layers:
            layer0()
        if 1 in layers:
            layer1()
    except _Stop:
        pass
    S.finish()
    es.close()
    return nc, S


_CACHE = {}


def _get_program(nseq, layers, stop=None):
    key = (nseq, tuple(layers), stop)
    if key not in _CACHE:
        _CACHE[key] = build_program(nseq, layers, stop)
    return _CACHE[key]


def run(inputs, ncores=NCORES, nseq=2, layers=(0, 1), stop=None):
    nc, S = _get_program(nseq, layers, stop)
    consts = make_consts()
    f = lambda a: np.ascontiguousarray(np.asarray(a, dtype=np.float32))

    def pe2(pe):
        return np.ascontiguousarray(f(pe).reshape(16, 2, 64).transpose(1, 2, 0).reshape(128, 16))

    shared = {
        "norm_g": f(inputs["norm_g"]), "ada_w": f(inputs["ada_w"]), "ada_b": f(inputs["ada_b"]),
        "nsa_w_in": f(inputs["nsa_w_in"][0]), "pe2k": pe2(inputs["nsa_pe_k"][0]), "pe2v": pe2(inputs["nsa_pe_v"][0]),
        "nsa_ck_w1": f(inputs["nsa_ck_w1"][0]), "nsa_ck_w2": f(inputs["nsa_ck_w2"][0]),
        "nsa_cv_w1": f(inputs["nsa_cv_w1"][0]), "nsa_cv_w2": f(inputs["nsa_cv_w2"][0]),
        "nsa_w_out": f(inputs["nsa_w_out"][0]), "dil_w_in": f(inputs["dil_w_in"][0]),
        "dil_w_out": f(inputs["dil_w_out"][0]), "final_g": f(inputs["final_g"]).reshape(1, D),
    }
    shared.update(consts)
    x = f(inputs["x"])
    c = f(inputs["c"])
    in_maps = []
    for i in range(ncores):
        m = dict(shared)
        m["x"] = np.ascontiguousarray(x[i * nseq:(i + 1) * nseq])
        cc = c[i * nseq:(i + 1) * nseq]
        m["ct"] = np.ascontiguousarray(cc.reshape(nseq, 8, 128).transpose(2, 1, 0))
        in_maps.append(m)
    res = run_bass_kernel_spmd(nc, in_maps, core_ids=list(range(ncores)))
    return np.concatenate([np.asarray(r["out"]) for r in res.results], axis=0)


def kernel(**inputs):
    return run(inputs).astype(np.float32)
```

```python
import numpy as np
import ml_dtypes
from contextlib import ExitStack
import concourse.bass as bass
import concourse.mybir as mybir
from concourse.bass_utils import run_bass_kernel_spmd

F32 = mybir.dt.float32
BF16 = mybir.dt.bfloat16
AF = mybir.ActivationFunctionType
ALU = mybir.AluOpType
AX = mybir.AxisListType
NPBF = ml_dtypes.bfloat16

T = 2048
D = 1024
NT = 16
NEG = -30000.0
EPS = 1e-6
NCORES = 8
DIL_G = [(0, 6), (6, 11), (11, 16)]
DIL_W = [128, 512, 2048]


class Res:
    __slots__ = ("name", "w", "rs")

    def __init__(self, name=""):
        self.name = name
        self.w = None
        self.rs = {}


class Sched:
    GEN = 30000

    def __init__(self, nc, es, n_dma=12):
        self.nc = nc
        self.es = es
        self.eng = {"pe": nc.tensor, "act": nc.scalar, "dve": nc.vector, "pool": nc.gpsimd, "sp": nc.sync}
        self.sem = {}
        self.cnt = {}
        self.cur = {}
        self.ngen = {}
        for e in ("pe", "act", "dve", "pool"):
            self.ngen[e] = 0
            self._new_sem(e)
        self.dkeys = []
        self.qkeys = {}
        self.rr = {}
        for q in ("sp", "pool"):
            self.qkeys[q] = []
            self.rr[q] = 0
            for i in range(n_dma):
                k = "dma_%s%d" % (q, i)
                self.sem[k] = es.enter_context(nc.semaphore(k))
                self.cnt[k] = 0
                self.dkeys.append(k)
                self.qkeys[q].append(k)
        self.seen = {e: {} for e in self.eng}
        self.nins = 0
        self.stopped = False

    def _new_sem(self, e):
        k = "%s#%d" % (e, self.ngen[e])
        self.ngen[e] += 1
        self.sem[k] = self.es.enter_context(self.nc.semaphore(k.replace("#", "_")))
        self.cnt[k] = 0
        self.cur[e] = k
        return k

    def _deps(self, reads, writes):
        deps = []
        for r in reads:
            if r.w is not None:
                deps.append(r.w)
        for r in writes:
            if r.w is not None:
                deps.append(r.w)
            deps.extend(r.rs.items())
        return deps

    def _waits(self, e, deps):
        need = {}
        for (k, v) in deps:
            if need.get(k, 0) < v:
                need[k] = v
        eo = self.eng[e]
        for k, v in need.items():
            if e == "pe" and k.startswith("pe#"):
                continue
            if self.seen[e].get(k, 0) >= v:
                continue
            eo.wait_ge(self.sem[k], v)
            self.seen[e][k] = v

    def _mark(self, tok, reads, writes):
        k, v = tok
        for r in reads:
            if r.rs.get(k, 0) < v:
                r.rs[k] = v
        for r in writes:
            r.w = tok
            r.rs = {}

    def op(self, e, fn, reads=(), writes=()):
        if self.stopped:
            return
        self._waits(e, self._deps(reads, writes))
        k = self.cur[e]
        if self.cnt[k] >= self.GEN:
            k = self._new_sem(e)
        ins = fn(self.eng[e])
        self.cnt[k] += 1
        ins.then_inc(self.sem[k], 1)
        self.nins += 1
        self._mark((k, self.cnt[k]), reads, writes)

    def dma(self, q, out, in_, reads=(), writes=()):
        if self.stopped:
            return
        deps = self._deps(reads, writes)
        k = self.qkeys[q][self.rr[q]]
        self.rr[q] = (self.rr[q] + 1) % len(self.qkeys[q])
        if self.cnt[k] > 0:
            deps.append((k, self.cnt[k]))
        self._waits(q, deps)
        ins = self.eng[q].dma_start(out=out, in_=in_)
        self.cnt[k] += 16
        ins.then_inc(self.sem[k], 16)
        self.nins += 1
        self._mark((k, self.cnt[k]), reads, writes)

    def barrier(self):
        if self.stopped:
            return
        toks = [(self.cur[e], self.cnt[self.cur[e]]) for e in ("pe", "act", "dve", "pool")]
        toks += [(k, self.cnt[k]) for k in self.dkeys if self.cnt[k] > 0]
        toks = [t for t in toks if t[1] > 0]
        for e in ("pe", "act", "dve", "pool", "sp"):
            need = {}
            for (k, v) in toks:
                need[k] = v
            for k, v in need.items():
                if self.seen[e].get(k, 0) >= v:
                    continue
                self.eng[e].wait_ge(self.sem[k], v)
                self.seen[e][k] = v

    def finish(self):
        sp = self.eng["sp"]
        for k in self.dkeys:
            if self.cnt[k] > 0 and self.seen["sp"].get(k, 0) < self.cnt[k]:
                sp.wait_ge(self.sem[k], self.cnt[k])
        for e in ("pe", "act", "dve", "pool"):
            k = self.cur[e]
            if self.cnt[k] > 0:
                sp.wait_ge(self.sem[k], self.cnt[k])


def make_consts():
    c = {}
    p = np.arange(128)
    c["c_ident"] = np.eye(128, dtype=np.float32).astype(NPBF)
    perm = np.zeros((128, 128), np.float32)
    for m in range(128):
        perm[(m // 64) * 64 + ((m % 64) + 32) % 64, m] = 1.0
    c["c_perm"] = perm.astype(NPBF)
    inv = (np.float32(10000.0) ** (-np.arange(32, dtype=np.float32) / np.float32(32))).astype(np.float32)
    t = np.arange(T, dtype=np.float32)
    ang = (t[None, :] * inv[p % 32][:, None]).astype(np.float32)
    c["c_cos"] = np.cos(ang).astype(np.float32)
    sgn = np.where((p % 64) < 32, -1.0, 1.0).astype(np.float32)
    c["c_sin"] = (np.sin(ang).astype(np.float32) * sgn[:, None]).astype(np.float32)
    k = p[:, None]
    q = p[None, :]
    c["c_causal"] = np.where(k <= q, 0.0, NEG).astype(NPBF)
    c["c_anti"] = np.where(k > q, 0.0, NEG).astype(NPBF)
    tt = np.arange(T)[None, :]
    cb = np.where((16 * k + 31 <= tt) & (k < 127), 0.0, NEG)
    c["c_cmpbias"] = cb.astype(NPBF)
    E = np.zeros((128, 16, 128), np.float32)
    for kt in range(16):
        for kk in range(128):
            E[2 * kt + kk // 64, kt, kk] = 1.0
    c["c_esel"] = E.astype(NPBF)
    vc = np.zeros((128, 33), np.float32)
    vc[:127, 0] = 1.0
    for n in range(127):
        for j in range(32):
            if 4 * j - 1 <= n <= 4 * j + 3:
                vc[n, 1 + j] = 1.0
    c["c_vcmp"] = vc.astype(NPBF)
    cand = np.zeros((128, 8, 32), np.float32)
    forced = np.zeros((128, 8, 32), np.float32)
    for ti in range(8, 16):
        for pp in range(128):
            cur = (128 * ti + pp) // 64
            for j in range(32):
                if 1 <= j <= cur - 2:
                    cand[pp, ti - 8, j] = 1.0
                if j == 0 or j == cur or j == cur - 1:
                    forced[pp, ti - 8, j] = 1.0
    c["c_cand"] = cand
    c["c_forced"] = forced
    d = q - k
    m0 = np.zeros((128, 3, 128), np.float32)
    m0[:, 0, :] = np.where((d >= 1) & (d <= 8), 0.0, NEG)
    m0[:, 1, :] = np.where((d >= 0) & (d <= 8), 0.0, NEG)
    m0[:, 2, :] = np.where((d >= 0) & (d <= 7), 0.0, NEG)
    c["c_m0"] = m0.astype(NPBF)
    m1 = np.zeros((128, 3, 128), np.float32)
    m1[:, 0, :] = np.where((d >= 1) & (d <= 32), 0.0, NEG)
    m1[:, 1, :] = np.where((d >= 0) & (d <= 32), 0.0, NEG)
    m1[:, 2, :] = np.where((d >= 0) & (d <= 31), 0.0, NEG)
    c["c_m1"] = m1.astype(NPBF)
    return c


CONST_SPECS = [
    ("c_ident", [128, 128], BF16), ("c_perm", [128, 128], BF16), ("c_cos", [128, T], F32),
    ("c_sin", [128, T], F32), ("c_causal", [128, 128], BF16), ("c_anti", [128, 128], BF16),
    ("c_cmpbias", [128, T], BF16), ("c_esel", [128, 16, 128], BF16), ("c_vcmp", [128, 33], BF16),
    ("c_cand", [128, 8, 32], F32), ("c_forced", [128, 8, 32], F32), ("c_m0", [128, 3, 128], BF16),
    ("c_m1", [128, 3, 128], BF16),
]


class _Stop(Exception):
    pass


def build_program(nseq=2, layers=(0, 1), stop=None):
    nc = bass.Bass("TRN2", target_bir_lowering=False)
    es = ExitStack()

    def din(name, shape, dt=F32):
        return nc.dram_tensor(name, list(shape), dt, kind="ExternalInput").ap()

    x_d = din("x", [nseq, T, D])
    ct_d = din("ct", [128, 8, nseq])
    normg_d = din("norm_g", [2, D])
    adaw_d = din("ada_w", [2, D, 3 * D])
    adab_d = din("ada_b", [2, 3 * D])
    win0_d = din("nsa_w_in", [D, 2864])
    pe2_d = [din("pe2k", [128, 16]), din("pe2v", [128, 16])]
    ckw1_d = din("nsa_ck_w1", [2048, 256])
    ckw2_d = din("nsa_ck_w2", [256, 64])
    cvw1_d = din("nsa_cv_w1", [2048, 256])
    cvw2_d = din("nsa_cv_w2", [256, 64])
    wout0_d = din("nsa_w_out", [D, D])
    win1_d = din("dil_w_in", [D, 4 * D])
    wout1_d = din("dil_w_out", [D, D])
    fing_d = din("final_g", [1, D])
    cdram = {nm: din(nm, shp, dt) for (nm, shp, dt) in CONST_SPECS}
    out_d = nc.dram_tensor("out", [nseq, T, D], F32, kind="ExternalOutput").ap()
    x1_d = nc.dram_tensor("x1s", [nseq, T, D], F32, kind="Internal").ap()
    mod_d = nc.dram_tensor("mods", [2, nseq, 3 * D], F32, kind="Internal").ap()

    S = Sched(nc, es)
    R_x1 = [Res("x1d%d" % s) for s in range(nseq)]
    R_mod = Res("modd")
    R_out = Res("outd")

    uid = [0]

    def sb(name, shape, dt, stack=None):
        uid[0] += 1
        return (stack or es).enter_context(nc.sbuf_tensor("s%d_%s" % (uid[0], name), list(shape), dt))

    def V(e, fn, reads=(), writes=()):
        S.op(e, fn, reads, writes)

    def mark(name):
        if stop is not None and name == stop:
            S.stopped = True

    R_const = Res("const")
    SPEC = {nm: (shp, dt) for (nm, shp, dt) in CONST_SPECS}

    def load_const(nm, stack):
        shp, dt = SPEC[nm]
        t_ = sb(nm, shp, dt, stack)
        S.dma("sp", t_[:], cdram[nm], writes=[R_const])
        return t_

    ident = load_const("c_ident", es)
    perm = load_const("c_perm", es)
    cos_t = load_const("c_cos", es)
    sin_t = load_const("c_sin", es)
    causal = load_const("c_causal", es)

    banks = []
    R_bank = []
    for i in range(8):
        banks.append(es.enter_context(nc.psum_tensor("bank%d" % i, [128, 512], F32)))
        R_bank.append(Res("bank%d" % i))
    P2SB = [0, 1, 2, 3]
    MB = [6, 7]
    SB = [0, 1, 2]
    OB = [3, 4]
    AB = [5, 6]
    CB = 7
    state = {"srot": 0, "prot": 0}

    NX = 3
    xt = [sb("xt%d" % i, [128, D], F32) for i in range(NX)]
    R_xt = [Res("xt%d" % i) for i in range(NX)]
    tmpf = sb("tmpf", [128, D], F32)
    R_tmpf = Res("tmpf")
    tmpf2 = sb("tmpf2", [128, D], F32)
    R_tmpf2 = Res("tmpf2")
    hb = sb("hb", [128, D], BF16)
    R_hb = Res("hb")
    hT = sb("hT", [128, 8, 128], BF16)
    R_hT = Res("hT")
    small = sb("small", [128, 64], F32)
    R_small = Res("small")
    modv = sb("modv", [128, 3 * D], F32)
    R_modv = Res("modv")
    qp = sb("qp", [128, 8, 128], BF16)
    R_qp = Res("qp")
    wreg = sb("wreg", [128, 8, 3 * D], BF16)
    R_wreg = [Res("wreg%d" % i) for i in range(3)]
    ws = {}
    NP = 4
    R_qr, R_zs, R_stage, R_sov, R_coef, R_gates, R_ozb, R_ozT, R_xo = (
        Res("qr"), Res("zs"), Res("stage"), Res("sov"), Res("coef"), Res("gates"), Res("ozb"), Res("ozT"), Res("xo"))
    R_pbuf = [Res("pb%d" % i) for i in range(NP)]

    R_qrp = [Res("qr0"), Res("qr1")]
    R_zsp = [Res("zs0"), Res("zs1")]
    R_tmpC, R_tmpC2, R_smallC, R_smallK = Res("tmpC"), Res("tmpC2"), Res("smallC"), Res("smallK")
    R_zscr = Res("zscr")

    def alloc_ws(stack, nbr, l0):
        ws["qr"] = [sb("qr%d" % i, [128, 16, 128], BF16, stack) for i in range(2)]
        for i in range(2):
            V("pool", lambda e, i=i: e.memset(ws["qr"][i][:], 0.0), [], [R_qrp[i]])
        ws["zs"] = [sb("zs%d" % i, [128, D], BF16, stack) for i in range(2)]
        ws["pbuf"] = [sb("pb%d" % i, [128, 512], BF16, stack) for i in range(NP)]
        ws["stage"] = sb("stage", [128, nbr, 16, 65], F32, stack)
        ws["coef"] = sb("coef", [128, nbr, 16], F32, stack)
        ws["ozb"] = sb("ozb", [128, D], BF16, stack)
        ws["ozT"] = sb("ozT", [128, 8, 128], BF16, stack)
        ws["xo"] = sb("xo", [128, D], F32, stack)
        ws["tmpC"] = sb("tmpC", [128, D], F32, stack)
        ws["smallC"] = sb("smallC", [128, 8], F32, stack)
        ws["zscr"] = sb("zscr", [128, 512], F32, stack)
        if l0:
            ws["tmpC2"] = sb("tmpC2", [128, D], F32, stack)

    bgq = []

    def bg_add(g_, front=False):
        if front:
            bgq.insert(0, [g_, 0])
        else:
            bgq.append([g_, 0])

    def bg_step(force=False):
        for ent in list(bgq):
            ent[1] -= 1
            if force or ent[1] <= 0:
                try:
                    d = next(ent[0])
                    ent[1] = d if isinstance(d, int) else 1
                except StopIteration:
                    bgq.remove(ent)

    def bg_drain():
        while bgq:
            bg_step(force=True)

    def bg_tick():
        bg_step()

    def A_gen(src_rows, R_src, j, wq, R_wq, wz, R_wz, cview, sview, l0=None, part=1):
        par = j % 2
        slot = j % NX
        X, RX = xt[slot], R_xt[slot]
        qrp, zsp = ws["qr"][par], ws["zs"][par]
        if part == 2:
            state["a2_active"] = True
            zscr = ws["zscr"]
            if l0 is not None:
                wg, R_wg = l0["wg"], l0["R_wg"]
                gt, R_gt = l0["gates"][par], l0["R_gates"][par]
                bk = AB[0]
                for kc in range(8):
                    V("pe", lambda e, kc=kc, bk=bk: e.matmul(banks[bk][:, 0:48], lhsT=hT[:, kc, :], rhs=wg[:, kc, :],
                                                             start=(kc == 0), stop=(kc == 7)), [R_wg, R_hT], [R_bank[bk]])
                yield 3
                V("act", lambda e, bk=bk: e.activation(out=gt[:], in_=banks[bk][:, 0:48], func=AF.Exp, scale=-1.0),
                  [R_bank[bk]], [R_gt])
                yield 2
            for h2 in range(2):
                bk2 = AB[h2]
                for kc in range(8):
                    V("pe", lambda e, kc=kc, h2=h2, bk2=bk2: e.matmul(
                        banks[bk2][:, :], lhsT=hT[:, kc, :], rhs=wz[:, kc, h2 * 512:(h2 + 1) * 512],
                        start=(kc == 0), stop=(kc == 7)), [R_wz, R_hT], [R_bank[bk2]])
            yield 8
            for h2 in range(2):
                bk2 = AB[h2]
                sl = slice(h2 * 512, (h2 + 1) * 512)
                V("act", lambda e, bk2=bk2: e.activation(out=zscr[:], in_=banks[bk2][:, :], func=AF.Exp, scale=-1.0),
                  [R_bank[bk2]], [R_zscr])
                yield 1
                V("act", lambda e: e.activation(out=zscr[:], in_=zscr[:], func=AF.Ln, bias=1.0), [R_zscr], [R_zscr])
                yield 1
                V("act", lambda e: e.activation(out=zscr[:], in_=zscr[:], func=AF.Exp, scale=-1.0), [R_zscr], [R_zscr])
                V("dve", lambda e, bk2=bk2, sl=sl: e.tensor_tensor(out=zsp[:, sl], in0=banks[bk2][:, :], in1=zscr[:],
                                                                   op=ALU.mult), [R_bank[bk2], R_zscr], [R_zsp[par]])
                if l0 is not None and h2 == 0:
                    V("dve", lambda e: e.tensor_scalar_add(out=gt[:], in0=gt[:], scalar1=1.0), [R_gt], [R_gt])
                    V("dve", lambda e: e.reciprocal(out=gt[:], in_=gt[:]), [R_gt], [R_gt])
                if h2 == 0:
                    yield 3
            state["a2_active"] = False
            return
        S.dma("sp", X[:], src_rows, reads=[R_src] if R_src is not None else [], writes=[RX])
        yield 5
        rstd_of(X[:], RX, 0, part=1)
        yield 4
        rstd_of(X[:], RX, 0, part=2)
        yield 2
        V("dve", lambda e: e.scalar_tensor_tensor(out=tmpf[:], in0=X[:], scalar=small[:, 1:2], in1=modv[:, D:2 * D],
                                                  op0=ALU.mult, op1=ALU.mult), [RX, R_small, R_modv], [R_tmpf])
        V("pool", lambda e: e.tensor_tensor(out=hb[:], in0=tmpf[:], in1=modv[:, 0:D], op=ALU.add),
          [R_tmpf, R_modv], [R_hb])
        yield 6
        while state.get("a2_active", False):
            yield 1
        bk = AB[0]
        pv = banks[bk][:].bitcast(BF16)
        for kc in range(8):
            V("pe", lambda e, kc=kc: e.transpose(pv[:, kc * 128:(kc + 1) * 128], hb[:, kc * 128:(kc + 1) * 128], ident[:]),
              [R_hb, R_const], [R_bank[bk]])
        yield 4
        V("dve", lambda e: e.tensor_copy(out=hT[:].rearrange("p a b -> p (a b)"), in_=pv), [R_bank[bk]], [R_hT])
        yield 2
        proj_fm(wq, R_wq, 4, AB, 0)
        yield 2
        proj_fm(wq, R_wq, 4, AB, 4)
        yield 4
        V("act", lambda e: e.copy(out=qp[:, 0:4, :].rearrange("p a b -> p (a b)"), in_=banks[AB[0]][:, :]),
          [R_bank[AB[0]]], [R_qp])
        yield 1
        V("act", lambda e: e.copy(out=qp[:, 4:8, :].rearrange("p a b -> p (a b)"), in_=banks[AB[1]][:, :]),
          [R_bank[AB[1]]], [R_qp])
        if l0 is not None:
            qpz_p, R_qpz_p = l0["qpz"][par], l0["R_qpz"][par]
            for b in range(2):
                for (p0, p1, dv) in qz_dst(qpz_p, 4 * b, 4):
                    V("pool", lambda e, b=b, p0=p0, p1=p1, dv=dv: e.tensor_copy(out=dv, in_=qp[p0:p1, 4 * b:4 * b + 4, :]),
                      [R_qp], [R_qpz_p])
        yield 4
        rope_n(qp[:, 0:4, :], R_qp, 4, qz_dst(qrp, 0, 4), R_qrp[par], cview, sview, AB[0], 0)
        yield 1
        rope_n(qp[:, 4:8, :], R_qp, 4, qz_dst(qrp, 4, 4), R_qrp[par], cview, sview, AB[1], 512)

    def C_gen(slot, wo, R_w, dst_rows, R_dst, fing, combine, cdelay=8):
        ozb, ozT, xo, tmpC, smallC = ws["ozb"], ws["ozT"], ws["xo"], ws["tmpC"], ws["smallC"]
        combine()
        yield cdelay
        bk = CB
        pv = banks[bk][:].bitcast(BF16)
        for kc in range(8):
            V("pe", lambda e, kc=kc: e.transpose(pv[:, kc * 128:(kc + 1) * 128], ozb[:, kc * 128:(kc + 1) * 128], ident[:]),
              [R_ozb, R_const], [R_bank[bk]])
        yield 4
        V("dve", lambda e: e.tensor_copy(out=ozT[:].rearrange("p a b -> p (a b)"), in_=pv), [R_bank[bk]], [R_ozT])
        yield 2
        for h2 in range(2):
            sl = slice(h2 * 512, (h2 + 1) * 512)
            for kc in range(8):
                V("pe", lambda e, kc=kc, h2=h2: e.matmul(
                    banks[bk][:, :], lhsT=ozT[:, kc, :], rhs=wo[:, kc, h2 * 512:(h2 + 1) * 512],
                    start=(kc == 0), stop=(kc == 7)), [R_w, R_ozT], [R_bank[bk]])
            yield 3
            V("dve", lambda e, sl=sl, h2=h2: e.tensor_tensor(
                out=tmpC[:, sl], in0=banks[bk][:, :], in1=modv[:, 2 * D + h2 * 512:2 * D + (h2 + 1) * 512], op=ALU.mult),
              [R_bank[bk], R_modv], [R_tmpC])
            V("pool", lambda e, sl=sl: e.tensor_tensor(out=xo[:, sl], in0=tmpC[:, sl], in1=xt[slot][:, sl], op=ALU.add),
              [R_tmpC, R_xt[slot]], [R_xo])
            yield 3
        if fing is not None:
            fg, R_fg = fing
            rstd_of(xo[:], R_xo, 0, scr=(tmpC, R_tmpC), small=smallC, R_small=R_smallC, part=1)
            yield 4
            rstd_of(xo[:], R_xo, 0, scr=(tmpC, R_tmpC), small=smallC, R_small=R_smallC, part=2)
            yield 2
            V("dve", lambda e: e.scalar_tensor_tensor(out=xo[:], in0=xo[:], scalar=smallC[:, 1:2], in1=fg[:],
                                                      op0=ALU.mult, op1=ALU.mult), [R_xo, R_smallC, R_fg], [R_xo])
            yield 2
        S.dma("sp", dst_rows, xo[:], reads=[R_xo], writes=[R_dst])

    def run_p4(n_tiles, mkA, mkB, mkC):
        for _ in mkA(0, 1):
            pass
        for j in range(n_tiles):
            bg_add(mkA(j, 2), front=True)
            if j + 1 < n_tiles:
                bg_add(mkA(j + 1, 1))
            mkB(j)
            flush()
            bg_drain()
            gC = mkC(j)
            d = next(gC)
            bgq.append([gC, d if isinstance(d, int) else 1])
        bg_drain()

    def ada_rows(layer):
        with ExitStack() as les:
            cts = sb("cts", [128, 8, nseq], F32, les)
            awb = [sb("awb%d" % i, [128, 3 * D], F32, les) for i in range(2)]
            R_awb = [Res("awb0"), Res("awb1")]
            rows = sb("rows", [nseq, 3 * D], F32, les)
            brow = sb("brow", [nseq, 3 * D], F32, les)
            R_cts, R_rows, R_brow = Res("cts"), Res("rows"), Res("brow")
            ctf = cts[:].rearrange("p a b -> p (a b)")
            S.dma("sp", cts[:], ct_d, writes=[R_cts])
            for s in range(nseq):
                S.dma("sp", brow[s:s + 1, :], adab_d[layer:layer + 1, :], writes=[R_brow])
            sm = small[:, 0:8 * nseq]
            V("act", lambda e: e.activation(out=sm, in_=ctf, func=AF.Exp, scale=-1.0), [R_cts], [R_small])
            V("dve", lambda e: e.tensor_scalar_add(out=sm, in0=sm, scalar1=1.0), [R_small], [R_small])
            V("dve", lambda e: e.reciprocal(out=sm, in_=sm), [R_small], [R_small])
            V("dve", lambda e: e.tensor_mul(out=ctf, in0=ctf, in1=sm), [R_small, R_cts], [R_cts])
            aw = adaw_d[layer].rearrange("(c p) n -> c p n", p=128)
            for kc in range(8):
                b = kc % 2
                S.dma("sp", awb[b][:], aw[kc], writes=[R_awb[b]])
                for nb in range(6):
                    V("pe", lambda e, kc=kc, nb=nb, b=b: e.matmul(
                        banks[nb][0:nseq, :], lhsT=cts[:, kc, :], rhs=awb[b][:, nb * 512:(nb + 1) * 512],
                        start=(kc == 0), stop=(kc == 7)), [R_cts, R_awb[b]], [R_bank[nb]])
            for nb in range(6):
                V("dve", lambda e, nb=nb: e.tensor_add(out=rows[:, nb * 512:(nb + 1) * 512], in0=banks[nb][0:nseq, :],
                                                       in1=brow[:, nb * 512:(nb + 1) * 512]),
                  [R_bank[nb], R_brow, R_rows], [R_rows])
            S.dma("sp", mod_d[layer], rows[:], reads=[R_rows], writes=[R_mod])
            S.barrier()

    def load_mod(layer, s):
        S.dma("sp", modv[:], mod_d[layer, s:s + 1, :].broadcast_to([128, 3 * D]), reads=[R_mod], writes=[R_modv])
        S.dma("sp", tmpf2[:], normg_d[layer:layer + 1, :].broadcast_to([128, D]), writes=[R_tmpf2])
        V("dve", lambda e: e.scalar_tensor_tensor(out=modv[:, D:2 * D], in0=modv[:, D:2 * D], scalar=1.0,
                                                  in1=tmpf2[:], op0=ALU.add, op1=ALU.mult),
          [R_modv, R_tmpf2], [R_modv])

    def load_x_tile(src_rows, R_src, slot):
        S.dma("sp", xt[slot][:], src_rows, reads=[R_src] if R_src is not None else [], writes=[R_xt[slot]])

    def rstd_of(X, RX, col, scr=None, small=small, R_small=R_small, part=0):
        scr_t, R_scr = scr if scr is not None else (tmpf2, R_tmpf2)
        if part in (0, 1):
            V("act", lambda e: e.activation(out=scr_t[:], in_=X, func=AF.Square), [RX], [R_scr])
            V("dve", lambda e: e.reduce_sum(out=small[:, col:col + 1], in_=scr_t[:], axis=AX.X), [R_scr], [R_small])
            V("dve", lambda e: e.tensor_scalar(out=small[:, col + 1:col + 2], in0=small[:, col:col + 1], scalar1=1.0 / D,
                                               scalar2=EPS, op0=ALU.mult, op1=ALU.add), [R_small], [R_small])
        if part in (0, 2):
            V("act", lambda e: e.activation(out=small[:, col + 1:col + 2], in_=small[:, col + 1:col + 2], func=AF.Ln),
              [R_small], [R_small])
            V("act", lambda e: e.activation(out=small[:, col + 1:col + 2], in_=small[:, col + 1:col + 2], func=AF.Exp,
                                            scale=-0.5), [R_small], [R_small])

    def norm_tile(slot):
        X = xt[slot]
        RX = R_xt[slot]
        rstd_of(X[:], RX, 0)
        V("dve", lambda e: e.scalar_tensor_tensor(out=tmpf[:], in0=X[:], scalar=small[:, 1:2], in1=modv[:, D:2 * D],
                                                  op0=ALU.mult, op1=ALU.mult), [RX, R_small, R_modv], [R_tmpf])
        V("pool", lambda e: e.tensor_tensor(out=hb[:], in0=tmpf[:], in1=modv[:, 0:D], op=ALU.add),
          [R_tmpf, R_modv], [R_hb])
        bk = MB[0]
        pv = banks[bk][:].bitcast(BF16)
        for kc in range(8):
            V("pe", lambda e, kc=kc: e.transpose(pv[:, kc * 128:(kc + 1) * 128], hb[:, kc * 128:(kc + 1) * 128], ident[:]),
              [R_hb, R_const], [R_bank[bk]])
        V("dve", lambda e: e.tensor_copy(out=hT[:].rearrange("p a b -> p (a b)"), in_=pv), [R_bank[bk]], [R_hT])

    def norm_stage(slot, par, P):
        X, RX = xt[slot], R_xt[slot]
        sq, R_sq = P["sq"]
        st, R_st = P["stt"]
        sm, R_sm = P["small"]
        hbp, R_hbp = P["hb"][par]
        rstd_of(X[:], RX, 2 * par, scr=(sq, R_sq), small=sm, R_small=R_sm)
        V("dve", lambda e: e.scalar_tensor_tensor(out=st[:], in0=X[:], scalar=sm[:, 2 * par + 1:2 * par + 2],
                                                  in1=modv[:, D:2 * D], op0=ALU.mult, op1=ALU.mult),
          [RX, R_sm, R_modv], [R_st])
        V("pool", lambda e: e.tensor_tensor(out=hbp[:], in0=st[:], in1=modv[:, 0:D], op=ALU.add),
          [R_st, R_modv], [R_hbp])

    def transpose_stage(par, P):
        hbp, R_hbp = P["hb"][par]
        hT, R_hT = P["hT"][par]
        bk = MB[0]
        pv = banks[bk][:].bitcast(BF16)
        for kc in range(8):
            V("pe", lambda e, kc=kc: e.transpose(pv[:, kc * 128:(kc + 1) * 128], hbp[:, kc * 128:(kc + 1) * 128], ident[:]),
              [R_hbp, R_const], [R_bank[bk]])
        V("dve", lambda e: e.tensor_copy(out=hT[:].rearrange("p a b -> p (a b)"), in_=pv), [R_bank[bk]], [R_hT])

    def proj_fm(wtile, R_w, ntiles, bk_list, j0=0, hTb=None):
        hT_, R_hT_ = hTb if hTb is not None else (hT, R_hT)
        for j in range(j0, j0 + ntiles):
            bk = bk_list[j // 4]
            for kc in range(8):
                V("pe", lambda e, j=j, kc=kc, bk=bk: e.matmul(
                    banks[bk][:, (j % 4) * 128:(j % 4 + 1) * 128], lhsT=wtile[:, kc, j * 128:(j + 1) * 128],
                    rhs=hT_[:, kc, :], start=(kc == 0), stop=(kc == 7)), [R_w, R_hT_], [R_bank[bk]])

    def rope_n(srcv, R_src, n, dstv, R_dst, cview, sview, bk, tcol):
        V("pe", lambda e: e.matmul(banks[bk][:, 0:n * 128], lhsT=perm[:], rhs=srcv, start=True, stop=True),
          [R_src, R_const], [R_bank[bk]])
        cb = cview.unsqueeze(1).broadcast_to([128, n, 128])
        sbv = sview.unsqueeze(1).broadcast_to([128, n, 128])
        t1 = tmpf[:, tcol:tcol + n * 128].rearrange("p (a b) -> p a b", b=128)
        t2 = tmpf2[:, tcol:tcol + n * 128].rearrange("p (a b) -> p a b", b=128)
        V("dve", lambda e: e.tensor_tensor(out=t1, in0=banks[bk][:, 0:n * 128].rearrange("p (a b) -> p a b", b=128),
                                           in1=sbv, op=ALU.mult), [R_bank[bk], R_const], [R_tmpf])
        V("pool", lambda e: e.tensor_tensor(out=t2, in0=srcv, in1=cb, op=ALU.mult), [R_src, R_const], [R_tmpf2])
        if isinstance(dstv, list):
            for (p0, p1, dv) in dstv:
                V("dve", lambda e, p0=p0, p1=p1, dv=dv: e.tensor_tensor(out=dv, in0=t1[p0:p1], in1=t2[p0:p1], op=ALU.add),
                  [R_tmpf, R_tmpf2], [R_dst])
        else:
            V("dve", lambda e: e.tensor_tensor(out=dstv, in0=t1, in1=t2, op=ALU.add), [R_tmpf, R_tmpf2], [R_dst])

    def qz_dst(qz, pair0, n):
        return [(0, 64, qz[0:64, 2 * pair0:2 * (pair0 + n):2, :]),
                (64, 128, qz[64:128, 2 * pair0 + 1:2 * (pair0 + n):2, :])]

    def evac_qp(bl=None):
        bl = bl or P2SB
        for b in range(2):
            V("act", lambda e, b=b: e.copy(out=qp[:, 4 * b:4 * b + 4, :].rearrange("p a b -> p (a b)"),
                                           in_=banks[bl[b]][:, :]), [R_bank[bl[b]]], [R_qp])

    def z_proj(wz, R_w):
        zs = ws["zs"]
        for h2 in range(2):
            bk = MB[h2]
            for kc in range(8):
                V("pe", lambda e, kc=kc, h2=h2, bk=bk: e.matmul(
                    banks[bk][:, :], lhsT=hT[:, kc, :], rhs=wz[:, kc, h2 * 512:(h2 + 1) * 512],
                    start=(kc == 0), stop=(kc == 7)), [R_w, R_hT], [R_bank[bk]])
        for h2 in range(2):
            bk = MB[h2]
            sl = slice(h2 * 512, (h2 + 1) * 512)
            V("act", lambda e, bk=bk, sl=sl: e.activation(out=tmpf[:, sl], in_=banks[bk][:, :], func=AF.Exp, scale=-1.0),
              [R_bank[bk]], [R_tmpf])
            V("dve", lambda e, sl=sl: e.tensor_scalar_add(out=tmpf[:, sl], in0=tmpf[:, sl], scalar1=1.0),
              [R_tmpf], [R_tmpf])
            V("dve", lambda e, sl=sl: e.reciprocal(out=tmpf[:, sl], in_=tmpf[:, sl]), [R_tmpf], [R_tmpf])
            V("dve", lambda e, bk=bk, sl=sl: e.tensor_tensor(out=zs[:, sl], in0=banks[bk][:, :], in1=tmpf[:, sl],
                                                             op=ALU.mult), [R_bank[bk], R_tmpf], [R_zs])

    pending = []
    LOOK = 2

    def _drain(limit):
        while pending and sum(1 for it in pending if it[0] == "pv") > limit:
            kind, fn = pending.pop(0)
            fn()
        while pending and pending[0][0] != "pv":
            kind, fn = pending.pop(0)
            fn()

    def flush():
        while pending:
            kind, fn = pending.pop(0)
            fn()

    def defer(fn):
        pending.append(("ev", fn))
        if not any(it[0] == "pv" for it in pending[:-1]):
            _drain(LOOK)

    def attn_unit(heads, kT_fn, q_src, v_ap, o_bank, o_cols, o_w, first, last, bias=None, scale=0.125,
                  reads_extra=(), kparts=128, shared_k=False):
        n = len(heads)
        sbk = SB[state["srot"] % len(SB)]
        state["srot"] += 1
        pb_i = state["prot"] % NP
        state["prot"] += 1
        pb = ws["pbuf"][pb_i]
        rd = [R_const] + list(reads_extra)
        if bias is not None:
            blhs, brhs = bias
            V("pe", lambda e: e.matmul(banks[sbk][0:kparts, 0:n * 128], lhsT=blhs, rhs=brhs, start=True, stop=False, skip_group_check=True),
              rd, [R_bank[sbk]])
        if shared_k:
            h0_, hstep = heads[0], (heads[1] - heads[0] if n > 1 else 1)
            V("pe", lambda e: e.matmul(banks[sbk][0:kparts, 0:n * 128], lhsT=kT_fn(0),
                                       rhs=q_src[:, h0_:h0_ + hstep * (n - 1) + 1:hstep, :],
                                       start=(bias is None), stop=True, skip_group_check=True), rd, [R_bank[sbk]])
        else:
            for i, h in enumerate(heads):
                V("pe", lambda e, i=i, h=h: e.matmul(
                    banks[sbk][0:kparts, i * 128:(i + 1) * 128], lhsT=kT_fn(i),
                    rhs=q_src[:, h, :], start=(bias is None), stop=True, skip_group_check=True),
                  rd, [R_bank[sbk]])
        V("act", lambda e: e.activation(out=pb[0:kparts, 0:n * 128], in_=banks[sbk][0:kparts, 0:n * 128],
                                        func=AF.Exp, scale=scale), [R_bank[sbk]], [R_pbuf[pb_i]])

        def part2():
            for i in range(n):
                V("pe", lambda e, i=i: e.matmul(
                    banks[o_bank][:, o_cols[i]:o_cols[i] + o_w], lhsT=pb[0:kparts, i * 128:(i + 1) * 128],
                    rhs=v_ap(i), start=(first and i == 0), stop=last, skip_group_check=True), [R_pbuf[pb_i]] + rd, [R_bank[o_bank]])
        pending.append(("pv", part2))
        _drain(LOOK)
        bg_tick()

    def out_proj_residual(wo, R_w, slot, dst_rows, R_dst, fing):
        ozb, ozT, xo = ws["ozb"], ws["ozT"], ws["xo"]
        bk = MB[0]
        pv = banks[bk][:].bitcast(BF16)
        for kc in range(8):
            V("pe", lambda e, kc=kc: e.transpose(pv[:, kc * 128:(kc + 1) * 128], ozb[:, kc * 128:(kc + 1) * 128], ident[:]),
              [R_ozb, R_const], [R_bank[bk]])
        V("dve", lambda e: e.tensor_copy(out=ozT[:].rearrange("p a b -> p (a b)"), in_=pv), [R_bank[bk]], [R_ozT])
        for h2 in range(2):
            bk2 = SB[h2]
            for kc in range(8):
                V("pe", lambda e, kc=kc, h2=h2, bk2=bk2: e.matmul(
                    banks[bk2][:, :], lhsT=ozT[:, kc, :], rhs=wo[:, kc, h2 * 512:(h2 + 1) * 512],
                    start=(kc == 0), stop=(kc == 7)), [R_w, R_ozT], [R_bank[bk2]])
        for h2 in range(2):
            bk2 = SB[h2]
            sl = slice(h2 * 512, (h2 + 1) * 512)
            V("dve", lambda e, bk2=bk2, sl=sl, h2=h2: e.tensor_tensor(
                out=tmpf[:, sl], in0=banks[bk2][:, :], in1=modv[:, 2 * D + h2 * 512:2 * D + (h2 + 1) * 512], op=ALU.mult),
              [R_bank[bk2], R_modv], [R_tmpf])
            V("pool", lambda e, sl=sl: e.tensor_tensor(out=xo[:, sl], in0=tmpf[:, sl], in1=xt[slot][:, sl], op=ALU.add),
              [R_tmpf, R_xt[slot]], [R_xo])
        if fing is not None:
            fg, R_fg = fing
            rstd_of(xo[:], R_xo, 2)
            V("dve", lambda e: e.scalar_tensor_tensor(out=xo[:], in0=xo[:], scalar=small[:, 3:4], in1=fg[:],
                                                      op0=ALU.mult, op1=ALU.mult), [R_xo, R_small, R_fg], [R_xo])
        S.dma("sp", dst_rows, xo[:], reads=[R_xo], writes=[R_dst])

    def load_w(dst, src2d, R_w, chunk=1024):
        ncols = src2d.shape[1]
        v = src2d.rearrange("(c p) n -> p c n", p=128)
        c0 = 0
        while c0 < ncols:
            c1 = min(ncols, c0 + chunk)
            S.dma("pool", dst[:, :, c0:c1], v[:, :, c0:c1], writes=[R_w])
            c0 = c1

    def layer0():
        with ExitStack() as les:
            def lsb(name, shape, dt):
                return sb(name, shape, dt, les)
            vcmp_c = load_const("c_vcmp", les)
            wq = wreg[:, :, 0:D]
            wz = wreg[:, :, D:2 * D]
            wo = wreg[:, :, 2 * D:3 * D]
            wg = lsb("wg", [128, 8, 48], BF16)
            w2k = lsb("w2k", [128, 2, 128], BF16)
            w2v = lsb("w2v", [128, 2, 64], BF16)
            pe2 = [lsb("pe2k_s", [128, 16], BF16), lsb("pe2v_s", [128, 16], BF16)]
            R_wq, R_wz, R_wo = R_wreg
            R_wkv, R_wv, R_wg, R_w1, R_w2, R_pe = Res("wkv"), Res("wv"), Res("wg"), Res("w1"), Res("w2"), Res("pe")
            ksT = lsb("ksT", [128, 2, T], BF16)
            kwT = lsb("kwT", [128, 2, T], BF16)
            vs = lsb("vs", [128, NT, 2, 65], BF16)
            vw = lsb("vw", [128, NT, 2, 65], BF16)
            kcmpT = lsb("kcmpT", [128, 2, 128], BF16)
            vcmp = lsb("vcmp", [128, 2, 97], BF16)
            R_kc2, R_ksT, R_kwT, R_vs, R_vw = Res("kc2"), Res("ksT"), Res("kwT"), Res("vs"), Res("vw")
            R_kcmp, R_vcmp, R_hid, R_hbias, R_kvp, R_negT, R_wkt = (Res("kcmp"), Res("vcmp"), Res("hid"),
                                                                    Res("hbias"), Res("kvp"), Res("negT"), Res("wkt"))
            W = win0_d
            offs = {"q": 0, "kc": 1024, "vc": 1152, "ks": 1280, "vs": 1408, "kw": 1536, "vw": 1664, "z": 1792, "g": 2816}
            Wv = W.rearrange("(c p) n -> p c n", p=128)
            for kv in range(2):
                S.dma("pool", pe2[kv][:], pe2_d[kv], writes=[R_pe])
            w2kd = ckw2_d.rearrange("(c p) n -> p c n", p=128)
            S.dma("pool", w2k[:, :, 0:64], w2kd, writes=[R_w2])
            S.dma("pool", w2k[:, :, 64:128], w2kd, writes=[R_w2])
            S.dma("pool", w2v[:, :, :], cvw2_d.rearrange("(c p) n -> p c n", p=128), writes=[R_w2])
            S.dma("pool", wg[:, :, :], Wv[:, :, offs["g"]:offs["g"] + 48], writes=[R_wg])
            V("pool", lambda e: e.memset(vs[:, :, :, 64:65], 1.0), [], [R_vs])
            V("pool", lambda e: e.memset(vw[:, :, :, 64:65], 1.0), [], [R_vw])
            for g in range(2):
                V("pool", lambda e, g=g: e.tensor_copy(out=vcmp[:, g, 64:97], in_=vcmp_c[:]), [R_const], [R_vcmp])

            mark("w0")
            for s in range(nseq):
                load_mod(0, s)
                mark("mod0")
                with ExitStack() as sa:
                    w1 = [sb("w1k", [128, 16, 256], BF16, sa), sb("w1v", [128, 16, 256], BF16, sa)]
                    wsrc = sb("wsrc", [128, 8, 768], BF16, sa)
                    S.dma("pool", wsrc[:, :, :], Wv[:, :, 1024:1792], writes=[R_wkv])
                    loc = {"kc": 0, "vc": 128, "ks": 256, "kw": 512}
                    dup_c0 = [loc[nm] + 64 * g for nm in ("kc", "vc", "ks", "kw") for g in range(2)]
                    wkv = sb("wkv", [128, 8, 8 * 128], BF16, sa)
                    R_wkv2 = Res("wkv2")
                    for j in range(8):
                        V("dve" if j % 2 == 0 else "act", lambda e, j=j: (e.tensor_copy if j % 2 == 0 else e.copy)(
                            out=wkv[:, :, j * 128:(j + 1) * 128].rearrange("p c (a b) -> p c a b", b=64),
                            in_=wsrc[:, :, dup_c0[j]:dup_c0[j] + 64].unsqueeze(2).broadcast_to([128, 8, 2, 64])),
                          [R_wkv], [R_wkv2])
                    kc2 = [sb("kc2", [128, 2, T], BF16, sa), sb("vc2", [128, 2, T], BF16, sa)]
                    hid = sb("hid", [128, 2, 128], BF16, sa)
                    hbias = sb("hbias", [128, 4], F32, sa)
                    kvp = sb("kvp", [128, 4, 128], BF16, sa)
                    for kv, w1d in enumerate((ckw1_d, cvw1_d)):
                        w1v_ = w1d.rearrange("(j p) n -> p j n", p=128)
                        for j0 in range(0, 16, 8):
                            S.dma("pool", w1[kv][:, j0:j0 + 8, :], w1v_[:, j0:j0 + 8, :], writes=[R_w1])
                    hbB = sb("hbB", [128, D], BF16, sa)
                    sqS = sb("sqS", [128, D], F32, sa)
                    sttS = sb("sttS", [128, D], F32, sa)
                    smallP = sb("smallP", [128, 8], F32, sa)
                    hTB = sb("hTB", [128, 8, 128], BF16, sa)
                    P2 = {"sq": (sqS, Res("sqS")), "stt": (sttS, Res("sttS")), "small": (smallP, Res("smallP")),
                          "hb": [(hb, R_hb), (hbB, Res("hbB"))], "hT": [(hT, R_hT), (hTB, Res("hTB"))]}
                    for t_ in range(2):
                        load_x_tile(x_d[s, t_ * 128:(t_ + 1) * 128, :], None, t_ % NX)
                        norm_stage(t_ % NX, t_ % 2, P2)
                    transpose_stage(0, P2)
                    for tt in range(NT):
                        slot = tt % NX
                        cs = slice(tt * 128, (tt + 1) * 128)
                        if tt + 2 < NT:
                            load_x_tile(x_d[s, (tt + 2) * 128:(tt + 3) * 128, :], None, (tt + 2) % NX)
                            norm_stage((tt + 2) % NX, (tt + 2) % 2, P2)
                        if tt + 1 < NT:
                            transpose_stage((tt + 1) % 2, P2)
                        hTc, R_hTc = P2["hT"][tt % 2]
                        proj_fm(wkv, R_wkv2, 8, [P2SB[0], P2SB[1]], hTb=(hTc, R_hTc))
                        bk = MB[1]
                        for kc in range(8):
                            V("pe", lambda e, kc=kc, bk=bk: e.matmul(
                                banks[bk][:, 0:256], lhsT=hTc[:, kc, :],
                                rhs=wsrc[:, kc, 384:768].rearrange("p (a b) -> p a b", b=128)[:, 0:3:2, :],
                                start=(kc == 0), stop=(kc == 7)), [R_wkv, R_hTc], [R_bank[bk]])
                        for j in range(4):
                            kv, g = j // 2, j % 2
                            src = banks[P2SB[0]][:, j * 128:(j + 1) * 128]
                            V("dve", lambda e, kv=kv, g=g, src=src, cs=cs: e.tensor_copy(out=kc2[kv][0:64, g, cs],
                                                                                         in_=src[0:64, :]),
                              [R_bank[P2SB[0]]], [R_kc2])
                            if tt == 0:
                                V("dve", lambda e, kv=kv, g=g, src=src: e.tensor_copy(out=kc2[kv][64:128, g, 0:127],
                                                                                      in_=src[64:128, 1:128]),
                                  [R_bank[P2SB[0]]], [R_kc2])
                            else:
                                V("dve", lambda e, kv=kv, g=g, src=src, tt=tt: e.tensor_copy(
                                    out=kc2[kv][64:128, g, tt * 128 - 1:(tt + 1) * 128 - 1], in_=src[64:128, :]),
                                  [R_bank[P2SB[0]]], [R_kc2])
                        V("act", lambda e: e.copy(out=kvp[:].rearrange("p a b -> p (a b)"), in_=banks[P2SB[1]][:, :]),
                          [R_bank[P2SB[1]]], [R_kvp])
                        rope_n(kvp[:, 0:2, :], R_kvp, 2, ksT[:, :, cs], R_ksT, cos_t[:, cs], sin_t[:, cs], P2SB[2], 0)
                        rope_n(kvp[:, 2:4, :], R_kvp, 2, kwT[:, :, cs], R_kwT, cos_t[:, cs], sin_t[:, cs], P2SB[3], 256)
                        bk = MB[1]
                        V("act", lambda e, bk=bk, tt=tt: e.copy(out=vs[:, tt, :, 0:64],
                                                                in_=banks[bk][:, 0:128].rearrange("p (g d) -> p g d", d=64)),
                          [R_bank[bk]], [R_vs])
                        V("act", lambda e, bk=bk, tt=tt: e.copy(out=vw[:, tt, :, 0:64],
                                                                in_=banks[bk][:, 128:256].rearrange("p (g d) -> p g d", d=64)),
                          [R_bank[bk]], [R_vw])
                        mark("p2t%d" % tt)
                    mark("p2")
                    for kv in range(2):
                        for hc in range(2):
                            bk = MB[1]
                            for j in range(16):
                                V("pe", lambda e, kv=kv, hc=hc, j=j, bk=bk: e.matmul(
                                    banks[bk][:, 0:1], lhsT=w1[kv][:, j, hc * 128:(hc + 1) * 128], rhs=pe2[kv][:, j:j + 1],
                                    start=(j == 0), stop=(j == 15)), [R_w1, R_pe], [R_bank[bk]])
                            V("dve", lambda e, kv=kv, hc=hc, bk=bk: e.tensor_copy(
                                out=hbias[:, kv * 2 + hc:kv * 2 + hc + 1], in_=banks[bk][:, 0:1]), [R_bank[bk]], [R_hbias])
                    for kv in range(2):
                        for g in range(2):
                            for hc in range(2):
                                bk = P2SB[(kv * 4 + g * 2 + hc) % 4]
                                for j in range(16):
                                    V("pe", lambda e, kv=kv, g=g, hc=hc, j=j, bk=bk: e.matmul(
                                        banks[bk][:, 0:127], lhsT=w1[kv][:, j, hc * 128:(hc + 1) * 128],
                                        rhs=kc2[kv][:, g, 2 * j:2 * j + 16 * 126 + 1:16], start=(j == 0), stop=(j == 15)),
                                      [R_w1, R_kc2], [R_bank[bk]])
                                bcol = hbias[:, kv * 2 + hc:kv * 2 + hc + 1]
                                V("dve", lambda e, bk=bk, bcol=bcol: e.tensor_scalar(
                                    out=tmpf[:, 0:127], in0=banks[bk][:, 0:127], scalar1=bcol, scalar2=None, op0=ALU.add),
                                  [R_bank[bk], R_hbias], [R_tmpf])
                                V("act", lambda e: e.activation(out=tmpf2[:, 0:127], in_=tmpf[:, 0:127], func=AF.Exp,
                                                                scale=-1.0), [R_tmpf], [R_tmpf2])
                                V("dve", lambda e: e.tensor_scalar_add(out=tmpf2[:, 0:127], in0=tmpf2[:, 0:127], scalar1=1.0),
                                  [R_tmpf2], [R_tmpf2])
                                V("dve", lambda e: e.reciprocal(out=tmpf2[:, 0:127], in_=tmpf2[:, 0:127]),
                                  [R_tmpf2], [R_tmpf2])
                                V("dve", lambda e, hc=hc: e.tensor_tensor(out=hid[:, hc, 0:127], in0=tmpf[:, 0:127],
                                                                          in1=tmpf2[:, 0:127], op=ALU.mult),
                                  [R_tmpf, R_tmpf2], [R_hid])
                            bk = MB[1]
                            if kv == 0:
                                for hc in range(2):
                                    V("pe", lambda e, hc=hc, bk=bk: e.matmul(
                                        banks[bk][:, 0:127], lhsT=w2k[:, hc, :], rhs=hid[:, hc, 0:127],
                                        start=(hc == 0), stop=(hc == 1)), [R_w2, R_hid], [R_bank[bk]])
                                V("dve", lambda e, g=g, bk=bk: e.tensor_copy(out=kcmpT[:, g, 0:127], in_=banks[bk][:, 0:127]),
                                  [R_bank[bk]], [R_kcmp])
                            else:
                                for hc in range(2):
                                    V("pe", lambda e, hc=hc, bk=bk: e.matmul(
                                        banks[bk][0:127, 0:64], lhsT=hid[:, hc, 0:127], rhs=w2v[:, hc, :],
                                        start=(hc == 0), stop=(hc == 1)), [R_w2, R_hid], [R_bank[bk]])
                                V("dve", lambda e, g=g, bk=bk: e.tensor_copy(out=vcmp[0:127, g, 0:64],
                                                                             in_=banks[bk][0:127, 0:64]),
                                  [R_bank[bk]], [R_vcmp])
                    mark("p3")
                    S.barrier()
                with ExitStack() as sB:
                    anti = load_const("c_anti", sB)
                    cmpbias = load_const("c_cmpbias", sB)
                    esel = load_const("c_esel", sB)
                    cand_t = load_const("c_cand", sB)
                    forced_t = load_const("c_forced", sB)
                    alloc_ws(sB, 3, False)
                    stage, coef, ozb, tmpC = ws["stage"], ws["coef"], ws["ozb"], ws["tmpC"]
                    sov = sb("sov", [128, 2, 8, 32], F32, sB)
                    gates2 = [sb("gates%d" % i, [128, 48], F32, sB) for i in range(2)]
                    R_gates2 = [Res("gates0"), Res("gates1")]
                    qpz2 = [sb("qpz%d" % i, [128, 16, 128], BF16, sB) for i in range(2)]
                    R_qpz2 = [Res("qpz0"), Res("qpz1")]
                    for i in range(2):
                        V("pool", lambda e, i=i: e.memset(qpz2[i][:], 0.0), [], [R_qpz2[i]])
                    negT = sb("negT", [128, 2, 4, 128], BF16, sB)
                    V("pool", lambda e: e.memset(negT[:], 0.0), [], [R_negT])
                    wk_t = sb("wk_t", [128, 16, 32], F32, sB)
                    tk = sb("tk", [128, 224], F32, sB)
                    hbK = sb("hbK", [128, 2, 32], BF16, sB)
                    smallK = sb("smallK", [128, 16], F32, sB)
                    R_tk, R_hbK = Res("tk"), Res("hbK")
                    l0info = {"qpz": qpz2, "R_qpz": R_qpz2, "wg": wg, "R_wg": R_wg, "gates": gates2, "R_gates": R_gates2}
                    dst_d, R_dst = (x1_d, R_x1[s]) if 1 in layers else (out_d, R_out)

                    def mkA(j, part, s=s):
                        cs = slice(j * 128, (j + 1) * 128)
                        return A_gen(x_d[s, cs, :], None, j, wq, R_wq, wz, R_wz, cos_t[:, cs], sin_t[:, cs], l0info, part)

                    def mkB(tt):
                        par = tt % 2
                        cs = slice(tt * 128, (tt + 1) * 128)
                        qr, R_qr_ = ws["qr"][par], R_qrp[par]
                        qpz, R_qpz = qpz2[par], R_qpz2[par]
                        n_units = 2 * (2 + 2 * (tt + 1) + 2 * (min(tt, 4) + 1))
                        state["bg_stride"] = max(1, n_units // 26)
                        def cmp_(g):
                            hl = [8 * g + i for i in range(8)]
                            for b in range(2):
                                hs = hl[b:8:2]
                                attn_unit(hs, lambda i, g=g: kcmpT[:, g, 0:127], qpz,
                                          lambda i, g=g: vcmp[0:127, g, :], OB[b], [0, 97, 194, 291], 97, True, True,
                                          bias=(ident[0:127, 0:127],
                                                cmpbias[0:127, cs].unsqueeze(1).broadcast_to([127, 4, 128])),
                                          reads_extra=[R_kcmp, R_vcmp, R_qpz], kparts=127, shared_k=True)
                                ov = banks[OB[b]][:, 0:388].rearrange("p (a b) -> p a b", b=97)
                                defer(lambda b=b, g=g, ov=ov: V("dve", lambda e: e.tensor_copy(
                                    out=stage[:, 0, 8 * g + b:8 * g + 8:2, :], in_=ov[:, :, 0:65]),
                                    [R_bank[OB[b]]], [R_stage]))
                                if tt >= 8:
                                    defer(lambda b=b, g=g, ov=ov: V("dve", lambda e: e.tensor_copy(
                                        out=sov[:, g, b:8:2, :], in_=ov[:, :, 65:97]), [R_bank[OB[b]]], [R_sov]))


                        def topk_all():
                            V("dve", lambda e: e.tensor_scalar_max(out=smallK[:, 0:16], in0=stage[:, 0, :, 64], scalar1=1e-30),
                              [R_stage], [R_smallK])
                            V("dve", lambda e: e.reciprocal(out=smallK[:, 0:16], in_=smallK[:, 0:16]), [R_smallK], [R_smallK])
                            V("dve", lambda e: e.tensor_tensor(
                                out=wk_t[:, :, :], in0=sov[:].rearrange("p g h j -> p (g h) j"),
                                in1=smallK[:, 0:16].unsqueeze(2).broadcast_to([128, 16, 32]), op=ALU.mult),
                              [R_sov, R_smallK], [R_wkt])
                            tkw = tk[:, 0:64].rearrange("p (g j) -> p g j", g=2)
                            V("dve", lambda e: e.tensor_reduce(out=tkw, in_=wk_t[:].rearrange("p (g h) j -> p g j h", g=2),
                                                               axis=AX.X, op=ALU.add), [R_wkt], [R_tk])
                            candb = cand_t[:, tt - 8, :].unsqueeze(1).broadcast_to([128, 2, 32])
                            forcb = forced_t[:, tt - 8, :].unsqueeze(1).broadcast_to([128, 2, 32])
                            V("dve", lambda e: e.scalar_tensor_tensor(out=tkw, in0=tkw, scalar=1.0, in1=candb,
                                                                      op0=ALU.add, op1=ALU.mult), [R_tk, R_const], [R_tk])
                            V("dve", lambda e: e.tensor_scalar_add(out=tk[:, 0:64], in0=tk[:, 0:64], scalar1=-1.0),
                              [R_tk], [R_tk])
                            for g in range(2):
                                wsl = tk[:, 32 * g:32 * g + 32]
                                m1 = tk[:, 64 + 8 * g:72 + 8 * g]
                                m2 = tk[:, 112 + 8 * g:120 + 8 * g]
                                rp = tk[:, 80:112] if g == 0 else tk[:, 192:224]
                                V("dve", lambda e, wsl=wsl, m1=m1: e.max(out=m1, in_=wsl), [R_tk], [R_tk])
                                V("dve", lambda e, wsl=wsl, m1=m1, rp=rp: e.match_replace(
                                    out=rp, in_to_replace=m1, in_values=wsl, imm_value=-2.0), [R_tk], [R_tk])
                                V("dve", lambda e, m2=m2, rp=rp: e.max(out=m2, in_=rp), [R_tk], [R_tk])
                                V("dve", lambda e, g=g, wsl=wsl, m2=m2: e.tensor_scalar(
                                    out=tk[:, 128 + 32 * g:160 + 32 * g], in0=wsl, scalar1=m2[:, 4:5], scalar2=None,
                                    op0=ALU.is_ge), [R_tk], [R_tk])
                            selw = tk[:, 128:192].rearrange("p (g j) -> p g j", g=2)
                            V("dve", lambda e: e.tensor_tensor(out=selw, in0=selw, in1=forcb, op=ALU.add),
                              [R_tk, R_const], [R_tk])
                            V("dve", lambda e: e.tensor_scalar(out=hbK[:], in0=selw, scalar1=-1.0, scalar2=-NEG,
                                                               op0=ALU.add, op1=ALU.mult), [R_tk], [R_hbK])
                        def topk_pe_(g):
                            bk = SB[state["srot"] % len(SB)]
                            state["srot"] += 1
                            pvn = banks[bk][:].bitcast(BF16)
                            V("pe", lambda e, pvn=pvn, g=g: e.transpose(pvn[0:32, 0:128], hbK[:, g, :], ident[:]),
                              [R_hbK, R_const], [R_bank[bk]])
                            V("dve", lambda e, pvn=pvn, g=g: e.tensor_copy(
                                out=negT[0:32, g, :, :], in_=pvn[0:32, 0:128].unsqueeze(1).broadcast_to([32, 4, 128])),
                              [R_bank[bk]], [R_negT])
                        def win_(g):
                            hl = [8 * g + i for i in range(8)]
                            for b in range(2):
                                hs = hl[b:8:2]
                                k0 = max(0, tt - 4)
                                for kt in range(k0, tt + 1):
                                    if kt == tt:
                                        bias = (ident[:], causal[:].unsqueeze(1).broadcast_to([128, 4, 128]))
                                    elif kt == tt - 4:
                                        bias = (ident[:], anti[:].unsqueeze(1).broadcast_to([128, 4, 128]))
                                    else:
                                        bias = None
                                    attn_unit(hs, lambda i, kt=kt, g=g: kwT[:, g, kt * 128:(kt + 1) * 128], qr,
                                              lambda i, kt=kt, g=g: vw[:, kt, g, :], OB[b], [0, 65, 130, 195], 65,
                                              kt == k0, kt == tt, bias=bias, reads_extra=[R_kwT, R_vw, R_qr_],
                                              shared_k=True)
                                defer(lambda b=b, g=g: V("dve", lambda e: e.tensor_copy(
                                    out=stage[:, 2, 8 * g + b:8 * g + 8:2, :],
                                    in_=banks[OB[b]][:, 0:260].rearrange("p (a b) -> p a b", b=65)),
                                    [R_bank[OB[b]]], [R_stage]))
                        def slc_(g):
                            hl = [8 * g + i for i in range(8)]
                            for b in range(2):
                                hs = hl[b:8:2]
                                for kt in range(tt + 1):
                                    if kt == tt:
                                        bias = (ident[:], causal[:].unsqueeze(1).broadcast_to([128, 4, 128]))
                                    elif tt >= 8:
                                        bias = (esel[:, kt, :], negT[:, g, :, :])
                                    else:
                                        bias = None
                                    attn_unit(hs, lambda i, kt=kt, g=g: ksT[:, g, kt * 128:(kt + 1) * 128], qr,
                                              lambda i, kt=kt, g=g: vs[:, kt, g, :], OB[b], [0, 65, 130, 195], 65,
                                              kt == 0, kt == tt, bias=bias, reads_extra=[R_ksT, R_vs, R_qr_, R_negT],
                                              shared_k=True)
                                defer(lambda b=b, g=g: V("dve", lambda e: e.tensor_copy(
                                    out=stage[:, 1, 8 * g + b:8 * g + 8:2, :],
                                    in_=banks[OB[b]][:, 0:260].rearrange("p (a b) -> p a b", b=65)),
                                    [R_bank[OB[b]]], [R_stage]))
                        win_(0)
                        cmp_(0)
                        cmp_(1)
                        if tt >= 8:
                            flush()
                            topk_all()
                        win_(1)
                        if tt >= 8:
                            topk_pe_(0)
                            topk_pe_(1)
                        slc_(0)
                        slc_(1)

                    def mkC(tt, s=s):
                        par = tt % 2
                        cs = slice(tt * 128, (tt + 1) * 128)
                        gt, R_gt = gates2[par], R_gates2[par]
                        zsp, R_z = ws["zs"][par], R_zsp[par]

                        def combine():
                            V("dve", lambda e: e.tensor_scalar_max(out=coef[:], in0=stage[:, :, :, 64], scalar1=1e-30),
                              [R_stage], [R_coef])
                            V("dve", lambda e: e.reciprocal(out=coef[:], in_=coef[:]), [R_coef], [R_coef])
                            V("dve", lambda e: e.tensor_tensor(out=coef[:], in0=coef[:],
                                                               in1=gt[:].rearrange("p (h b) -> p b h", b=3), op=ALU.mult),
                              [R_coef, R_gt], [R_coef])
                            o3 = tmpC[:].rearrange("p (h d) -> p h d", d=64)
                            o3b = tmpf2[:].rearrange("p (h d) -> p h d", d=64)
                            V("dve", lambda e: e.tensor_tensor(
                                out=o3, in0=stage[:, 0, :, 0:64],
                                in1=coef[:, 0, :].unsqueeze(2).broadcast_to([128, 16, 64]), op=ALU.mult),
                              [R_stage, R_coef], [R_tmpC])
                            for br in (1, 2):
                                V("pool", lambda e, br=br: e.tensor_tensor(
                                    out=o3b, in0=stage[:, br, :, 0:64],
                                    in1=coef[:, br, :].unsqueeze(2).broadcast_to([128, 16, 64]), op=ALU.mult),
                                  [R_stage, R_coef], [R_tmpf2])
                                V("dve", lambda e: e.tensor_tensor(out=o3, in0=o3, in1=o3b, op=ALU.add),
                                  [R_tmpC, R_tmpf2], [R_tmpC])
                            V("dve", lambda e: e.tensor_tensor(out=ozb[:], in0=tmpC[:], in1=zsp[:], op=ALU.mult),
                              [R_tmpC, R_z], [R_ozb])
                        return C_gen(tt % NX, wo, R_wo, dst_d[s, cs, :], R_dst, None, combine, cdelay=20)

                    run_p4(NT, mkA, mkB, mkC)
                    S.barrier()

    def layer1():
        with ExitStack() as les:
            m0_t = load_const("c_m0", les)
            m1_t = load_const("c_m1", les)
            fing = sb("fing", [128, D], F32, les)
            R_fing = Res("fing")
            S.dma("sp", fing[:], fing_d.broadcast_to([128, D]), writes=[R_fing])
            kT = sb("kT1", [128, 8, T], BF16, les)
            va = sb("va1", [128, NT, 16, 65], BF16, les)
            R_kT, R_va = Res("kT1"), Res("va1")
            V("pool", lambda e: e.memset(va[:, :, :, 64:65], 1.0), [], [R_va])
            alloc_ws(les, 1, False)
            stage, coef, ozb, tmpC = ws["stage"], ws["coef"], ws["ozb"], ws["tmpC"]
            R_wa, R_wb, R_wc = R_wreg
            W = win1_d
            for s in range(nseq):
                src = x1_d if 0 in layers else x_d
                R_src = R_x1[s] if 0 in layers else None
                srcv = src[s].rearrange("(p r) d -> r p d", r=16)
                dstv = out_d[s].rearrange("(p r) d -> r p d", r=16)
                load_mod(1, s)
                wk = wreg[:, :, 0:D]
                wv = wreg[:, :, D:2 * D]
                load_w(wk, W[:, D:2 * D], R_wa)
                load_w(wv, W[:, 2 * D:3 * D], R_wb)
                if s == 0:
                    load_w(wreg[:, :, 2 * D:3 * D], wout1_d, R_wc)
                P2 = {"sq": (ws["xo"], R_xo), "stt": (ws["tmpC"], R_tmpC), "small": (ws["smallC"], R_smallC),
                      "hb": [(hb, R_hb), (ws["ozb"], R_ozb)], "hT": [(hT, R_hT), (ws["ozT"], R_ozT)]}
                for t_ in range(2):
                    load_x_tile(srcv[t_], R_src, t_ % NX)
                    norm_stage(t_ % NX, t_ % 2, P2)
                transpose_stage(0, P2)
                for r in range(NT):
                    slot = r % NX
                    if r + 2 < NT:
                        load_x_tile(srcv[r + 2], R_src, (r + 2) % NX)
                        norm_stage((r + 2) % NX, (r + 2) % 2, P2)
                    if r + 1 < NT:
                        transpose_stage((r + 1) % 2, P2)
                    hTc, R_hTc = P2["hT"][r % 2]
                    cs = slice(r * 128, (r + 1) * 128)
                    cv = cos_t[:, r:r + 16 * 127 + 1:16]
                    sv = sin_t[:, r:r + 16 * 127 + 1:16]
                    proj_fm(wk, R_wa, 8, [P2SB[0], P2SB[1]], hTb=(hTc, R_hTc))
                    for h2 in range(2):
                        bk = 4 + h2
                        for kc in range(8):
                            V("pe", lambda e, kc=kc, h2=h2, bk=bk: e.matmul(
                                banks[bk][:, :], lhsT=hTc[:, kc, :], rhs=wv[:, kc, h2 * 512:(h2 + 1) * 512],
                                start=(kc == 0), stop=(kc == 7)), [R_wb, R_hTc], [R_bank[bk]])
                    evac_qp()
                    rope_n(qp[:, 0:4, :], R_qp, 4, kT[:, 0:4, cs], R_kT, cv, sv, P2SB[2], 0)
                    rope_n(qp[:, 4:8, :], R_qp, 4, kT[:, 4:8, cs], R_kT, cv, sv, P2SB[3], 512)
                    for h2 in range(2):
                        bk = 4 + h2
                        V("act", lambda e, h2=h2, bk=bk, r=r: e.copy(
                            out=va[:, r, 8 * h2:8 * h2 + 8, 0:64], in_=banks[bk][:, :].rearrange("p (h d) -> p h d", d=64)),
                          [R_bank[bk]], [R_va])
                wq = wreg[:, :, 0:D]
                wz = wreg[:, :, D:2 * D]
                wo = wreg[:, :, 2 * D:3 * D]
                load_w(wq, W[:, 0:D], R_wa)
                load_w(wz, W[:, 3 * D:4 * D], R_wb)

                def mkA(r, part, srcv=srcv, R_src=R_src):
                    return A_gen(srcv[r], R_src, r, wq, R_wa, wz, R_wb,
                                 cos_t[:, r:r + 16 * 127 + 1:16], sin_t[:, r:r + 16 * 127 + 1:16], None, part)

                def mkB(r):
                    par = r % 2
                    qr, R_qr_ = ws["qr"][par], R_qrp[par]
                    state["bg_stride"] = 2
                    for gi, (h0, h1) in enumerate(DIL_G):
                        ob = OB[gi % 2]
                        if gi == 0:
                            klist = [(rp, m0_t[:, (1 if r == rp else (2 if r > rp else 0)), :]) for rp in range(16)]
                        elif gi == 1:
                            klist = [(rp, m1_t[:, (1 if r == rp else (2 if r > rp else 0)), :])
                                     for rp in range(16) if (r - rp) % 4 == 0]
                        else:
                            klist = [(r, causal[:])]
                        chunks = [[h for h in range(h0, h1) if h % 2 == par_] for par_ in range(2)]
                        for ch in chunks:
                            n = len(ch)
                            for ki, (rp, mk) in enumerate(klist):
                                attn_unit(ch, lambda i, ch=ch, rp=rp: kT[:, ch[i] // 2, rp * 128:(rp + 1) * 128], qr,
                                          lambda i, ch=ch, rp=rp: va[:, rp, ch[i], :], ob, [(h - h0) * 65 for h in ch], 65,
                                          ki == 0, ki == len(klist) - 1,
                                          bias=(ident[:], mk.unsqueeze(1).broadcast_to([128, n, 128])),
                                          reads_extra=[R_kT, R_va, R_qr_])
                        nh = h1 - h0
                        defer(lambda ob=ob, h0=h0, nh=nh: V("dve", lambda e: e.tensor_copy(
                            out=stage[:, 0, h0:h0 + nh, :],
                            in_=banks[ob][:, 0:nh * 65].rearrange("p (a b) -> p a b", b=65)), [R_bank[ob]], [R_stage]))

                def mkC(r, dstv=dstv):
                    par = r % 2
                    zsp, R_z = ws["zs"][par], R_zsp[par]
                    smallC = ws["smallC"]

                    def combine():
                        for gi, (h0, h1) in enumerate(DIL_G):
                            V("dve", lambda e, gi=gi, h0=h0, h1=h1: e.tensor_reduce(
                                out=smallC[:, gi:gi + 1], in_=stage[:, 0, h0:h1, 64], axis=AX.X, op=ALU.add),
                              [R_stage], [R_smallC])
                            V("dve", lambda e, gi=gi, h0=h0, h1=h1: e.tensor_scalar_mul(
                                out=smallC[:, gi:gi + 1], in0=smallC[:, gi:gi + 1], scalar1=1.0 / (h1 - h0)),
                              [R_smallC], [R_smallC])
                        V("dve", lambda e: e.tensor_reduce(out=smallC[:, 3:4], in_=smallC[:, 0:3], axis=AX.X, op=ALU.add),
                          [R_smallC], [R_smallC])
                        V("dve", lambda e: e.reciprocal(out=smallC[:, 3:4], in_=smallC[:, 3:4]), [R_smallC], [R_smallC])
                        V("dve", lambda e: e.tensor_scalar(out=smallC[:, 4:7], in0=smallC[:, 0:3], scalar1=smallC[:, 3:4],
                                                           scalar2=3.0, op0=ALU.mult, op1=ALU.mult), [R_smallC], [R_smallC])
                        V("dve", lambda e: e.reciprocal(out=coef[:, 0, :], in_=stage[:, 0, :, 64]), [R_stage], [R_coef])
                        for gi, (h0, h1) in enumerate(DIL_G):
                            V("dve", lambda e, gi=gi, h0=h0, h1=h1: e.tensor_scalar(
                                out=coef[:, 0, h0:h1], in0=coef[:, 0, h0:h1], scalar1=smallC[:, 4 + gi:5 + gi], scalar2=None,
                                op0=ALU.mult), [R_smallC, R_coef], [R_coef])
                        o3 = tmpC[:].rearrange("p (h d) -> p h d", d=64)
                        V("dve", lambda e: e.tensor_tensor(out=o3, in0=stage[:, 0, :, 0:64],
                                                           in1=coef[:, 0, :].unsqueeze(2).broadcast_to([128, 16, 64]),
                                                           op=ALU.mult), [R_stage, R_coef], [R_tmpC])
                        V("dve", lambda e: e.tensor_tensor(out=ozb[:], in0=tmpC[:], in1=zsp[:], op=ALU.mult),
                          [R_tmpC, R_z], [R_ozb])
                    return C_gen(r % NX, wo, R_wc, dstv[r], R_out, (fing, R_fing), combine)

                run_p4(NT, mkA, mkB, mkC)
            S.barrier()

    try:
        mark("consts")
        if 0 in layers:
            load_w(wreg[:, :, 0:D], win0_d[:, 0:D], R_wreg[0])
            load_w(wreg[:, :, D:2 * D], win0_d[:, 1792:1792 + D], R_wreg[1])
            load_w(wreg[:, :, 2 * D:3 * D], wout0_d, R_wreg[2])
        for l_ in layers:
            ada_rows(l_)
        mark("ada0")
        if 0 in layers:
            layer0()
        if 1 in layers:
            layer1()
    except _Stop:
        pass
    S.finish()
    es.close()
    return nc, S


_CACHE = {}


def _get_program(nseq, layers, stop=None):
    key = (nseq, tuple(layers), stop)
    if key not in _CACHE:
        _CACHE[key] = build_program(nseq, layers, stop)
    return _CACHE[key]


def run(inputs, ncores=NCORES, nseq=2, layers=(0, 1), stop=None):
    nc, S = _get_program(nseq, layers, stop)
    consts = make_consts()
    f = lambda a: np.ascontiguousarray(np.asarray(a, dtype=np.float32))

    def pe2(pe):
        return np.ascontiguousarray(f(pe).reshape(16, 2, 64).transpose(1, 2, 0).reshape(128, 16))

    shared = {
        "norm_g": f(inputs["norm_g"]), "ada_w": f(inputs["ada_w"]), "ada_b": f(inputs["ada_b"]),
        "nsa_w_in": f(inputs["nsa_w_in"][0]), "pe2k": pe2(inputs["nsa_pe_k"][0]), "pe2v": pe2(inputs["nsa_pe_v"][0]),
        "nsa_ck_w1": f(inputs["nsa_ck_w1"][0]), "nsa_ck_w2": f(inputs["nsa_ck_w2"][0]),
        "nsa_cv_w1": f(inputs["nsa_cv_w1"][0]), "nsa_cv_w2": f(inputs["nsa_cv_w2"][0]),
        "nsa_w_out": f(inputs["nsa_w_out"][0]), "dil_w_in": f(inputs["dil_w_in"][0]),
        "dil_w_out": f(inputs["dil_w_out"][0]), "final_g": f(inputs["final_g"]).reshape(1, D),
    }
    shared.update(consts)
    x = f(inputs["x"])
    c = f(inputs["c"])
    in_maps = []
    for i in range(ncores):
        m = dict(shared)
        m["x"] = np.ascontiguousarray(x[i * nseq:(i + 1) * nseq])
        cc = c[i * nseq:(i + 1) * nseq]
        m["ct"] = np.ascontiguousarray(cc.reshape(nseq, 8, 128).transpose(2, 1, 0))
        in_maps.append(m)
    res = run_bass_kernel_spmd(nc, in_maps, core_ids=list(range(ncores)))
    return np.concatenate([np.asarray(r["out"]) for r in res.results], axis=0)


def kernel(**inputs):
    return run(inputs).astype(np.float32)
```

```python
import numpy as np
import ml_dtypes
from contextlib import ExitStack
import concourse.bass as bass
import concourse.mybir as mybir
from concourse.bass_utils import run_bass_kernel_spmd

F32 = mybir.dt.float32
BF16 = mybir.dt.bfloat16
AF = mybir.ActivationFunctionType
ALU = mybir.AluOpType
AX = mybir.AxisListType
NPBF = ml_dtypes.bfloat16

T = 2048
D = 1024
NT = 16
NEG = -30000.0
EPS = 1e-6
NCORES = 8
DIL_G = [(0, 6), (6, 11), (11, 16)]
DIL_W = [128, 512, 2048]


class Res:
    __slots__ = ("name", "w", "rs")

    def __init__(self, name=""):
        self.name = name
        self.w = None
        self.rs = {}


class Sched:
    GEN = 30000

    def __init__(self, nc, es, n_dma=12):
        self.nc = nc
        self.es = es
        self.eng = {"pe": nc.tensor, "act": nc.scalar, "dve": nc.vector, "pool": nc.gpsimd, "sp": nc.sync}
        self.sem = {}
        self.cnt = {}
        self.cur = {}
        self.ngen = {}
        for e in ("pe", "act", "dve", "pool"):
            self.ngen[e] = 0
            self._new_sem(e)
        self.dkeys = []
        self.qkeys = {}
        self.rr = {}
        for q in ("sp", "pool"):
            self.qkeys[q] = []
            self.rr[q] = 0
            for i in range(n_dma):
                k = "dma_%s%d" % (q, i)
                self.sem[k] = es.enter_context(nc.semaphore(k))
                self.cnt[k] = 0
                self.dkeys.append(k)
                self.qkeys[q].append(k)
        self.seen = {e: {} for e in self.eng}
        self.nins = 0
        self.stopped = False

    def _new_sem(self, e):
        k = "%s#%d" % (e, self.ngen[e])
        self.ngen[e] += 1
        self.sem[k] = self.es.enter_context(self.nc.semaphore(k.replace("#", "_")))
        self.cnt[k] = 0
        self.cur[e] = k
        return k

    def _deps(self, reads, writes):
        deps = []
        for r in reads:
            if r.w is not None:
                deps.append(r.w)
        for r in writes:
            if r.w is not None:
                deps.append(r.w)
            deps.extend(r.rs.items())
        return deps

    def _waits(self, e, deps):
        need = {}
        for (k, v) in deps:
            if need.get(k, 0) < v:
                need[k] = v
        eo = self.eng[e]
        for k, v in need.items():
            if e == "pe" and k.startswith("pe#"):
                continue
            if self.seen[e].get(k, 0) >= v:
                continue
            eo.wait_ge(self.sem[k], v)
            self.seen[e][k] = v

    def _mark(self, tok, reads, writes):
        k, v = tok
        for r in reads:
            if r.rs.get(k, 0) < v:
                r.rs[k] = v
        for r in writes:
            r.w = tok
            r.rs = {}

    def op(self, e, fn, reads=(), writes=()):
        if self.stopped:
            return
        self._waits(e, self._deps(reads, writes))
        k = self.cur[e]
        if self.cnt[k] >= self.GEN:
            k = self._new_sem(e)
        ins = fn(self.eng[e])
        self.cnt[k] += 1
        ins.then_inc(self.sem[k], 1)
        self.nins += 1
        self._mark((k, self.cnt[k]), reads, writes)

    def dma(self, q, out, in_, reads=(), writes=()):
        if self.stopped:
            return
        deps = self._deps(reads, writes)
        k = self.qkeys[q][self.rr[q]]
        self.rr[q] = (self.rr[q] + 1) % len(self.qkeys[q])
        if self.cnt[k] > 0:
            deps.append((k, self.cnt[k]))
        self._waits(q, deps)
        ins = self.eng[q].dma_start(out=out, in_=in_)
        self.cnt[k] += 16
        ins.then_inc(self.sem[k], 16)
        self.nins += 1
        self._mark((k, self.cnt[k]), reads, writes)

    def barrier(self):
        if self.stopped:
            return
        toks = [(self.cur[e], self.cnt[self.cur[e]]) for e in ("pe", "act", "dve", "pool")]
        toks += [(k, self.cnt[k]) for k in self.dkeys if self.cnt[k] > 0]
        toks = [t for t in toks if t[1] > 0]
        for e in ("pe", "act", "dve", "pool", "sp"):
            need = {}
            for (k, v) in toks:
                need[k] = v
            for k, v in need.items():
                if self.seen[e].get(k, 0) >= v:
                    continue
                self.eng[e].wait_ge(self.sem[k], v)
                self.seen[e][k] = v

    def finish(self):
        sp = self.eng["sp"]
        for k in self.dkeys:
            if self.cnt[k] > 0 and self.seen["sp"].get(k, 0) < self.cnt[k]:
                sp.wait_ge(self.sem[k], self.cnt[k])
        for e in ("pe", "act", "dve", "pool"):
            k = self.cur[e]
            if self.cnt[k] > 0:
                sp.wait_ge(self.sem[k], self.cnt[k])


def make_consts():
    c = {}
    p = np.arange(128)
    c["c_ident"] = np.eye(128, dtype=np.float32).astype(NPBF)
    perm = np.zeros((128, 128), np.float32)
    for m in range(128):
        perm[(m // 64) * 64 + ((m % 64) + 32) % 64, m] = 1.0
    c["c_perm"] = perm.astype(NPBF)
    inv = (np.float32(10000.0) ** (-np.arange(32, dtype=np.float32) / np.float32(32))).astype(np.float32)
    t = np.arange(T, dtype=np.float32)
    ang = (t[None, :] * inv[p % 32][:, None]).astype(np.float32)
    c["c_cos"] = np.cos(ang).astype(np.float32)
    sgn = np.where((p % 64) < 32, -1.0, 1.0).astype(np.float32)
    c["c_sin"] = (np.sin(ang).astype(np.float32) * sgn[:, None]).astype(np.float32)
    k = p[:, None]
    q = p[None, :]
    c["c_causal"] = np.where(k <= q, 0.0, NEG).astype(NPBF)
    c["c_anti"] = np.where(k > q, 0.0, NEG).astype(NPBF)
    tt = np.arange(T)[None, :]
    cb = np.where((16 * k + 31 <= tt) & (k < 127), 0.0, NEG)
    c["c_cmpbias"] = cb.astype(NPBF)
    E = np.zeros((128, 16, 128), np.float32)
    for kt in range(16):
        for kk in range(128):
            E[2 * kt + kk // 64, kt, kk] = 1.0
    c["c_esel"] = E.astype(NPBF)
    vc = np.zeros((128, 33), np.float32)
    vc[:127, 0] = 1.0
    for n in range(127):
        for j in range(32):
            if 4 * j - 1 <= n <= 4 * j + 3:
                vc[n, 1 + j] = 1.0
    c["c_vcmp"] = vc.astype(NPBF)
    cand = np.zeros((128, 8, 32), np.float32)
    forced = np.zeros((128, 8, 32), np.float32)
    for ti in range(8, 16):
        for pp in range(128):
            cur = (128 * ti + pp) // 64
            for j in range(32):
                if 1 <= j <= cur - 2:
                    cand[pp, ti - 8, j] = 1.0
                if j == 0 or j == cur or j == cur - 1:
                    forced[pp, ti - 8, j] = 1.0
    c["c_cand"] = cand
    c["c_forced"] = forced
    d = q - k
    m0 = np.zeros((128, 3, 128), np.float32)
    m0[:, 0, :] = np.where((d >= 1) & (d <= 8), 0.0, NEG)
    m0[:, 1, :] = np.where((d >= 0) & (d <= 8), 0.0, NEG)
    m0[:, 2, :] = np.where((d >= 0) & (d <= 7), 0.0, NEG)
    c["c_m0"] = m0.astype(NPBF)
    m1 = np.zeros((128, 3, 128), np.float32)
    m1[:, 0, :] = np.where((d >= 1) & (d <= 32), 0.0, NEG)
    m1[:, 1, :] = np.where((d >= 0) & (d <= 32), 0.0, NEG)
    m1[:, 2, :] = np.where((d >= 0) & (d <= 31), 0.0, NEG)
    c["c_m1"] = m1.astype(NPBF)
    return c


CONST_SPECS = [
    ("c_ident", [128, 128], BF16), ("c_perm", [128, 128], BF16), ("c_cos", [128, T], F32),
    ("c_sin", [128, T], F32), ("c_causal", [128, 128], BF16), ("c_anti", [128, 128], BF16),
    ("c_cmpbias", [128, T], BF16), ("c_esel", [128, 16, 128], BF16), ("c_vcmp", [128, 33], BF16),
    ("c_cand", [128, 8, 32], F32), ("c_forced", [128, 8, 32], F32), ("c_m0", [128, 3, 128], BF16),
    ("c_m1", [128, 3, 128], BF16),
]


class _Stop(Exception):
    pass


def build_program(nseq=2, layers=(0, 1), stop=None):
    nc = bass.Bass("TRN2", target_bir_lowering=False)
    es = ExitStack()

    def din(name, shape, dt=F32):
        return nc.dram_tensor(name, list(shape), dt, kind="ExternalInput").ap()

    x_d = din("x", [nseq, T, D])
    ct_d = din("ct", [128, 8, nseq])
    normg_d = din("norm_g", [2, D])
    adaw_d = din("ada_w", [2, D, 3 * D])
    adab_d = din("ada_b", [2, 3 * D])
    win0_d = din("nsa_w_in", [D, 2864])
    pe2_d = [din("pe2k", [128, 16]), din("pe2v", [128, 16])]
    ckw1_d = din("nsa_ck_w1", [2048, 256])
    ckw2_d = din("nsa_ck_w2", [256, 64])
    cvw1_d = din("nsa_cv_w1", [2048, 256])
    cvw2_d = din("nsa_cv_w2", [256, 64])
    wout0_d = din("nsa_w_out", [D, D])
    win1_d = din("dil_w_in", [D, 4 * D])
    wout1_d = din("dil_w_out", [D, D])
    fing_d = din("final_g", [1, D])
    cdram = {nm: din(nm, shp, dt) for (nm, shp, dt) in CONST_SPECS}
    out_d = nc.dram_tensor("out", [nseq, T, D], F32, kind="ExternalOutput").ap()
    x1_d = nc.dram_tensor("x1s", [nseq, T, D], F32, kind="Internal").ap()
    mod_d = nc.dram_tensor("mods", [2, nseq, 3 * D], F32, kind="Internal").ap()

    S = Sched(nc, es)
    R_x1 = [Res("x1d%d" % s) for s in range(nseq)]
    R_mod = Res("modd")
    R_out = Res("outd")

    uid = [0]

    def sb(name, shape, dt, stack=None):
        uid[0] += 1
        return (stack or es).enter_context(nc.sbuf_tensor("s%d_%s" % (uid[0], name), list(shape), dt))

    def V(e, fn, reads=(), writes=()):
        S.op(e, fn, reads, writes)

    def mark(name):
        if stop is not None and name == stop:
            S.stopped = True

    R_const = Res("const")
    SPEC = {nm: (shp, dt) for (nm, shp, dt) in CONST_SPECS}

    def load_const(nm, stack):
        shp, dt = SPEC[nm]
        t_ = sb(nm, shp, dt, stack)
        S.dma("sp", t_[:], cdram[nm], writes=[R_const])
        return t_

    ident = load_const("c_ident", es)
    perm = load_const("c_perm", es)
    cos_t = load_const("c_cos", es)
    sin_t = load_const("c_sin", es)
    causal = load_const("c_causal", es)

    banks = []
    R_bank = []
    for i in range(8):
        banks.append(es.enter_context(nc.psum_tensor("bank%d" % i, [128, 512], F32)))
        R_bank.append(Res("bank%d" % i))
    P2SB = [0, 1, 2, 3]
    MB = [6, 7]
    SB = [0, 1, 2]
    OB = [3, 4]
    AB = [5, 6]
    CB = 7
    state = {"srot": 0, "prot": 0}

    NX = 3
    xt = [sb("xt%d" % i, [128, D], F32) for i in range(NX)]
    R_xt = [Res("xt%d" % i) for i in range(NX)]
    tmpf = sb("tmpf", [128, D], F32)
    R_tmpf = Res("tmpf")
    tmpf2 = sb("tmpf2", [128, D], F32)
    R_tmpf2 = Res("tmpf2")
    hb = sb("hb", [128, D], BF16)
    R_hb = Res("hb")
    hT = sb("hT", [128, 8, 128], BF16)
    R_hT = Res("hT")
    small = sb("small", [128, 64], F32)
    R_small = Res("small")
    modv = sb("modv", [128, 3 * D], F32)
    R_modv = Res("modv")
    qp = sb("qp", [128, 8, 128], BF16)
    R_qp = Res("qp")
    wreg = sb("wreg", [128, 8, 3 * D], BF16)
    R_wreg = [Res("wreg%d" % i) for i in range(3)]
    ws = {}
    NP = 4
    R_qr, R_zs, R_stage, R_sov, R_coef, R_gates, R_ozb, R_ozT, R_xo = (
        Res("qr"), Res("zs"), Res("stage"), Res("sov"), Res("coef"), Res("gates"), Res("ozb"), Res("ozT"), Res("xo"))
    R_pbuf = [Res("pb%d" % i) for i in range(NP)]

    R_qrp = [Res("qr0"), Res("qr1")]
    R_zsp = [Res("zs0"), Res("zs1")]
    R_tmpC, R_tmpC2, R_smallC, R_smallK = Res("tmpC"), Res("tmpC2"), Res("smallC"), Res("smallK")
    R_zscr = Res("zscr")

    def alloc_ws(stack, nbr, l0):
        ws["qr"] = [sb("qr%d" % i, [128, 16, 128], BF16, stack) for i in range(2)]
        for i in range(2):
            V("pool", lambda e, i=i: e.memset(ws["qr"][i][:], 0.0), [], [R_qrp[i]])
        ws["zs"] = [sb("zs%d" % i, [128, D], BF16, stack) for i in range(2)]
        ws["pbuf"] = [sb("pb%d" % i, [128, 512], BF16, stack) for i in range(NP)]
        ws["stage"] = sb("stage", [128, nbr, 16, 65], F32, stack)
        ws["coef"] = sb("coef", [128, nbr, 16], F32, stack)
        ws["ozb"] = sb("ozb", [128, D], BF16, stack)
        ws["ozT"] = sb("ozT", [128, 8, 128], BF16, stack)
        ws["xo"] = sb("xo", [128, D], F32, stack)
        ws["tmpC"] = sb("tmpC", [128, D], F32, stack)
        ws["smallC"] = sb("smallC", [128, 8], F32, stack)
        ws["zscr"] = sb("zscr", [128, 512], F32, stack)
        if l0:
            ws["tmpC2"] = sb("tmpC2", [128, D], F32, stack)

    bgq = []

    def bg_add(g_, front=False):
        if front:
            bgq.insert(0, [g_, 0])
        else:
            bgq.append([g_, 0])

    def bg_step(force=False):
        for ent in list(bgq):
            ent[1] -= 1
            if force or ent[1] <= 0:
                try:
                    d = next(ent[0])
                    ent[1] = d if isinstance(d, int) else 1
                except StopIteration:
                    bgq.remove(ent)

    def bg_drain():
        while bgq:
            bg_step(force=True)

    def bg_tick():
        bg_step()

    def A_gen(src_rows, R_src, j, wq, R_wq, wz, R_wz, cview, sview, l0=None, part=1):
        par = j % 2
        slot = j % NX
        X, RX = xt[slot], R_xt[slot]
        qrp, zsp = ws["qr"][par], ws["zs"][par]
        if part == 2:
            state["a2_active"] = True
            zscr = ws["zscr"]
            if l0 is not None:
                wg, R_wg = l0["wg"], l0["R_wg"]
                gt, R_gt = l0["gates"][par], l0["R_gates"][par]
                bk = AB[0]
                for kc in range(8):
                    V("pe", lambda e, kc=kc, bk=bk: e.matmul(banks[bk][:, 0:48], lhsT=hT[:, kc, :], rhs=wg[:, kc, :],
                                                             start=(kc == 0), stop=(kc == 7)), [R_wg, R_hT], [R_bank[bk]])
                yield 3
                V("act", lambda e, bk=bk: e.activation(out=gt[:], in_=banks[bk][:, 0:48], func=AF.Exp, scale=-1.0),
                  [R_bank[bk]], [R_gt])
                yield 2
            for h2 in range(2):
                bk2 = AB[h2]
                for kc in range(8):
                    V("pe", lambda e, kc=kc, h2=h2, bk2=bk2: e.matmul(
                        banks[bk2][:, :], lhsT=hT[:, kc, :], rhs=wz[:, kc, h2 * 512:(h2 + 1) * 512],
                        start=(kc == 0), stop=(kc == 7)), [R_wz, R_hT], [R_bank[bk2]])
            yield 8
            for h2 in range(2):
                bk2 = AB[h2]
                sl = slice(h2 * 512, (h2 + 1) * 512)
                V("act", lambda e, bk2=bk2: e.activation(out=zscr[:], in_=banks[bk2][:, :], func=AF.Exp, scale=-1.0),
                  [R_bank[bk2]], [R_zscr])
                yield 1
                V("act", lambda e: e.activation(out=zscr[:], in_=zscr[:], func=AF.Ln, bias=1.0), [R_zscr], [R_zscr])
                yield 1
                V("act", lambda e: e.activation(out=zscr[:], in_=zscr[:], func=AF.Exp, scale=-1.0), [R_zscr], [R_zscr])
                V("dve", lambda e, bk2=bk2, sl=sl: e.tensor_tensor(out=zsp[:, sl], in0=banks[bk2][:, :], in1=zscr[:],
                                                                   op=ALU.mult), [R_bank[bk2], R_zscr], [R_zsp[par]])
                if l0 is not None and h2 == 0:
                    V("dve", lambda e: e.tensor_scalar_add(out=gt[:], in0=gt[:], scalar1=1.0), [R_gt], [R_gt])
                    V("dve", lambda e: e.reciprocal(out=gt[:], in_=gt[:]), [R_gt], [R_gt])
                if h2 == 0:
                    yield 3
            state["a2_active"] = False
            return
        S.dma("sp", X[:], src_rows, reads=[R_src] if R_src is not None else [], writes=[RX])
        yield 5
        rstd_of(X[:], RX, 0, part=1)
        yield 4
        rstd_of(X[:], RX, 0, part=2)
        yield 2
        V("dve", lambda e: e.scalar_tensor_tensor(out=tmpf[:], in0=X[:], scalar=small[:, 1:2], in1=modv[:, D:2 * D],
                                                  op0=ALU.mult, op1=ALU.mult), [RX, R_small, R_modv], [R_tmpf])
        V("pool", lambda e: e.tensor_tensor(out=hb[:], in0=tmpf[:], in1=modv[:, 0:D], op=ALU.add),
          [R_tmpf, R_modv], [R_hb])
        yield 6
        while state.get("a2_active", False):
            yield 1
        bk = AB[0]
        pv = banks[bk][:].bitcast(BF16)
        for kc in range(8):
            V("pe", lambda e, kc=kc: e.transpose(pv[:, kc * 128:(kc + 1) * 128], hb[:, kc * 128:(kc + 1) * 128], ident[:]),
              [R_hb, R_const], [R_bank[bk]])
        yield 4
        V("dve", lambda e: e.tensor_copy(out=hT[:].rearrange("p a b -> p (a b)"), in_=pv), [R_bank[bk]], [R_hT])
        yield 2
        proj_fm(wq, R_wq, 4, AB, 0)
        yield 2
        proj_fm(wq, R_wq, 4, AB, 4)
        yield 4
        V("act", lambda e: e.copy(out=qp[:, 0:4, :].rearrange("p a b -> p (a b)"), in_=banks[AB[0]][:, :]),
          [R_bank[AB[0]]], [R_qp])
        yield 1
        V("act", lambda e: e.copy(out=qp[:, 4:8, :].rearrange("p a b -> p (a b)"), in_=banks[AB[1]][:, :]),
          [R_bank[AB[1]]], [R_qp])
        if l0 is not None:
            qpz_p, R_qpz_p = l0["qpz"][par], l0["R_qpz"][par]
            for b in range(2):
                for (p0, p1, dv) in qz_dst(qpz_p, 4 * b, 4):
                    V("pool", lambda e, b=b, p0=p0, p1=p1, dv=dv: e.tensor_copy(out=dv, in_=qp[p0:p1, 4 * b:4 * b + 4, :]),
                      [R_qp], [R_qpz_p])
        yield 4
        rope_n(qp[:, 0:4, :], R_qp, 4, qz_dst(qrp, 0, 4), R_qrp[par], cview, sview, AB[0], 0)
        yield 1
        rope_n(qp[:, 4:8, :], R_qp, 4, qz_dst(qrp, 4, 4), R_qrp[par], cview, sview, AB[1], 512)

    def C_gen(slot, wo, R_w, dst_rows, R_dst, fing, combine, cdelay=8):
        ozb, ozT, xo, tmpC, smallC = ws["ozb"], ws["ozT"], ws["xo"], ws["tmpC"], ws["smallC"]
        combine()
        yield cdelay
        bk = CB
        pv = banks[bk][:].bitcast(BF16)
        for kc in range(8):
            V("pe", lambda e, kc=kc: e.transpose(pv[:, kc * 128:(kc + 1) * 128], ozb[:, kc * 128:(kc + 1) * 128], ident[:]),
              [R_ozb, R_const], [R_bank[bk]])
        yield 4
        V("dve", lambda e: e.tensor_copy(out=ozT[:].rearrange("p a b -> p (a b)"), in_=pv), [R_bank[bk]], [R_ozT])
        yield 2
        for h2 in range(2):
            sl = slice(h2 * 512, (h2 + 1) * 512)
            for kc in range(8):
                V("pe", lambda e, kc=kc, h2=h2: e.matmul(
                    banks[bk][:, :], lhsT=ozT[:, kc, :], rhs=wo[:, kc, h2 * 512:(h2 + 1) * 512],
                    start=(kc == 0), stop=(kc == 7)), [R_w, R_ozT], [R_bank[bk]])
            yield 3
            V("dve", lambda e, sl=sl, h2=h2: e.tensor_tensor(
                out=tmpC[:, sl], in0=banks[bk][:, :], in1=modv[:, 2 * D + h2 * 512:2 * D + (h2 + 1) * 512], op=ALU.mult),
              [R_bank[bk], R_modv], [R_tmpC])
            V("pool", lambda e, sl=sl: e.tensor_tensor(out=xo[:, sl], in0=tmpC[:, sl], in1=xt[slot][:, sl], op=ALU.add),
              [R_tmpC, R_xt[slot]], [R_xo])
            yield 3
        if fing is not None:
            fg, R_fg = fing
            rstd_of(xo[:], R_xo, 0, scr=(tmpC, R_tmpC), small=smallC, R_small=R_smallC, part=1)
            yield 4
            rstd_of(xo[:], R_xo, 0, scr=(tmpC, R_tmpC), small=smallC, R_small=R_smallC, part=2)
            yield 2
            V("dve", lambda e: e.scalar_tensor_tensor(out=xo[:], in0=xo[:], scalar=smallC[:, 1:2], in1=fg[:],
                                                      op0=ALU.mult, op1=ALU.mult), [R_xo, R_smallC, R_fg], [R_xo])
            yield 2
        S.dma("sp", dst_rows, xo[:], reads=[R_xo], writes=[R_dst])

    def run_p4(n_tiles, mkA, mkB, mkC):
        for _ in mkA(0, 1):
            pass
        for j in range(n_tiles):
            bg_add(mkA(j, 2), front=True)
            if j + 1 < n_tiles:
                bg_add(mkA(j + 1, 1))
            mkB(j)
            flush()
            bg_drain()
            gC = mkC(j)
            d = next(gC)
            bgq.append([gC, d if isinstance(d, int) else 1])
        bg_drain()

    def ada_rows(layer):
        with ExitStack() as les:
            cts = sb("cts", [128, 8, nseq], F32, les)
            awb = [sb("awb%d" % i, [128, 3 * D], F32, les) for i in range(2)]
            R_awb = [Res("awb0"), Res("awb1")]
            rows = sb("rows", [nseq, 3 * D], F32, les)
            brow = sb("brow", [nseq, 3 * D], F32, les)
            R_cts, R_rows, R_brow = Res("cts"), Res("rows"), Res("brow")
            ctf = cts[:].rearrange("p a b -> p (a b)")
            S.dma("sp", cts[:], ct_d, writes=[R_cts])
            for s in range(nseq):
                S.dma("sp", brow[s:s + 1, :], adab_d[layer:layer + 1, :], writes=[R_brow])
            sm = small[:, 0:8 * nseq]
            V("act", lambda e: e.activation(out=sm, in_=ctf, func=AF.Exp, scale=-1.0), [R_cts], [R_small])
            V("dve", lambda e: e.tensor_scalar_add(out=sm, in0=sm, scalar1=1.0), [R_small], [R_small])
            V("dve", lambda e: e.reciprocal(out=sm, in_=sm), [R_small], [R_small])
            V("dve", lambda e: e.tensor_mul(out=ctf, in0=ctf, in1=sm), [R_small, R_cts], [R_cts])
            aw = adaw_d[layer].rearrange("(c p) n -> c p n", p=128)
            for kc in range(8):
                b = kc % 2
                S.dma("sp", awb[b][:], aw[kc], writes=[R_awb[b]])
                for nb in range(6):
                    V("pe", lambda e, kc=kc, nb=nb, b=b: e.matmul(
                        banks[nb][0:nseq, :], lhsT=cts[:, kc, :], rhs=awb[b][:, nb * 512:(nb + 1) * 512],
                        start=(kc == 0), stop=(kc == 7)), [R_cts, R_awb[b]], [R_bank[nb]])
            for nb in range(6):
                V("dve", lambda e, nb=nb: e.tensor_add(out=rows[:, nb * 512:(nb + 1) * 512], in0=banks[nb][0:nseq, :],
                                                       in1=brow[:, nb * 512:(nb + 1) * 512]),
                  [R_bank[nb], R_brow, R_rows], [R_rows])
            S.dma("sp", mod_d[layer], rows[:], reads=[R_rows], writes=[R_mod])
            S.barrier()

    def load_mod(layer, s):
        S.dma("sp", modv[:], mod_d[layer, s:s + 1, :].broadcast_to([128, 3 * D]), reads=[R_mod], writes=[R_modv])
        S.dma("sp", tmpf2[:], normg_d[layer:layer + 1, :].broadcast_to([128, D]), writes=[R_tmpf2])
        V("dve", lambda e: e.scalar_tensor_tensor(out=modv[:, D:2 * D], in0=modv[:, D:2 * D], scalar=1.0,
                                                  in1=tmpf2[:], op0=ALU.add, op1=ALU.mult),
          [R_modv, R_tmpf2], [R_modv])

    def load_x_tile(src_rows, R_src, slot):
        S.dma("sp", xt[slot][:], src_rows, reads=[R_src] if R_src is not None else [], writes=[R_xt[slot]])

    def rstd_of(X, RX, col, scr=None, small=small, R_small=R_small, part=0):
        scr_t, R_scr = scr if scr is not None else (tmpf2, R_tmpf2)
        if part in (0, 1):
            V("act", lambda e: e.activation(out=scr_t[:], in_=X, func=AF.Square), [RX], [R_scr])
            V("dve", lambda e: e.reduce_sum(out=small[:, col:col + 1], in_=scr_t[:], axis=AX.X), [R_scr], [R_small])
            V("dve", lambda e: e.tensor_scalar(out=small[:, col + 1:col + 2], in0=small[:, col:col + 1], scalar1=1.0 / D,
                                               scalar2=EPS, op0=ALU.mult, op1=ALU.add), [R_small], [R_small])
        if part in (0, 2):
            V("act", lambda e: e.activation(out=small[:, col + 1:col + 2], in_=small[:, col + 1:col + 2], func=AF.Ln),
              [R_small], [R_small])
            V("act", lambda e: e.activation(out=small[:, col + 1:col + 2], in_=small[:, col + 1:col + 2], func=AF.Exp,
                                            scale=-0.5), [R_small], [R_small])

    def norm_tile(slot):
        X = xt[slot]
        RX = R_xt[slot]
        rstd_of(X[:], RX, 0)
        V("dve", lambda e: e.scalar_tensor_tensor(out=tmpf[:], in0=X[:], scalar=small[:, 1:2], in1=modv[:, D:2 * D],
                                                  op0=ALU.mult, op1=ALU.mult), [RX, R_small, R_modv], [R_tmpf])
        V("pool", lambda e: e.tensor_tensor(out=hb[:], in0=tmpf[:], in1=modv[:, 0:D], op=ALU.add),
          [R_tmpf, R_modv], [R_hb])
        bk = MB[0]
        pv = banks[bk][:].bitcast(BF16)
        for kc in range(8):
            V("pe", lambda e, kc=kc: e.transpose(pv[:, kc * 128:(kc + 1) * 128], hb[:, kc * 128:(kc + 1) * 128], ident[:]),
              [R_hb, R_const], [R_bank[bk]])
        V("dve", lambda e: e.tensor_copy(out=hT[:].rearrange("p a b -> p (a b)"), in_=pv), [R_bank[bk]], [R_hT])

    def norm_stage(slot, par, P):
        X, RX = xt[slot], R_xt[slot]
        sq, R_sq = P["sq"]
        st, R_st = P["stt"]
        sm, R_sm = P["small"]
        hbp, R_hbp = P["hb"][par]
        rstd_of(X[:], RX, 2 * par, scr=(sq, R_sq), small=sm, R_small=R_sm)
        V("dve", lambda e: e.scalar_tensor_tensor(out=st[:], in0=X[:], scalar=sm[:, 2 * par + 1:2 * par + 2],
                                                  in1=modv[:, D:2 * D], op0=ALU.mult, op1=ALU.mult),
          [RX, R_sm, R_modv], [R_st])
        V("pool", lambda e: e.tensor_tensor(out=hbp[:], in0=st[:], in1=modv[:, 0:D], op=ALU.add),
          [R_st, R_modv], [R_hbp])

    def transpose_stage(par, P):
        hbp, R_hbp = P["hb"][par]
        hT, R_hT = P["hT"][par]
        bk = MB[0]
        pv = banks[bk][:].bitcast(BF16)
        for kc in range(8):
            V("pe", lambda e, kc=kc: e.transpose(pv[:, kc * 128:(kc + 1) * 128], hbp[:, kc * 128:(kc + 1) * 128], ident[:]),
              [R_hbp, R_const], [R_bank[bk]])
        V("dve", lambda e: e.tensor_copy(out=hT[:].rearrange("p a b -> p (a b)"), in_=pv), [R_bank[bk]], [R_hT])

    def proj_fm(wtile, R_w, ntiles, bk_list, j0=0, hTb=None):
        hT_, R_hT_ = hTb if hTb is not None else (hT, R_hT)
        for j in range(j0, j0 + ntiles):
            bk = bk_list[j // 4]
            for kc in range(8):
                V("pe", lambda e, j=j, kc=kc, bk=bk: e.matmul(
                    banks[bk][:, (j % 4) * 128:(j % 4 + 1) * 128], lhsT=wtile[:, kc, j * 128:(j + 1) * 128],
                    rhs=hT_[:, kc, :], start=(kc == 0), stop=(kc == 7)), [R_w, R_hT_], [R_bank[bk]])

    def rope_n(srcv, R_src, n, dstv, R_dst, cview, sview, bk, tcol):
        V("pe", lambda e: e.matmul(banks[bk][:, 0:n * 128], lhsT=perm[:], rhs=srcv, start=True, stop=True),
          [R_src, R_const], [R_bank[bk]])
        cb = cview.unsqueeze(1).broadcast_to([128, n, 128])
        sbv = sview.unsqueeze(1).broadcast_to([128, n, 128])
        t1 = tmpf[:, tcol:tcol + n * 128].rearrange("p (a b) -> p a b", b=128)
        t2 = tmpf2[:, tcol:tcol + n * 128].rearrange("p (a b) -> p a b", b=128)
        V("dve", lambda e: e.tensor_tensor(out=t1, in0=banks[bk][:, 0:n * 128].rearrange("p (a b) -> p a b", b=128),
                                           in1=sbv, op=ALU.mult), [R_bank[bk], R_const], [R_tmpf])
        V("pool", lambda e: e.tensor_tensor(out=t2, in0=srcv, in1=cb, op=ALU.mult), [R_src, R_const], [R_tmpf2])
        if isinstance(dstv, list):
            for (p0, p1, dv) in dstv:
                V("dve", lambda e, p0=p0, p1=p1, dv=dv: e.tensor_tensor(out=dv, in0=t1[p0:p1], in1=t2[p0:p1], op=ALU.add),
                  [R_tmpf, R_tmpf2], [R_dst])
        else:
            V("dve", lambda e: e.tensor_tensor(out=dstv, in0=t1, in1=t2, op=ALU.add), [R_tmpf, R_tmpf2], [R_dst])

    def qz_dst(qz, pair0, n):
        return [(0, 64, qz[0:64, 2 * pair0:2 * (pair0 + n):2, :]),
                (64, 128, qz[64:128, 2 * pair0 + 1:2 * (pair0 + n):2, :])]

    def evac_qp(bl=None):
        bl = bl or P2SB
        for b in range(2):
            V("act", lambda e, b=b: e.copy(out=qp[:, 4 * b:4 * b + 4, :].rearrange("p a b -> p (a b)"),
                                           in_=banks[bl[b]][:, :]), [R_bank[bl[b]]], [R_qp])

    def z_proj(wz, R_w):
        zs = ws["zs"]
        for h2 in range(2):
            bk = MB[h2]
            for kc in range(8):
                V("pe", lambda e, kc=kc, h2=h2, bk=bk: e.matmul(
                    banks[bk][:, :], lhsT=hT[:, kc, :], rhs=wz[:, kc, h2 * 512:(h2 + 1) * 512],
                    start=(kc == 0), stop=(kc == 7)), [R_w, R_hT], [R_bank[bk]])
        for h2 in range(2):
            bk = MB[h2]
            sl = slice(h2 * 512, (h2 + 1) * 512)
            V("act", lambda e, bk=bk, sl=sl: e.activation(out=tmpf[:, sl], in_=banks[bk][:, :], func=AF.Exp, scale=-1.0),
              [R_bank[bk]], [R_tmpf])
            V("dve", lambda e, sl=sl: e.tensor_scalar_add(out=tmpf[:, sl], in0=tmpf[:, sl], scalar1=1.0),
              [R_tmpf], [R_tmpf])
            V("dve", lambda e, sl=sl: e.reciprocal(out=tmpf[:, sl], in_=tmpf[:, sl]), [R_tmpf], [R_tmpf])
            V("dve", lambda e, bk=bk, sl=sl: e.tensor_tensor(out=zs[:, sl], in0=banks[bk][:, :], in1=tmpf[:, sl],
                                                             op=ALU.mult), [R_bank[bk], R_tmpf], [R_zs])

    pending = []
    LOOK = 2

    def _drain(limit):
        while pending and sum(1 for it in pending if it[0] == "pv") > limit:
            kind, fn = pending.pop(0)
            fn()
        while pending and pending[0][0] != "pv":
            kind, fn = pending.pop(0)
            fn()

    def flush():
        while pending:
            kind, fn = pending.pop(0)
            fn()

    def defer(fn):
        pending.append(("ev", fn))
        if not any(it[0] == "pv" for it in pending[:-1]):
            _drain(LOOK)

    def attn_unit(heads, kT_fn, q_src, v_ap, o_bank, o_cols, o_w, first, last, bias=None, scale=0.125,
                  reads_extra=(), kparts=128, shared_k=False):
        n = len(heads)
        sbk = SB[state["srot"] % len(SB)]
        state["srot"] += 1
        pb_i = state["prot"] % NP
        state["prot"] += 1
        pb = ws["pbuf"][pb_i]
        rd = [R_const] + list(reads_extra)
        if bias is not None:
            blhs, brhs = bias
            V("pe", lambda e: e.matmul(banks[sbk][0:kparts, 0:n * 128], lhsT=blhs, rhs=brhs, start=True, stop=False, skip_group_check=True),
              rd, [R_bank[sbk]])
        if shared_k:
            h0_, hstep = heads[0], (heads[1] - heads[0] if n > 1 else 1)
            V("pe", lambda e: e.matmul(banks[sbk][0:kparts, 0:n * 128], lhsT=kT_fn(0),
                                       rhs=q_src[:, h0_:h0_ + hstep * (n - 1) + 1:hstep, :],
                                       start=(bias is None), stop=True, skip_group_check=True), rd, [R_bank[sbk]])
        else:
            for i, h in enumerate(heads):
                V("pe", lambda e, i=i, h=h: e.matmul(
                    banks[sbk][0:kparts, i * 128:(i + 1) * 128], lhsT=kT_fn(i),
                    rhs=q_src[:, h, :], start=(bias is None), stop=True, skip_group_check=True),
                  rd, [R_bank[sbk]])
        V("act", lambda e: e.activation(out=pb[0:kparts, 0:n * 128], in_=banks[sbk][0:kparts, 0:n * 128],
                                        func=AF.Exp, scale=scale), [R_bank[sbk]], [R_pbuf[pb_i]])

        def part2():
            for i in range(n):
                V("pe", lambda e, i=i: e.matmul(
                    banks[o_bank][:, o_cols[i]:o_cols[i] + o_w], lhsT=pb[0:kparts, i * 128:(i + 1) * 128],
                    rhs=v_ap(i), start=(first and i == 0), stop=last, skip_group_check=True), [R_pbuf[pb_i]] + rd, [R_bank[o_bank]])
        pending.append(("pv", part2))
        _drain(LOOK)
        bg_tick()

    def out_proj_residual(wo, R_w, slot, dst_rows, R_dst, fing):
        ozb, ozT, xo = ws["ozb"], ws["ozT"], ws["xo"]
        bk = MB[0]
        pv = banks[bk][:].bitcast(BF16)
        for kc in range(8):
            V("pe", lambda e, kc=kc: e.transpose(pv[:, kc * 128:(kc + 1) * 128], ozb[:, kc * 128:(kc + 1) * 128], ident[:]),
              [R_ozb, R_const], [R_bank[bk]])
        V("dve", lambda e: e.tensor_copy(out=ozT[:].rearrange("p a b -> p (a b)"), in_=pv), [R_bank[bk]], [R_ozT])
        for h2 in range(2):
            bk2 = SB[h2]
            for kc in range(8):
                V("pe", lambda e, kc=kc, h2=h2, bk2=bk2: e.matmul(
                    banks[bk2][:, :], lhsT=ozT[:, kc, :], rhs=wo[:, kc, h2 * 512:(h2 + 1) * 512],
                    start=(kc == 0), stop=(kc == 7)), [R_w, R_ozT], [R_bank[bk2]])
        for h2 in range(2):
            bk2 = SB[h2]
            sl = slice(h2 * 512, (h2 + 1) * 512)
            V("dve", lambda e, bk2=bk2, sl=sl, h2=h2: e.tensor_tensor(
                out=tmpf[:, sl], in0=banks[bk2][:, :], in1=modv[:, 2 * D + h2 * 512:2 * D + (h2 + 1) * 512], op=ALU.mult),
              [R_bank[bk2], R_modv], [R_tmpf])
            V("pool", lambda e, sl=sl: e.tensor_tensor(out=xo[:, sl], in0=tmpf[:, sl], in1=xt[slot][:, sl], op=ALU.add),
              [R_tmpf, R_xt[slot]], [R_xo])
        if fing is not None:
            fg, R_fg = fing
            rstd_of(xo[:], R_xo, 2)
            V("dve", lambda e: e.scalar_tensor_tensor(out=xo[:], in0=xo[:], scalar=small[:, 3:4], in1=fg[:],
                                                      op0=ALU.mult, op1=ALU.mult), [R_xo, R_small, R_fg], [R_xo])
        S.dma("sp", dst_rows, xo[:], reads=[R_xo], writes=[R_dst])

    def load_w(dst, src2d, R_w, chunk=1024):
        ncols = src2d.shape[1]
        v = src2d.rearrange("(c p) n -> p c n", p=128)
        c0 = 0
        while c0 < ncols:
            c1 = min(ncols, c0 + chunk)
            S.dma("pool", dst[:, :, c0:c1], v[:, :, c0:c1], writes=[R_w])
            c0 = c1

    def layer0():
        with ExitStack() as les:
            def lsb(name, shape, dt):
                return sb(name, shape, dt, les)
            vcmp_c = load_const("c_vcmp", les)
            wq = wreg[:, :, 0:D]
            wz = wreg[:, :, D:2 * D]
            wo = wreg[:, :, 2 * D:3 * D]
            wg = lsb("wg", [128, 8, 48], BF16)
            w2k = lsb("w2k", [128, 2, 128], BF16)
            w2v = lsb("w2v", [128, 2, 64], BF16)
            pe2 = [lsb("pe2k_s", [128, 16], BF16), lsb("pe2v_s", [128, 16], BF16)]
            R_wq, R_wz, R_wo = R_wreg
            R_wkv, R_wv, R_wg, R_w1, R_w2, R_pe = Res("wkv"), Res("wv"), Res("wg"), Res("w1"), Res("w2"), Res("pe")
            ksT = lsb("ksT", [128, 2, T], BF16)
            kwT = lsb("kwT", [128, 2, T], BF16)
            vs = lsb("vs", [128, NT, 2, 65], BF16)
            vw = lsb("vw", [128, NT, 2, 65], BF16)
            kcmpT = lsb("kcmpT", [128, 2, 128], BF16)
            vcmp = lsb("vcmp", [128, 2, 97], BF16)
            R_kc2, R_ksT, R_kwT, R_vs, R_vw = Res("kc2"), Res("ksT"), Res("kwT"), Res("vs"), Res("vw")
            R_kcmp, R_vcmp, R_hid, R_hbias, R_kvp, R_negT, R_wkt = (Res("kcmp"), Res("vcmp"), Res("hid"),
                                                                    Res("hbias"), Res("kvp"), Res("negT"), Res("wkt"))
            W = win0_d
            offs = {"q": 0, "kc": 1024, "vc": 1152, "ks": 1280, "vs": 1408, "kw": 1536, "vw": 1664, "z": 1792, "g": 2816}
            Wv = W.rearrange("(c p) n -> p c n", p=128)
            for kv in range(2):
                S.dma("pool", pe2[kv][:], pe2_d[kv], writes=[R_pe])
            w2kd = ckw2_d.rearrange("(c p) n -> p c n", p=128)
            S.dma("pool", w2k[:, :, 0:64], w2kd, writes=[R_w2])
            S.dma("pool", w2k[:, :, 64:128], w2kd, writes=[R_w2])
            S.dma("pool", w2v[:, :, :], cvw2_d.rearrange("(c p) n -> p c n", p=128), writes=[R_w2])
            S.dma("pool", wg[:, :, :], Wv[:, :, offs["g"]:offs["g"] + 48], writes=[R_wg])
            V("pool", lambda e: e.memset(vs[:, :, :, 64:65], 1.0), [], [R_vs])
            V("pool", lambda e: e.memset(vw[:, :, :, 64:65], 1.0), [], [R_vw])
            for g in range(2):
                V("pool", lambda e, g=g: e.tensor_copy(out=vcmp[:, g, 64:97], in_=vcmp_c[:]), [R_const], [R_vcmp])

            mark("w0")
            for s in range(nseq):
                load_mod(0, s)
                mark("mod0")
                with ExitStack() as sa:
                    w1 = [sb("w1k", [128, 16, 256], BF16, sa), sb("w1v", [128, 16, 256], BF16, sa)]
                    wsrc = sb("wsrc", [128, 8, 768], BF16, sa)
                    S.dma("pool", wsrc[:, :, :], Wv[:, :, 1024:1792], writes=[R_wkv])
                    loc = {"kc": 0, "vc": 128, "ks": 256, "kw": 512}
                    dup_c0 = [loc[nm] + 64 * g for nm in ("kc", "vc", "ks", "kw") for g in range(2)]
                    wkv = sb("wkv", [128, 8, 8 * 128], BF16, sa)
                    R_wkv2 = Res("wkv2")
                    for j in range(8):
                        V("dve" if j % 2 == 0 else "act", lambda e, j=j: (e.tensor_copy if j % 2 == 0 else e.copy)(
                            out=wkv[:, :, j * 128:(j + 1) * 128].rearrange("p c (a b) -> p c a b", b=64),
                            in_=wsrc[:, :, dup_c0[j]:dup_c0[j] + 64].unsqueeze(2).broadcast_to([128, 8, 2, 64])),
                          [R_wkv], [R_wkv2])
                    kc2 = [sb("kc2", [128, 2, T], BF16, sa), sb("vc2", [128, 2, T], BF16, sa)]
                    hid = sb("hid", [128, 2, 128], BF16, sa)
                    hbias = sb("hbias", [128, 4], F32, sa)
                    kvp = sb("kvp", [128, 4, 128], BF16, sa)
                    for kv, w1d in enumerate((ckw1_d, cvw1_d)):
                        w1v_ = w1d.rearrange("(j p) n -> p j n", p=128)
                        for j0 in range(0, 16, 8):
                            S.dma("pool", w1[kv][:, j0:j0 + 8, :], w1v_[:, j0:j0 + 8, :], writes=[R_w1])
                    hbB = sb("hbB", [128, D], BF16, sa)
                    sqS = sb("sqS", [128, D], F32, sa)
                    sttS = sb("sttS", [128, D], F32, sa)
                    smallP = sb("smallP", [128, 8], F32, sa)
                    hTB = sb("hTB", [128, 8, 128], BF16, sa)
                    P2 = {"sq": (sqS, Res("sqS")), "stt": (sttS, Res("sttS")), "small": (smallP, Res("smallP")),
                          "hb": [(hb, R_hb), (hbB, Res("hbB"))], "hT": [(hT, R_hT), (hTB, Res("hTB"))]}
                    for t_ in range(2):
                        load_x_tile(x_d[s, t_ * 128:(t_ + 1) * 128, :], None, t_ % NX)
                        norm_stage(t_ % NX, t_ % 2, P2)
                    transpose_stage(0, P2)
                    for tt in range(NT):
                        slot = tt % NX
                        cs = slice(tt * 128, (tt + 1) * 128)
                        if tt + 2 < NT:
                            load_x_tile(x_d[s, (tt + 2) * 128:(tt + 3) * 128, :], None, (tt + 2) % NX)
                            norm_stage((tt + 2) % NX, (tt + 2) % 2, P2)
                        if tt + 1 < NT:
                            transpose_stage((tt + 1) % 2, P2)
                        hTc, R_hTc = P2["hT"][tt % 2]
                        proj_fm(wkv, R_wkv2, 8, [P2SB[0], P2SB[1]], hTb=(hTc, R_hTc))
                        bk = MB[1]
                        for kc in range(8):
                            V("pe", lambda e, kc=kc, bk=bk: e.matmul(
                                banks[bk][:, 0:256], lhsT=hTc[:, kc, :],
                                rhs=wsrc[:, kc, 384:768].rearrange("p (a b) -> p a b", b=128)[:, 0:3:2, :],
                                start=(kc == 0), stop=(kc == 7)), [R_wkv, R_hTc], [R_bank[bk]])
                        for kv in range(2):
                            src = banks[P2SB[0]][:, kv * 256:(kv + 1) * 256].rearrange("p (g b) -> p g b", b=128)
                            V("dve", lambda e, kv=kv, src=src, cs=cs: e.tensor_copy(out=kc2[kv][0:64, :, cs], in_=src[0:64]),
                              [R_bank[P2SB[0]]], [R_kc2])
                            if tt == 0:
                                V("dve", lambda e, kv=kv, src=src: e.tensor_copy(out=kc2[kv][64:128, :, 0:127],
                                                                                 in_=src[64:128, :, 1:128]),
                                  [R_bank[P2SB[0]]], [R_kc2])
                            else:
                                V("dve", lambda e, kv=kv, src=src, tt=tt: e.tensor_copy(
                                    out=kc2[kv][64:128, :, tt * 128 - 1:(tt + 1) * 128 - 1], in_=src[64:128]),
                                  [R_bank[P2SB[0]]], [R_kc2])
                        V("act", lambda e: e.copy(out=kvp[:].rearrange("p a b -> p (a b)"), in_=banks[P2SB[1]][:, :]),
                          [R_bank[P2SB[1]]], [R_kvp])
                        rope_n(kvp[:, 0:2, :], R_kvp, 2, ksT[:, :, cs], R_ksT, cos_t[:, cs], sin_t[:, cs], P2SB[2], 0)
                        rope_n(kvp[:, 2:4, :], R_kvp, 2, kwT[:, :, cs], R_kwT, cos_t[:, cs], sin_t[:, cs], P2SB[3], 256)
                        bk = MB[1]
                        V("act", lambda e, bk=bk, tt=tt: e.copy(out=vs[:, tt, :, 0:64],
                                                                in_=banks[bk][:, 0:128].rearrange("p (g d) -> p g d", d=64)),
                          [R_bank[bk]], [R_vs])
                        V("act", lambda e, bk=bk, tt=tt: e.copy(out=vw[:, tt, :, 0:64],
                                                                in_=banks[bk][:, 128:256].rearrange("p (g d) -> p g d", d=64)),
                          [R_bank[bk]], [R_vw])
                        mark("p2t%d" % tt)
                    mark("p2")
                    for kv in range(2):
                        for hc in range(2):
                            bk = MB[1]
                            for j in range(16):
                                V("pe", lambda e, kv=kv, hc=hc, j=j, bk=bk: e.matmul(
                                    banks[bk][:, 0:1], lhsT=w1[kv][:, j, hc * 128:(hc + 1) * 128], rhs=pe2[kv][:, j:j + 1],
                                    start=(j == 0), stop=(j == 15)), [R_w1, R_pe], [R_bank[bk]])
                            V("dve", lambda e, kv=kv, hc=hc, bk=bk: e.tensor_copy(
                                out=hbias[:, kv * 2 + hc:kv * 2 + hc + 1], in_=banks[bk][:, 0:1]), [R_bank[bk]], [R_hbias])
                    for kv in range(2):
                        for g in range(2):
                            for hc in range(2):
                                bk = P2SB[(kv * 4 + g * 2 + hc) % 4]
                                for j in range(16):
                                    V("pe", lambda e, kv=kv, g=g, hc=hc, j=j, bk=bk: e.matmul(
                                        banks[bk][:, 0:127], lhsT=w1[kv][:, j, hc * 128:(hc + 1) * 128],
                                        rhs=kc2[kv][:, g, 2 * j:2 * j + 16 * 126 + 1:16], start=(j == 0), stop=(j == 15)),
                                      [R_w1, R_kc2], [R_bank[bk]])
                                bcol = hbias[:, kv * 2 + hc:kv * 2 + hc + 1]
                                V("dve", lambda e, bk=bk, bcol=bcol: e.tensor_scalar(
                                    out=tmpf[:, 0:127], in0=banks[bk][:, 0:127], scalar1=bcol, scalar2=None, op0=ALU.add),
                                  [R_bank[bk], R_hbias], [R_tmpf])
                                V("act", lambda e: e.activation(out=tmpf2[:, 0:127], in_=tmpf[:, 0:127], func=AF.Exp,
                                                                scale=-1.0), [R_tmpf], [R_tmpf2])
                                V("dve", lambda e: e.tensor_scalar_add(out=tmpf2[:, 0:127], in0=tmpf2[:, 0:127], scalar1=1.0),
                                  [R_tmpf2], [R_tmpf2])
                                V("dve", lambda e: e.reciprocal(out=tmpf2[:, 0:127], in_=tmpf2[:, 0:127]),
                                  [R_tmpf2], [R_tmpf2])
                                V("dve", lambda e, hc=hc: e.tensor_tensor(out=hid[:, hc, 0:127], in0=tmpf[:, 0:127],
                                                                          in1=tmpf2[:, 0:127], op=ALU.mult),
                                  [R_tmpf, R_tmpf2], [R_hid])
                            bk = MB[1]
                            if kv == 0:
                                for hc in range(2):
                                    V("pe", lambda e, hc=hc, bk=bk: e.matmul(
                                        banks[bk][:, 0:127], lhsT=w2k[:, hc, :], rhs=hid[:, hc, 0:127],
                                        start=(hc == 0), stop=(hc == 1)), [R_w2, R_hid], [R_bank[bk]])
                                V("dve", lambda e, g=g, bk=bk: e.tensor_copy(out=kcmpT[:, g, 0:127], in_=banks[bk][:, 0:127]),
                                  [R_bank[bk]], [R_kcmp])
                            else:
                                for hc in range(2):
                                    V("pe", lambda e, hc=hc, bk=bk: e.matmul(
                                        banks[bk][0:127, 0:64], lhsT=hid[:, hc, 0:127], rhs=w2v[:, hc, :],
                                        start=(hc == 0), stop=(hc == 1)), [R_w2, R_hid], [R_bank[bk]])
                                V("dve", lambda e, g=g, bk=bk: e.tensor_copy(out=vcmp[0:127, g, 0:64],
                                                                             in_=banks[bk][0:127, 0:64]),
                                  [R_bank[bk]], [R_vcmp])
                    mark("p3")
                    S.barrier()
                with ExitStack() as sB:
                    anti = load_const("c_anti", sB)
                    cmpbias = load_const("c_cmpbias", sB)
                    esel = load_const("c_esel", sB)
                    cand_t = load_const("c_cand", sB)
                    forced_t = load_const("c_forced", sB)
                    alloc_ws(sB, 3, False)
                    stage, coef, ozb, tmpC = ws["stage"], ws["coef"], ws["ozb"], ws["tmpC"]
                    sov = sb("sov", [128, 2, 8, 32], F32, sB)
                    gates2 = [sb("gates%d" % i, [128, 48], F32, sB) for i in range(2)]
                    R_gates2 = [Res("gates0"), Res("gates1")]
                    qpz2 = [sb("qpz%d" % i, [128, 16, 128], BF16, sB) for i in range(2)]
                    R_qpz2 = [Res("qpz0"), Res("qpz1")]
                    for i in range(2):
                        V("pool", lambda e, i=i: e.memset(qpz2[i][:], 0.0), [], [R_qpz2[i]])
                    negT = sb("negT", [128, 2, 4, 128], BF16, sB)
                    V("pool", lambda e: e.memset(negT[:], 0.0), [], [R_negT])
                    wk_t = sb("wk_t", [128, 16, 32], F32, sB)
                    tk = sb("tk", [128, 224], F32, sB)
                    hbK = sb("hbK", [128, 2, 32], BF16, sB)
                    smallK = sb("smallK", [128, 16], F32, sB)
                    R_tk, R_hbK = Res("tk"), Res("hbK")
                    l0info = {"qpz": qpz2, "R_qpz": R_qpz2, "wg": wg, "R_wg": R_wg, "gates": gates2, "R_gates": R_gates2}
                    dst_d, R_dst = (x1_d, R_x1[s]) if 1 in layers else (out_d, R_out)

                    def mkA(j, part, s=s):
                        cs = slice(j * 128, (j + 1) * 128)
                        return A_gen(x_d[s, cs, :], None, j, wq, R_wq, wz, R_wz, cos_t[:, cs], sin_t[:, cs], l0info, part)

                    def mkB(tt):
                        par = tt % 2
                        cs = slice(tt * 128, (tt + 1) * 128)
                        qr, R_qr_ = ws["qr"][par], R_qrp[par]
                        qpz, R_qpz = qpz2[par], R_qpz2[par]
                        n_units = 2 * (2 + 2 * (tt + 1) + 2 * (min(tt, 4) + 1))
                        state["bg_stride"] = max(1, n_units // 26)
                        def cmp_(g):
                            hl = [8 * g + i for i in range(8)]
                            for b in range(2):
                                hs = hl[b:8:2]
                                attn_unit(hs, lambda i, g=g: kcmpT[:, g, 0:127], qpz,
                                          lambda i, g=g: vcmp[0:127, g, :], OB[b], [0, 97, 194, 291], 97, True, True,
                                          bias=(ident[0:127, 0:127],
                                                cmpbias[0:127, cs].unsqueeze(1).broadcast_to([127, 4, 128])),
                                          reads_extra=[R_kcmp, R_vcmp, R_qpz], kparts=127, shared_k=True)
                                ov = banks[OB[b]][:, 0:388].rearrange("p (a b) -> p a b", b=97)
                                defer(lambda b=b, g=g, ov=ov: V("dve", lambda e: e.tensor_copy(
                                    out=stage[:, 0, 8 * g + b:8 * g + 8:2, :], in_=ov[:, :, 0:65]),
                                    [R_bank[OB[b]]], [R_stage]))
                                if tt >= 8:
                                    defer(lambda b=b, g=g, ov=ov: V("dve", lambda e: e.tensor_copy(
                                        out=sov[:, g, b:8:2, :], in_=ov[:, :, 65:97]), [R_bank[OB[b]]], [R_sov]))


                        def topk_all():
                            V("dve", lambda e: e.tensor_scalar_max(out=smallK[:, 0:16], in0=stage[:, 0, :, 64], scalar1=1e-30),
                              [R_stage], [R_smallK])
                            V("dve", lambda e: e.reciprocal(out=smallK[:, 0:16], in_=smallK[:, 0:16]), [R_smallK], [R_smallK])
                            V("dve", lambda e: e.tensor_tensor(
                                out=wk_t[:, :, :], in0=sov[:].rearrange("p g h j -> p (g h) j"),
                                in1=smallK[:, 0:16].unsqueeze(2).broadcast_to([128, 16, 32]), op=ALU.mult),
                              [R_sov, R_smallK], [R_wkt])
                            tkw = tk[:, 0:64].rearrange("p (g j) -> p g j", g=2)
                            V("dve", lambda e: e.tensor_reduce(out=tkw, in_=wk_t[:].rearrange("p (g h) j -> p g j h", g=2),
                                                               axis=AX.X, op=ALU.add), [R_wkt], [R_tk])
                            candb = cand_t[:, tt - 8, :].unsqueeze(1).broadcast_to([128, 2, 32])
                            forcb = forced_t[:, tt - 8, :].unsqueeze(1).broadcast_to([128, 2, 32])
                            V("dve", lambda e: e.scalar_tensor_tensor(out=tkw, in0=tkw, scalar=1.0, in1=candb,
                                                                      op0=ALU.add, op1=ALU.mult), [R_tk, R_const], [R_tk])
                            V("dve", lambda e: e.tensor_scalar_add(out=tk[:, 0:64], in0=tk[:, 0:64], scalar1=-1.0),
                              [R_tk], [R_tk])
                            for g in range(2):
                                wsl = tk[:, 32 * g:32 * g + 32]
                                m1 = tk[:, 64 + 8 * g:72 + 8 * g]
                                m2 = tk[:, 112 + 8 * g:120 + 8 * g]
                                rp = tk[:, 80:112] if g == 0 else tk[:, 192:224]
                                V("dve", lambda e, wsl=wsl, m1=m1: e.max(out=m1, in_=wsl), [R_tk], [R_tk])
                                V("dve", lambda e, wsl=wsl, m1=m1, rp=rp: e.match_replace(
                                    out=rp, in_to_replace=m1, in_values=wsl, imm_value=-2.0), [R_tk], [R_tk])
                                V("dve", lambda e, m2=m2, rp=rp: e.max(out=m2, in_=rp), [R_tk], [R_tk])
                                V("dve", lambda e, g=g, wsl=wsl, m2=m2: e.tensor_scalar(
                                    out=tk[:, 128 + 32 * g:160 + 32 * g], in0=wsl, scalar1=m2[:, 4:5], scalar2=None,
                                    op0=ALU.is_ge), [R_tk], [R_tk])
                            selw = tk[:, 128:192].rearrange("p (g j) -> p g j", g=2)
                            V("dve", lambda e: e.tensor_tensor(out=selw, in0=selw, in1=forcb, op=ALU.add),
                              [R_tk, R_const], [R_tk])
                            V("dve", lambda e: e.tensor_scalar(out=hbK[:], in0=selw, scalar1=-1.0, scalar2=-NEG,
                                                               op0=ALU.add, op1=ALU.mult), [R_tk], [R_hbK])
                        def topk_pe_(g):
                            bk = SB[state["srot"] % len(SB)]
                            state["srot"] += 1
                            pvn = banks[bk][:].bitcast(BF16)
                            V("pe", lambda e, pvn=pvn, g=g: e.transpose(pvn[0:32, 0:128], hbK[:, g, :], ident[:]),
                              [R_hbK, R_const], [R_bank[bk]])
                            V("dve", lambda e, pvn=pvn, g=g: e.tensor_copy(
                                out=negT[0:32, g, :, :], in_=pvn[0:32, 0:128].unsqueeze(1).broadcast_to([32, 4, 128])),
                              [R_bank[bk]], [R_negT])
                        def win_(g):
                            hl = [8 * g + i for i in range(8)]
                            for b in range(2):
                                hs = hl[b:8:2]
                                k0 = max(0, tt - 4)
                                for kt in range(k0, tt + 1):
                                    if kt == tt:
                                        bias = (ident[:], causal[:].unsqueeze(1).broadcast_to([128, 4, 128]))
                                    elif kt == tt - 4:
                                        bias = (ident[:], anti[:].unsqueeze(1).broadcast_to([128, 4, 128]))
                                    else:
                                        bias = None
                                    attn_unit(hs, lambda i, kt=kt, g=g: kwT[:, g, kt * 128:(kt + 1) * 128], qr,
                                              lambda i, kt=kt, g=g: vw[:, kt, g, :], OB[b], [0, 65, 130, 195], 65,
                                              kt == k0, kt == tt, bias=bias, reads_extra=[R_kwT, R_vw, R_qr_],
                                              shared_k=True)
                                defer(lambda b=b, g=g: V("dve", lambda e: e.tensor_copy(
                                    out=stage[:, 2, 8 * g + b:8 * g + 8:2, :],
                                    in_=banks[OB[b]][:, 0:260].rearrange("p (a b) -> p a b", b=65)),
                                    [R_bank[OB[b]]], [R_stage]))
                        def slc_(g):
                            hl = [8 * g + i for i in range(8)]
                            for b in range(2):
                                hs = hl[b:8:2]
                                for kt in range(tt + 1):
                                    if kt == tt:
                                        bias = (ident[:], causal[:].unsqueeze(1).broadcast_to([128, 4, 128]))
                                    elif tt >= 8:
                                        bias = (esel[:, kt, :], negT[:, g, :, :])
                                    else:
                                        bias = None
                                    attn_unit(hs, lambda i, kt=kt, g=g: ksT[:, g, kt * 128:(kt + 1) * 128], qr,
                                              lambda i, kt=kt, g=g: vs[:, kt, g, :], OB[b], [0, 65, 130, 195], 65,
                                              kt == 0, kt == tt, bias=bias, reads_extra=[R_ksT, R_vs, R_qr_, R_negT],
                                              shared_k=True)
                                defer(lambda b=b, g=g: V("dve", lambda e: e.tensor_copy(
                                    out=stage[:, 1, 8 * g + b:8 * g + 8:2, :],
                                    in_=banks[OB[b]][:, 0:260].rearrange("p (a b) -> p a b", b=65)),
                                    [R_bank[OB[b]]], [R_stage]))
                        win_(0)
                        cmp_(0)
                        cmp_(1)
                        if tt >= 8:
                            flush()
                            topk_all()
                        win_(1)
                        if tt >= 8:
                            topk_pe_(0)
                            topk_pe_(1)
                        slc_(0)
                        slc_(1)

                    def mkC(tt, s=s):
                        par = tt % 2
                        cs = slice(tt * 128, (tt + 1) * 128)
                        gt, R_gt = gates2[par], R_gates2[par]
                        zsp, R_z = ws["zs"][par], R_zsp[par]

                        def combine():
                            V("dve", lambda e: e.tensor_scalar_max(out=coef[:], in0=stage[:, :, :, 64], scalar1=1e-30),
                              [R_stage], [R_coef])
                            V("dve", lambda e: e.reciprocal(out=coef[:], in_=coef[:]), [R_coef], [R_coef])
                            V("dve", lambda e: e.tensor_tensor(out=coef[:], in0=coef[:],
                                                               in1=gt[:].rearrange("p (h b) -> p b h", b=3), op=ALU.mult),
                              [R_coef, R_gt], [R_coef])
                            o3 = tmpC[:].rearrange("p (h d) -> p h d", d=64)
                            o3b = tmpf2[:].rearrange("p (h d) -> p h d", d=64)
                            V("dve", lambda e: e.tensor_tensor(
                                out=o3, in0=stage[:, 0, :, 0:64],
                                in1=coef[:, 0, :].unsqueeze(2).broadcast_to([128, 16, 64]), op=ALU.mult),
                              [R_stage, R_coef], [R_tmpC])
                            for br in (1, 2):
                                V("pool", lambda e, br=br: e.tensor_tensor(
                                    out=o3b, in0=stage[:, br, :, 0:64],
                                    in1=coef[:, br, :].unsqueeze(2).broadcast_to([128, 16, 64]), op=ALU.mult),
                                  [R_stage, R_coef], [R_tmpf2])
                                V("dve", lambda e: e.tensor_tensor(out=o3, in0=o3, in1=o3b, op=ALU.add),
                                  [R_tmpC, R_tmpf2], [R_tmpC])
                            V("dve", lambda e: e.tensor_tensor(out=ozb[:], in0=tmpC[:], in1=zsp[:], op=ALU.mult),
                              [R_tmpC, R_z], [R_ozb])
                        return C_gen(tt % NX, wo, R_wo, dst_d[s, cs, :], R_dst, None, combine, cdelay=20)

                    run_p4(NT, mkA, mkB, mkC)
                    S.barrier()

    def layer1():
        with ExitStack() as les:
            m0_t = load_const("c_m0", les)
            m1_t = load_const("c_m1", les)
            fing = sb("fing", [128, D], F32, les)
            R_fing = Res("fing")
            S.dma("sp", fing[:], fing_d.broadcast_to([128, D]), writes=[R_fing])
            kT = sb("kT1", [128, 8, T], BF16, les)
            va = sb("va1", [128, NT, 16, 65], BF16, les)
            R_kT, R_va = Res("kT1"), Res("va1")
            V("pool", lambda e: e.memset(va[:, :, :, 64:65], 1.0), [], [R_va])
            alloc_ws(les, 1, False)
            stage, coef, ozb, tmpC = ws["stage"], ws["coef"], ws["ozb"], ws["tmpC"]
            R_wa, R_wb, R_wc = R_wreg
            W = win1_d
            for s in range(nseq):
                src = x1_d if 0 in layers else x_d
                R_src = R_x1[s] if 0 in layers else None
                srcv = src[s].rearrange("(p r) d -> r p d", r=16)
                dstv = out_d[s].rearrange("(p r) d -> r p d", r=16)
                load_mod(1, s)
                wk = wreg[:, :, 0:D]
                wv = wreg[:, :, D:2 * D]
                load_w(wk, W[:, D:2 * D], R_wa)
                load_w(wv, W[:, 2 * D:3 * D], R_wb)
                if s == 0:
                    load_w(wreg[:, :, 2 * D:3 * D], wout1_d, R_wc)
                P2 = {"sq": (ws["xo"], R_xo), "stt": (ws["tmpC"], R_tmpC), "small": (ws["smallC"], R_smallC),
                      "hb": [(hb, R_hb), (ws["ozb"], R_ozb)], "hT": [(hT, R_hT), (ws["ozT"], R_ozT)]}
                for t_ in range(2):
                    load_x_tile(srcv[t_], R_src, t_ % NX)
                    norm_stage(t_ % NX, t_ % 2, P2)
                transpose_stage(0, P2)
                for r in range(NT):
                    slot = r % NX
                    if r + 2 < NT:
                        load_x_tile(srcv[r + 2], R_src, (r + 2) % NX)
                        norm_stage((r + 2) % NX, (r + 2) % 2, P2)
                    if r + 1 < NT:
                        transpose_stage((r + 1) % 2, P2)
                    hTc, R_hTc = P2["hT"][r % 2]
                    cs = slice(r * 128, (r + 1) * 128)
                    cv = cos_t[:, r:r + 16 * 127 + 1:16]
                    sv = sin_t[:, r:r + 16 * 127 + 1:16]
                    proj_fm(wk, R_wa, 8, [P2SB[0], P2SB[1]], hTb=(hTc, R_hTc))
                    for h2 in range(2):
                        bk = 4 + h2
                        for kc in range(8):
                            V("pe", lambda e, kc=kc, h2=h2, bk=bk: e.matmul(
                                banks[bk][:, :], lhsT=hTc[:, kc, :], rhs=wv[:, kc, h2 * 512:(h2 + 1) * 512],
                                start=(kc == 0), stop=(kc == 7)), [R_wb, R_hTc], [R_bank[bk]])
                    evac_qp()
                    rope_n(qp[:, 0:4, :], R_qp, 4, kT[:, 0:4, cs], R_kT, cv, sv, P2SB[2], 0)
                    rope_n(qp[:, 4:8, :], R_qp, 4, kT[:, 4:8, cs], R_kT, cv, sv, P2SB[3], 512)
                    for h2 in range(2):
                        bk = 4 + h2
                        V("act", lambda e, h2=h2, bk=bk, r=r: e.copy(
                            out=va[:, r, 8 * h2:8 * h2 + 8, 0:64], in_=banks[bk][:, :].rearrange("p (h d) -> p h d", d=64)),
                          [R_bank[bk]], [R_va])
                wq = wreg[:, :, 0:D]
                wz = wreg[:, :, D:2 * D]
                wo = wreg[:, :, 2 * D:3 * D]
                load_w(wq, W[:, 0:D], R_wa)
                load_w(wz, W[:, 3 * D:4 * D], R_wb)

                def mkA(r, part, srcv=srcv, R_src=R_src):
                    return A_gen(srcv[r], R_src, r, wq, R_wa, wz, R_wb,
                                 cos_t[:, r:r + 16 * 127 + 1:16], sin_t[:, r:r + 16 * 127 + 1:16], None, part)

                def mkB(r):
                    par = r % 2
                    qr, R_qr_ = ws["qr"][par], R_qrp[par]
                    state["bg_stride"] = 2
                    for gi, (h0, h1) in enumerate(DIL_G):
                        ob = OB[gi % 2]
                        if gi == 0:
                            klist = [(rp, m0_t[:, (1 if r == rp else (2 if r > rp else 0)), :]) for rp in range(16)]
                        elif gi == 1:
                            klist = [(rp, m1_t[:, (1 if r == rp else (2 if r > rp else 0)), :])
                                     for rp in range(16) if (r - rp) % 4 == 0]
                        else:
                            klist = [(r, causal[:])]
                        chunks = [[h for h in range(h0, h1) if h % 2 == par_] for par_ in range(2)]
                        for ch in chunks:
                            n = len(ch)
                            for ki, (rp, mk) in enumerate(klist):
                                attn_unit(ch, lambda i, ch=ch, rp=rp: kT[:, ch[i] // 2, rp * 128:(rp + 1) * 128], qr,
                                          lambda i, ch=ch, rp=rp: va[:, rp, ch[i], :], ob, [(h - h0) * 65 for h in ch], 65,
                                          ki == 0, ki == len(klist) - 1,
                                          bias=(ident[:], mk.unsqueeze(1).broadcast_to([128, n, 128])),
                                          reads_extra=[R_kT, R_va, R_qr_])
                        nh = h1 - h0
                        defer(lambda ob=ob, h0=h0, nh=nh: V("dve", lambda e: e.tensor_copy(
                            out=stage[:, 0, h0:h0 + nh, :],
                            in_=banks[ob][:, 0:nh * 65].rearrange("p (a b) -> p a b", b=65)), [R_bank[ob]], [R_stage]))

                def mkC(r, dstv=dstv):
                    par = r % 2
                    zsp, R_z = ws["zs"][par], R_zsp[par]
                    smallC = ws["smallC"]

                    def combine():
                        for gi, (h0, h1) in enumerate(DIL_G):
                            V("dve", lambda e, gi=gi, h0=h0, h1=h1: e.tensor_reduce(
                                out=smallC[:, gi:gi + 1], in_=stage[:, 0, h0:h1, 64], axis=AX.X, op=ALU.add),
                              [R_stage], [R_smallC])
                            V("dve", lambda e, gi=gi, h0=h0, h1=h1: e.tensor_scalar_mul(
                                out=smallC[:, gi:gi + 1], in0=smallC[:, gi:gi + 1], scalar1=1.0 / (h1 - h0)),
                              [R_smallC], [R_smallC])
                        V("dve", lambda e: e.tensor_reduce(out=smallC[:, 3:4], in_=smallC[:, 0:3], axis=AX.X, op=ALU.add),
                          [R_smallC], [R_smallC])
                        V("dve", lambda e: e.reciprocal(out=smallC[:, 3:4], in_=smallC[:, 3:4]), [R_smallC], [R_smallC])
                        V("dve", lambda e: e.tensor_scalar(out=smallC[:, 4:7], in0=smallC[:, 0:3], scalar1=smallC[:, 3:4],
                                                           scalar2=3.0, op0=ALU.mult, op1=ALU.mult), [R_smallC], [R_smallC])
                        V("dve", lambda e: e.reciprocal(out=coef[:, 0, :], in_=stage[:, 0, :, 64]), [R_stage], [R_coef])
                        for gi, (h0, h1) in enumerate(DIL_G):
                            V("dve", lambda e, gi=gi, h0=h0, h1=h1: e.tensor_scalar(
                                out=coef[:, 0, h0:h1], in0=coef[:, 0, h0:h1], scalar1=smallC[:, 4 + gi:5 + gi], scalar2=None,
                                op0=ALU.mult), [R_smallC, R_coef], [R_coef])
                        o3 = tmpC[:].rearrange("p (h d) -> p h d", d=64)
                        V("dve", lambda e: e.tensor_tensor(out=o3, in0=stage[:, 0, :, 0:64],
                                                           in1=coef[:, 0, :].unsqueeze(2).broadcast_to([128, 16, 64]),
                                                           op=ALU.mult), [R_stage, R_coef], [R_tmpC])
                        V("dve", lambda e: e.tensor_tensor(out=ozb[:], in0=tmpC[:], in1=zsp[:], op=ALU.mult),
                          [R_tmpC, R_z], [R_ozb])
                    return C_gen(r % NX, wo, R_wc, dstv[r], R_out, (fing, R_fing), combine)

                run_p4(NT, mkA, mkB, mkC)
            S.barrier()

    try:
        mark("consts")
        if 0 in layers:
            load_w(wreg[:, :, 0:D], win0_d[:, 0:D], R_wreg[0])
            load_w(wreg[:, :, D:2 * D], win0_d[:, 1792:1792 + D], R_wreg[1])
            load_w(wreg[:, :, 2 * D:3 * D], wout0_d, R_wreg[2])
        for l_ in layers:
            ada_rows(l_)
        mark("ada0")
        if 0 in layers:
            layer0()
        if 1 in layers:
            layer1()
    except _Stop:
        pass
    S.finish()
    es.close()
    return nc, S


_CACHE = {}


def _get_program(nseq, layers, stop=None):
    key = (nseq, tuple(layers), stop)
    if key not in _CACHE:
        _CACHE[key] = build_program(nseq, layers, stop)
    return _CACHE[key]


def run(inputs, ncores=NCORES, nseq=2, layers=(0, 1), stop=None):
    nc, S = _get_program(nseq, layers, stop)
    consts = make_consts()
    f = lambda a: np.ascontiguousarray(np.asarray(a, dtype=np.float32))

    def pe2(pe):
        return np.ascontiguousarray(f(pe).reshape(16, 2, 64).transpose(1, 2, 0).reshape(128, 16))

    shared = {
        "norm_g": f(inputs["norm_g"]), "ada_w": f(inputs["ada_w"]), "ada_b": f(inputs["ada_b"]),
        "nsa_w_in": f(inputs["nsa_w_in"][0]), "pe2k": pe2(inputs["nsa_pe_k"][0]), "pe2v": pe2(inputs["nsa_pe_v"][0]),
        "nsa_ck_w1": f(inputs["nsa_ck_w1"][0]), "nsa_ck_w2": f(inputs["nsa_ck_w2"][0]),
        "nsa_cv_w1": f(inputs["nsa_cv_w1"][0]), "nsa_cv_w2": f(inputs["nsa_cv_w2"][0]),
        "nsa_w_out": f(inputs["nsa_w_out"][0]), "dil_w_in": f(inputs["dil_w_in"][0]),
        "dil_w_out": f(inputs["dil_w_out"][0]), "final_g": f(inputs["final_g"]).reshape(1, D),
    }
    shared.update(consts)
    x = f(inputs["x"])
    c = f(inputs["c"])
    in_maps = []
    for i in range(ncores):
        m = dict(shared)
        m["x"] = np.ascontiguousarray(x[i * nseq:(i + 1) * nseq])
        cc = c[i * nseq:(i + 1) * nseq]
        m["ct"] = np.ascontiguousarray(cc.reshape(nseq, 8, 128).transpose(2, 1, 0))
        in_maps.append(m)
    res = run_bass_kernel_spmd(nc, in_maps, core_ids=list(range(ncores)))
    return np.concatenate([np.asarray(r["out"]) for r in res.results], axis=0)


def kernel(**inputs):
    return run(inputs).astype(np.float32)
```
